# Optimizing a Trainium2 kernel written in Bass

```python
import math
import jax
import jax.numpy as jnp
from jax import lax
import numpy as np

D_MODEL = 2048
BATCH = 4
SEQ = 2048
DEPTH = 2

MEM_LEN = 256
CHUNK = 64
CONV_K = 4
NORM_EPS = 1e-6
L2_EPS = 1e-6

N_EVEN = (DEPTH + 1) // 2
N_ODD = DEPTH // 2

A_HEADS = 8
A_DK = 128
A_DV = D_MODEL // 2 // A_HEADS
A_QK = A_HEADS * A_DK
A_V = A_HEADS * A_DV
B_HEADS = 8
B_DK = 128
B_DV = D_MODEL // 2 // B_HEADS
B_QK = B_HEADS * B_DK
B_V = B_HEADS * B_DV
HY_SIZES = (A_QK, A_QK, A_V, A_V, B_QK, B_QK, B_V, B_V, B_HEADS, B_HEADS)
HY_PROJ = sum(HY_SIZES)
HY_MIX = A_V + B_V

SSD_DINNER = 2 * D_MODEL
SSD_HEADDIM = 64
SSD_HEADS = SSD_DINNER // SSD_HEADDIM
SSD_GROUPS = 8
SSD_DSTATE = 128
SSD_CONV_DIM = SSD_DINNER + 2 * SSD_GROUPS * SSD_DSTATE
SSD_PROJ = SSD_DINNER + SSD_CONV_DIM + SSD_HEADS

XA_HEADS = 4
XA_HEAD_DIM = 128
XA_DIM = XA_HEADS * XA_HEAD_DIM

D_FF = ((8 * D_MODEL + 3 * 256 - 1) // (3 * 256)) * 256

kernel_name = 'hybrid_hgrn2_gdn_ssd_trunk'


def _split(t, sizes):
    idx = [int(s) for s in np.cumsum(sizes)[:-1]]
    return jnp.split(t, idx, axis=-1)


def rmsnorm(x, w):
    xf = x.astype(jnp.float32)
    y = xf * lax.rsqrt(jnp.mean(xf * xf, axis=-1, keepdims=True) + NORM_EPS)
    return (y * w.astype(jnp.float32)).astype(x.dtype)


def heads(t, n):
    return t.reshape(t.shape[0], t.shape[1], n, -1)


def l2norm(t):
    return t * lax.rsqrt(jnp.sum(t * t, axis=-1, keepdims=True) + L2_EPS)


def gated_head_norm(o, w, gate):
    y = o * lax.rsqrt(jnp.mean(o * o, axis=-1, keepdims=True) + NORM_EPS)
    y = y.reshape(o.shape[0], o.shape[1], -1)
    return y * w.astype(jnp.float32) * jax.nn.silu(gate)


def causal_depthwise_conv(x, w):
    k_len, ch = w.shape
    return lax.conv_general_dilated(
        x, w.astype(x.dtype)[:, None, :], window_strides=(1,), padding=[(k_len - 1, 0)],
        dimension_numbers=('NWC', 'WIO', 'NWC'), feature_group_count=ch)


def _to_chunks(t):
    b, s, h, d = t.shape
    return t.reshape(b, s // CHUNK, CHUNK, h, d).transpose(1, 0, 3, 2, 4)


def _to_chunks_scalar(t):
    b, s, h = t.shape
    return t.reshape(b, s // CHUNK, CHUNK, h).transpose(1, 0, 3, 2)


def _from_chunks(t):
    nc, b, h, c, d = t.shape
    return t.transpose(1, 0, 3, 2, 4).reshape(b, nc * c, h, d)


def hgrn2_chunked(q, k, v, log_f):
    bsz, _, nh, dk = q.shape
    dv = v.shape[-1]
    qc, kc, vc = _to_chunks(q), _to_chunks(k), _to_chunks(v)
    gcum = jnp.cumsum(_to_chunks(log_f), axis=3)
    causal = jnp.tril(jnp.ones((CHUNK, CHUNK), bool))

    def step(state, inp):
        q_, k_, v_, g_ = inp
        diff = g_[:, :, :, None, :] - g_[:, :, None, :, :]
        decay = jnp.exp(jnp.where(causal[:, :, None], diff, -jnp.inf))
        attn = jnp.einsum('bhik,bhjk,bhijk->bhij', q_, k_, decay)
        o = jnp.einsum('bhij,bhjv->bhiv', attn, v_) + \
            jnp.einsum('bhik,bhkv->bhiv', q_ * jnp.exp(g_), state)
        g_last = g_[:, :, -1:, :]
        state = state * jnp.exp(g_last[:, :, 0, :])[..., None] + \
            jnp.einsum('bhjk,bhjv->bhkv', k_ * jnp.exp(g_last - g_), v_)
        return state, o

    s0 = jnp.zeros((bsz, nh, dk, dv), jnp.float32)
    _, o = lax.scan(step, s0, (qc, kc, vc, gcum))
    return _from_chunks(o)


def gated_delta_chunked(q, k, v, beta, g):
    bsz, _, nh, dk = q.shape
    dv = v.shape[-1]
    qc, kc, vc = _to_chunks(q), _to_chunks(k), _to_chunks(v)
    bc = _to_chunks_scalar(beta)
    gam = jnp.cumsum(_to_chunks_scalar(g), axis=-1)
    lower = jnp.tril(jnp.ones((CHUNK, CHUNK), bool))
    strict = jnp.tril(jnp.ones((CHUNK, CHUNK), jnp.float32), -1)
    decay = jnp.exp(jnp.where(lower, gam[..., :, None] - gam[..., None, :], -jnp.inf))
    kb = kc * bc[..., None]
    a_mat = jnp.einsum('zbhik,zbhjk->zbhij', kb, kc) * decay * strict
    eye = jnp.eye(CHUNK, dtype=jnp.float32)
    t_mat = lax.linalg.triangular_solve(eye + a_mat, jnp.broadcast_to(eye, a_mat.shape),
                                        left_side=True, lower=True, unit_diagonal=True)
    w = jnp.einsum('zbhij,zbhjk->zbhik', t_mat, kb * jnp.exp(gam)[..., None])
    u = jnp.einsum('zbhij,zbhjv->zbhiv', t_mat, vc * bc[..., None])
    qk = jnp.einsum('zbhik,zbhjk->zbhij', qc, kc) * decay
    q_dec = qc * jnp.exp(gam)[..., None]
    g_last = gam[..., -1]
    k_dec = kc * jnp.exp(g_last[..., None] - gam)[..., None]

    def step(state, inp):
        w_, u_, qk_, qd_, kd_, gl_ = inp
        v_new = u_ - jnp.einsum('bhck,bhkv->bhcv', w_, state)
        o = jnp.einsum('bhck,bhkv->bhcv', qd_, state) + jnp.einsum('bhij,bhjv->bhiv', qk_, v_new)
        state = state * jnp.exp(gl_)[..., None, None] + jnp.einsum('bhck,bhcv->bhkv', kd_, v_new)
        return state, o

    s0 = jnp.zeros((bsz, nh, dk, dv), jnp.float32)
    _, o = lax.scan(step, s0, (w, u, qk, q_dec, k_dec, g_last))
    return _from_chunks(o)


def ssd_chunked(x, dt, a_neg, b_in, c_in):
    bsz, seq, nh, hp = x.shape
    ng, ns = b_in.shape[2], b_in.shape[3]
    nr = nh // ng
    nc = seq // CHUNK
    xc = (x * dt[..., None]).reshape(bsz, nc, CHUNK, ng, nr, hp).transpose(1, 0, 2, 3, 4, 5)
    ac = (dt * a_neg).reshape(bsz, nc, CHUNK, ng, nr).transpose(1, 0, 3, 4, 2)
    bc = b_in.reshape(bsz, nc, CHUNK, ng, ns).transpose(1, 0, 2, 3, 4)
    cc = c_in.reshape(bsz, nc, CHUNK, ng, ns).transpose(1, 0, 2, 3, 4)
    acum = jnp.cumsum(ac, axis=-1)
    lower = jnp.tril(jnp.ones((CHUNK, CHUNK), bool))
    lmat = jnp.exp(jnp.where(lower, acum[..., :, None] - acum[..., None, :], -jnp.inf))
    cb = jnp.einsum('zbign,zbjgn->zbgij', cc, bc)
    y_diag = jnp.einsum('zbgrij,zbjgrp->zbigrp', cb[:, :, :, None] * lmat, xc)
    decay_states = jnp.exp(acum[..., -1:] - acum).transpose(0, 1, 4, 2, 3)
    chunk_states = jnp.einsum('zbjgn,zbjgrp->zbgrpn', bc, xc * decay_states[..., None])
    chunk_decay = jnp.exp(acum[..., -1])

    def step(h, inp):
        st, dec = inp
        return h * dec[..., None, None] + st, h

    h0 = jnp.zeros((bsz, ng, nr, hp, ns), jnp.float32)
    _, h_prev = lax.scan(step, h0, (chunk_states, chunk_decay))
    y_off = jnp.einsum('zbign,zbgrpn->zbigrp', cc, h_prev) * \
        jnp.exp(acum).transpose(0, 1, 4, 2, 3)[..., None]
    y = (y_diag + y_off).transpose(1, 0, 2, 3, 4, 5)
    return y.reshape(bsz, seq, nh, hp)


def hgrn2_gdn_mixer(u, w_in, lb, hgrn_norm, gdn_conv_w, gdn_a_log, gdn_dt_bias, gdn_norm, w_out):
    proj = jnp.matmul(u, w_in).astype(jnp.float32)
    a_q, a_f, a_i, a_g, b_q, b_k, b_v, b_g, b_beta, b_a = _split(proj, HY_SIZES)
    lb = lb.astype(jnp.float32)
    log_f = jnp.log(lb + (1.0 - lb) * jax.nn.sigmoid(a_f))
    k_a = (1.0 - lb) * jax.nn.sigmoid(-a_f)
    o_a = hgrn2_chunked(heads(jax.nn.silu(a_q), A_HEADS), heads(k_a, A_HEADS),
                        heads(a_i, A_HEADS), heads(log_f, A_HEADS))
    o_a = gated_head_norm(o_a, hgrn_norm, a_g)
    qkv = jax.nn.silu(causal_depthwise_conv(jnp.concatenate([b_q, b_k, b_v], axis=-1), gdn_conv_w))
    q_b, k_b, v_b = _split(qkv, (B_QK, B_QK, B_V))
    q_b = l2norm(heads(q_b, B_HEADS)) * (B_DK ** -0.5)
    k_b = l2norm(heads(k_b, B_HEADS))
    beta = jax.nn.sigmoid(b_beta)
    g = -jnp.exp(gdn_a_log.astype(jnp.float32)) * jax.nn.softplus(b_a + gdn_dt_bias.astype(jnp.float32))
    o_b = gated_delta_chunked(q_b, k_b, heads(v_b, B_HEADS), beta, g)
    o_b = gated_head_norm(o_b, gdn_norm, b_g)
    return jnp.matmul(jnp.concatenate([o_a, o_b], axis=-1), w_out.astype(jnp.float32))


def mamba2_mixer(u, w_in, conv_w, conv_b, dt_bias, a_log, d_skip, norm_w, w_out):
    proj = jnp.matmul(u, w_in).astype(jnp.float32)
    z, xbc, dt_raw = _split(proj, (SSD_DINNER, SSD_CONV_DIM, SSD_HEADS))
    xbc = jax.nn.silu(causal_depthwise_conv(xbc, conv_w) + conv_b.astype(jnp.float32))
    xs, b_in, c_in = _split(xbc, (SSD_DINNER, SSD_GROUPS * SSD_DSTATE, SSD_GROUPS * SSD_DSTATE))
    bsz, seq, _ = u.shape
    dt = jax.nn.softplus(dt_raw + dt_bias.astype(jnp.float32))
    a_neg = -jnp.exp(a_log.astype(jnp.float32))
    xh = xs.reshape(bsz, seq, SSD_HEADS, SSD_HEADDIM)
    y = ssd_chunked(xh, dt, a_neg,
                    b_in.reshape(bsz, seq, SSD_GROUPS, SSD_DSTATE),
                    c_in.reshape(bsz, seq, SSD_GROUPS, SSD_DSTATE))
    y = (y + d_skip.astype(jnp.float32)[:, None] * xh).reshape(bsz, seq, SSD_DINNER)
    y = rmsnorm(y * jax.nn.silu(z), norm_w)
    return jnp.matmul(y, w_out.astype(jnp.float32))


def memory_cross_attention(h, m, wq, wk, wv, wo):
    bsz, seq, _ = h.shape
    q = jnp.matmul(h, wq).reshape(bsz, seq, XA_HEADS, XA_HEAD_DIM)
    k = jnp.matmul(m, wk).reshape(bsz, m.shape[1], XA_HEADS, XA_HEAD_DIM)
    v = jnp.matmul(m, wv).reshape(bsz, m.shape[1], XA_HEADS, XA_HEAD_DIM)
    s = jnp.einsum('bshd,bmhd->bhsm', q.astype(jnp.float32), k.astype(jnp.float32)) * (XA_HEAD_DIM ** -0.5)
    p = jax.nn.softmax(s, axis=-1)
    o = jnp.einsum('bhsm,bmhd->bshd', p, v.astype(jnp.float32)).reshape(bsz, seq, XA_DIM)
    return jnp.matmul(o, wo.astype(jnp.float32))


def swiglu(h, w_gate, w_up, w_down):
    return jnp.matmul(jax.nn.silu(jnp.matmul(h, w_gate)) * jnp.matmul(h, w_up), w_down)


def _normal(key, shape, scale):
    return scale * jax.random.normal(key, shape, jnp.float32)


def _dense(key, shape):
    return _normal(key, shape, shape[-2] ** -0.5)


def _gain(key, shape):
    return 1.0 + _normal(key, shape, 0.02)


def _dt_bias(key, shape):
    dt = jnp.exp(jax.random.uniform(key, shape, jnp.float32, math.log(1e-3), math.log(1e-1)))
    return dt + jnp.log(-jnp.expm1(-dt))


def _a_log(key, shape):
    return jnp.log(jax.random.uniform(key, shape, jnp.float32, 1.0, 16.0))


def setup_inputs(seed: int = 0) -> dict:
    key = jax.random.key(seed)
    ks = jax.random.split(key, 32)
    return {
        'x': _normal(ks[0], (BATCH, SEQ, D_MODEL), 1.0),
        'mem': _normal(ks[1], (BATCH, MEM_LEN, D_MODEL), 1.0),
        'norm_mix': _gain(ks[2], (DEPTH, D_MODEL)),
        'norm_xattn': _gain(ks[3], (DEPTH, D_MODEL)),
        'norm_mem': _gain(ks[4], (DEPTH, D_MODEL)),
        'norm_ffn': _gain(ks[5], (DEPTH, D_MODEL)),
        'norm_final': _gain(ks[6], (D_MODEL,)),
        'hy_w_in': _dense(ks[7], (N_EVEN, D_MODEL, HY_PROJ)),
        'hgrn_lb_logits': _normal(ks[8], (N_EVEN + 1, A_QK), 0.1),
        'hgrn_norm': _gain(ks[9], (N_EVEN, A_V)),
        'gdn_conv_w': _normal(ks[10], (N_EVEN, CONV_K, B_QK + B_QK + B_V), CONV_K ** -0.5),
        'gdn_a_log': _a_log(ks[11], (N_EVEN, B_HEADS)),
        'gdn_dt_bias': _dt_bias(ks[12], (N_EVEN, B_HEADS)),
        'gdn_norm': _gain(ks[13], (N_EVEN, B_V)),
        'hy_w_out': _dense(ks[14], (N_EVEN, HY_MIX, D_MODEL)),
        'ssd_w_in': _dense(ks[15], (N_ODD, D_MODEL, SSD_PROJ)),
        'ssd_conv_w': _normal(ks[16], (N_ODD, CONV_K, SSD_CONV_DIM), CONV_K ** -0.5),
        'ssd_conv_b': _normal(ks[17], (N_ODD, SSD_CONV_DIM), 0.02),
        'ssd_dt_bias': _dt_bias(ks[18], (N_ODD, SSD_HEADS)),
        'ssd_a_log': _a_log(ks[19], (N_ODD, SSD_HEADS)),
        'ssd_d': 1.0 + _normal(ks[20], (N_ODD, SSD_HEADS), 0.1),
        'ssd_norm': _gain(ks[21], (N_ODD, SSD_DINNER)),
        'ssd_w_out': _dense(ks[22], (N_ODD, SSD_DINNER, D_MODEL)),
        'xa_wq': _dense(ks[23], (DEPTH, D_MODEL, XA_DIM)),
        'xa_wk': _dense(ks[24], (DEPTH, D_MODEL, XA_DIM)),
        'xa_wv': _dense(ks[25], (DEPTH, D_MODEL, XA_DIM)),
        'xa_wo': _dense(ks[26], (DEPTH, XA_DIM, D_MODEL)),
        'ffn_w_gate': _dense(ks[27], (DEPTH, D_MODEL, D_FF)),
        'ffn_w_up': _dense(ks[28], (DEPTH, D_MODEL, D_FF)),
        'ffn_w_down': _dense(ks[29], (DEPTH, D_FF, D_MODEL)),
    }


def reference(x, mem, norm_mix, norm_xattn, norm_mem, norm_ffn, norm_final,
              hy_w_in, hgrn_lb_logits, hgrn_norm, gdn_conv_w, gdn_a_log, gdn_dt_bias, gdn_norm, hy_w_out,
              ssd_w_in, ssd_conv_w, ssd_conv_b, ssd_dt_bias, ssd_a_log, ssd_d, ssd_norm, ssd_w_out,
              xa_wq, xa_wk, xa_wv, xa_wo, ffn_w_gate, ffn_w_up, ffn_w_down):
    lower_bounds = jnp.cumsum(jax.nn.softmax(hgrn_lb_logits.astype(jnp.float32), axis=0), axis=0)
    h = x
    for layer in range(DEPTH):
        u = rmsnorm(h, norm_mix[layer])
        if layer % 2 == 0:
            e = layer // 2
            mix = hgrn2_gdn_mixer(u, hy_w_in[e], lower_bounds[e], hgrn_norm[e], gdn_conv_w[e],
                                  gdn_a_log[e], gdn_dt_bias[e], gdn_norm[e], hy_w_out[e])
        else:
            o = layer // 2
            mix = mamba2_mixer(u, ssd_w_in[o], ssd_conv_w[o], ssd_conv_b[o], ssd_dt_bias[o],
                               ssd_a_log[o], ssd_d[o], ssd_norm[o], ssd_w_out[o])
        h = h + mix.astype(h.dtype)
        xa = memory_cross_attention(rmsnorm(h, norm_xattn[layer]), rmsnorm(mem, norm_mem[layer]),
                                    xa_wq[layer], xa_wk[layer], xa_wv[layer], xa_wo[layer])
        h = h + xa.astype(h.dtype)
        ff = swiglu(rmsnorm(h, norm_ffn[layer]), ffn_w_gate[layer], ffn_w_up[layer], ffn_w_down[layer])
        h = h + ff.astype(h.dtype)
    return rmsnorm(h, norm_final)
```

```python
import numpy as np
import concourse.bass as bass
import concourse.mybir as mybir
from concourse.bass_utils import run_bass_kernel_spmd

F32 = mybir.dt.float32
BF16 = mybir.dt.bfloat16
AF = mybir.ActivationFunctionType
ALU = mybir.AluOpType
AX = mybir.AxisListType

_DT_SIZE = {F32: 4, BF16: 2}


class T:
    def __init__(self, base, space, lo, hi, name):
        self.h = base
        self.space = space
        self.lo = lo
        self.hi = hi
        self.name = name

    def __getitem__(self, idx):
        v = V(self.h[idx], self)
        if self.space == "dram":
            i0 = idx[0] if isinstance(idx, tuple) else idx
            if isinstance(i0, slice) and (i0.start is None or isinstance(i0.start, int)) and (i0.stop is None or isinstance(i0.stop, int)) \
                    and i0.step in (None, 1):
                r0 = 0 if i0.start is None else i0.start
                r1 = (self.hi - self.lo) if i0.stop is None else i0.stop
                v.lo, v.hi = self.lo + r0, self.lo + r1
        return v

    @property
    def ap(self):
        return V(self.h, self)


class V:
    def __init__(self, ap, t, lo=None, hi=None):
        self.ap = ap
        self.t = t
        self.lo = t.lo if lo is None else lo
        self.hi = t.hi if hi is None else hi

    def __getitem__(self, idx):
        return V(self.ap[idx], self.t, self.lo, self.hi)

    def __getattr__(self, name):
        if name in ("ap", "t", "lo", "hi"):
            raise AttributeError(name)
        a = getattr(self.ap, name)
        if callable(a):
            def f(*args, **kw):
                r = a(*args, **kw)
                if isinstance(r, bass.AP):
                    return V(r, self.t, self.lo, self.hi)
                return r
            return f
        return a


class CCAP:
    def __init__(self, v):
        self.v = v


class Ring:
    def __init__(self, tiles, prog):
        self.tiles = tiles
        self.i = 0
        self.p = prog

    def next(self):
        t = self.tiles[self.i % len(self.tiles)]
        self.i += 1
        return t

    def free(self):
        self.p.free(*self.tiles)


class Op:
    __slots__ = ("eng", "name", "args", "kw", "deps", "is_dma", "is_cc", "sem", "val", "has_dep", "idx", "pre")

    def __init__(self, eng, name, args, kw):
        self.eng = eng
        self.name = name
        self.args = args
        self.kw = kw
        self.deps = []
        self.is_dma = name == "dma_start"
        self.is_cc = name == "collective_compute"
        self.sem = None
        self.val = None
        self.has_dep = False
        self.pre = []


class EngProxy:
    def __init__(self, prog, eng):
        self.p = prog
        self.e = eng

    def __getattr__(self, name):
        def f(*args, **kw):
            return self.p.record(self.e, name, args, kw)
        return f


WRITE_KEYS = ("out", "accum_out")
CC_BLOCK = True


class Prog:
    ENGS = ("pe", "act", "dve", "pool", "sp")

    def __init__(self, nc, same_engine_sync=True, n_dma_sems=(12, 6, 6)):
        self.nc = nc
        self.ops = []
        self.same_engine_sync = same_engine_sync
        self.recs = {"sb": [], "ps": [], "dram": []}
        self.pe = EngProxy(self, "pe")
        self.act = EngProxy(self, "act")
        self.dve = EngProxy(self, "dve")
        self.pool = EngProxy(self, "pool")
        self.sp = EngProxy(self, "sp")
        self.sb_lo = (nc.sbuf_base + 63) // 64 * 64
        self.sb_hi = nc.sbuf_top // 64 * 64
        self.sb_free = [(self.sb_lo, self.sb_hi)]
        self.sb_peak = 0
        self.n_tiles = 0
        self.psum = []
        self.dram_next = 0
        self.live = []
        self.last_cc = None
        self.n_dma_sems = dict(zip(("sp", "act", "pool"), n_dma_sems))

    def tile(self, shape, dtype=F32, name=None):
        nbytes = int(np.prod(shape[1:])) * _DT_SIZE[dtype]
        nbytes_al = (nbytes + 31) // 32 * 32
        for i, (lo, hi) in enumerate(self.sb_free):
            if hi - lo >= nbytes_al:
                if hi - lo == nbytes_al:
                    self.sb_free.pop(i)
                else:
                    self.sb_free[i] = (lo + nbytes_al, hi)
                self.n_tiles += 1
                nm = f"{name or 't'}_{self.n_tiles}"
                h = self.nc.alloc_sbuf_tensor_at(nm, list(shape), dtype, offset=lo)
                self.sb_peak = max(self.sb_peak, lo + nbytes_al)
                t = T(h[:], "sb", lo, lo + nbytes_al, nm)
                self.live.append(t)
                return t
        raise RuntimeError(f"SBUF arena full allocating {shape} {name}; free={self.sb_free}")

    def mark(self):
        return list(self.live)

    def release(self, mark):
        keep = set(id(t) for t in mark)
        self.free(*[t for t in self.live if id(t) not in keep])

    def free(self, *tiles):
        for t in tiles:
            if not any(t is x for x in self.live):
                continue
            self.live = [x for x in self.live if x is not t]
            self.sb_free.append((t.lo, t.hi))
        self.sb_free.sort()
        merged = []
        for lo, hi in self.sb_free:
            if merged and merged[-1][1] == lo:
                merged[-1] = (merged[-1][0], hi)
            else:
                merged.append((lo, hi))
        self.sb_free = merged

    def psum_init(self):
        self.ps_h = self.nc.alloc_psum_tensor("psall", [128, 4096], F32)

    def bank(self, b0, nb=1):
        return T(self.ps_h[:, b0 * 512:(b0 + nb) * 512], "ps", b0 * 2048, (b0 + nb) * 2048, f"ps{b0}_{nb}")

    def psum_slice(self, b, col0, ncols):
        c0 = b * 512 + col0
        return T(self.ps_h[:, c0:c0 + ncols], "ps", c0 * 4, (c0 + ncols) * 4, f"pss{b}_{col0}")

    def bank_bf(self, b0, nb=1):
        return T(self.ps_h[:, b0 * 512:(b0 + nb) * 512].bitcast(BF16), "ps", b0 * 2048, (b0 + nb) * 2048, f"psb{b0}_{nb}")

    def ring(self, shape, dtype, n, name=None):
        return Ring([self.tile(shape, dtype, name) for _ in range(n)], self)

    def dram(self, name, shape, dtype, kind):
        if kind == "Internal":
            h = self.nc.dram_tensor(name, list(shape), dtype)
        else:
            h = self.nc.dram_tensor(name, list(shape), dtype, kind=kind)
        base = self.dram_next
        self.dram_next += int(shape[0]) + 8
        return T(h.ap(), "dram", base, base + int(shape[0]), name)

    def _access(self, op, t, lo, hi, is_write):
        recs = self.recs[t.space]
        keep = []
        for r in recs:
            rlo, rhi, rop, rw = r
            if rlo < hi and lo < rhi:
                if (is_write or rw) and rop is not op:
                    op.deps.append(rop)
                if is_write and rlo >= lo and rhi <= hi:
                    continue
                if (not is_write) and (not rw) and rop.eng == op.eng and not rop.is_dma and not op.is_dma and not rop.is_cc \
                        and rlo >= lo and rhi <= hi:
                    continue
            keep.append(r)
        keep.append((lo, hi, op, is_write))
        self.recs[t.space] = keep

    def record(self, eng, name, args, kw, extra_reads=(), extra_writes=()):
        op = Op(eng, name, args, kw)
        reads, writes = [], []
        for k, v in kw.items():
            if isinstance(v, V):
                (writes if k in WRITE_KEYS else reads).append(v)
        for i, v in enumerate(args):
            if isinstance(v, V):
                (writes if i == 0 else reads).append(v)
        reads += list(extra_reads)
        writes += list(extra_writes)
        for v in reads:
            t = v.t if isinstance(v, V) else v
            self._access(op, t, v.lo, v.hi, False)
        for v in writes:
            t = v.t if isinstance(v, V) else v
            self._access(op, t, v.lo, v.hi, True)
        if op.is_dma and self.last_cc is not None:
            op.deps.append(self.last_cc)
        if op.is_cc:
            self.last_cc = op
        op.idx = len(self.ops)
        self.ops.append(op)
        return op

    def collective(self, kind, op, groups, src, dst):
        return self.record("pool", "collective_compute", (kind, op), dict(replica_groups=groups, ins=[CCAP(src)], outs=[CCAP(dst)]),
                           extra_reads=[src], extra_writes=[dst])

    def dma(self, q, out, in_, **kw):
        return self.record(q, "dma_start", (), dict(out=out, in_=in_, **kw))

    def emit(self, final_wait_ops=()):
        nc = self.nc
        ops = self.ops
        for op in ops:
            seen = set()
            d2 = []
            for d in op.deps:
                if id(d) in seen:
                    continue
                seen.add(id(d))
                if d.eng == op.eng and not d.is_dma and not d.is_cc:
                    if d.eng == "pe" or not self.same_engine_sync:
                        continue
                d2.append(d)
            op.deps = d2
            for d in d2:
                d.has_dep = True
        for op in final_wait_ops:
            op.has_dep = True
        eng_sem = {}
        eng_cnt = {}
        SEM_MAX = 30000

        def new_sem(nm):
            return nc.alloc_semaphore(nm)

        for e in ("pe", "act", "dve", "pool"):
            eng_sem[e] = new_sem(f"c_{e}_0")
            eng_cnt[e] = 0
        n_epoch = 0
        dma_sems = {q: [[new_sem(f"d_{q}_{i}"), 0, None] for i in range(n)] for q, n in self.n_dma_sems.items()}
        dma_rr = {q: 0 for q in dma_sems}
        for op in ops:
            if op.is_cc:
                op.sem, op.val = new_sem(f"cc_{op.idx}"), 1
                for q_, pool_ in dma_sems.items():
                    for sl_ in pool_:
                        if sl_[2] is not None:
                            op.pre.append((sl_[0], sl_[1]))
            elif op.is_dma:
                pool = dma_sems[op.eng]
                slot = pool[dma_rr[op.eng] % len(pool)]
                dma_rr[op.eng] += 1
                if slot[2] is not None:
                    op.pre.append((slot[0], slot[1]))
                slot[1] += 16
                slot[2] = op
                op.sem, op.val = slot[0], slot[1]
            elif op.has_dep:
                e = op.eng
                if eng_cnt[e] >= SEM_MAX:
                    n_epoch += 1
                    eng_sem[e] = new_sem(f"c_{e}_{n_epoch}")
                    eng_cnt[e] = 0
                eng_cnt[e] += 1
                op.sem, op.val = eng_sem[e], eng_cnt[e]
        self.eng_cnt_final = dict(eng_cnt)
        print("sem counts", eng_cnt, "epochs", n_epoch, "dma", {q: [x[1] for x in v] for q, v in dma_sems.items()})
        per_eng = {e: [] for e in self.ENGS}
        for op in ops:
            per_eng[op.eng].append(op)
        handles = {"pe": "tensor", "act": "scalar", "dve": "vector", "pool": "gpsimd", "sp": "sync"}
        print("per-engine instr", {e: len(v) for e, v in per_eng.items()})
        self.n_waits = 0

        def unwrap(x):
            if isinstance(x, V):
                return x.ap
            if isinstance(x, CCAP):
                return x.v.ap.opt()
            if isinstance(x, list):
                return [unwrap(y) for y in x]
            return x

        def emit_engine(e, eng):
            waited = {}
            for op in per_eng[e]:
                need = {}
                for (s, v) in op.pre:
                    need[id(s)] = (s, max(v, need.get(id(s), (s, 0))[1]))
                for d in op.deps:
                    k = id(d.sem)
                    if k not in need or need[k][1] < d.val:
                        need[k] = (d.sem, d.val)
                for k, (s, v) in need.items():
                    if waited.get(k, 0) >= v:
                        continue
                    eng.wait_ge(s, v)
                    self.n_waits += 1
                    waited[k] = v
                args = [unwrap(a) for a in op.args]
                kw = {k: unwrap(v) for k, v in op.kw.items()}
                ins = getattr(eng, op.name)(*args, **kw)
                if op.sem is not None:
                    ins.then_inc(op.sem, 16 if op.is_dma else 1)
                if op.is_cc and CC_BLOCK:
                    eng.wait_ge(op.sem, op.val)
            if e in final_eng:
                for op in final_wait_ops:
                    eng.wait_ge(op.sem, op.val)

        final_eng = {"sp"}
        with nc.Block() as block:
            for e in self.ENGS:
                fn = getattr(block, handles[e])
                fn(lambda eng, e=e: emit_engine(e, eng))
        return nc


D = 2048
NT = 1024
NTT = NT // 128
EPS = 1e-6
DFF = 5632
MEM = 256
NCONST = 512


class Ctx:
    pass


def setup_ctx(P, cdram):
    cx = Ctx()
    cx.P = P
    P.psum_init()
    cx.consts = P.tile([128, NCONST], F32, "consts")
    P.dma("sp", out=cx.consts[:], in_=cdram[:])
    cx.identf = cx.consts[:, 0:128]
    cx.identb_t = P.tile([128, 128], BF16, "identb")
    P.dve.tensor_copy(out=cx.identb_t[:], in_=cx.consts[:, 0:128])
    cx.identb = cx.identb_t[:]
    cx.ones_f = cx.consts[:, 320:448]
    cx.small = P.ring([128, 8], F32, 8, "small")
    cx.small32 = P.ring([128, 32], F32, 6, "small32")
    return cx


def bcast_load(P, dram_row, K, name):
    t = P.tile([128, K], F32, name)
    P.dma("sp", out=t[:], in_=dram_row[:, :].to_broadcast([128, K]))
    return t


def dma_w(P, out_tile, w_view, KC, ncols, q="pool"):
    step = 16
    for k0 in range(0, KC, step):
        k1 = min(KC, k0 + step)
        P.dma(q, out=out_tile[:, k0:k1, :ncols],
              in_=w_view[k0 * 128:k1 * 128, :].rearrange("(c p) n -> p c n", p=128))


def norm_T(cx, srcs, K, wbc, dstT, tok0, xn_ring, tp_banks, do_norm=True):
    P = cx.P
    KC = K // 128
    for i, src in enumerate(srcs):
        xn = xn_ring.next()
        if do_norm:
            ss = cx.small.next()
            P.act.activation(out=xn[:, :K], in_=src, func=AF.Square, accum_out=ss[:, 0:1])
            P.dve.tensor_scalar(out=ss[:, 1:2], in0=ss[:, 0:1], scalar1=1.0 / K, scalar2=EPS, op0=ALU.mult, op1=ALU.add)
            P.act.activation(out=ss[:, 2:3], in_=ss[:, 1:2], func=AF.Ln)
            P.act.activation(out=ss[:, 2:3], in_=ss[:, 2:3], func=AF.Exp, scale=-0.5)
            P.dve.scalar_tensor_tensor(out=xn[:, :K], in0=src, scalar=ss[:, 2:3], in1=wbc, op0=ALU.mult, op1=ALU.mult)
        else:
            P.act.activation(out=xn[:, :K], in_=src, func=AF.Copy)
        for j, c0 in enumerate(range(0, KC, 8)):
            pb = tp_banks.next()
            n = min(8, KC - c0)
            for c in range(n):
                P.pe.transpose(out=pb[:, c * 128:(c + 1) * 128], in_=xn[:, (c0 + c) * 128:(c0 + c + 1) * 128],
                               identity=cx.identb)
            o = dstT[:, c0:c0 + n, tok0 + i * 128:tok0 + (i + 1) * 128]
            s = pb[:, :n * 128].rearrange("p (c t) -> p c t", c=n)
            if j % 2 == 0:
                P.dve.tensor_copy(out=o, in_=s)
            else:
                P.act.activation(out=o, in_=s, func=AF.Copy)


def stage_b(P, cx, io, layer_kind, final):
    mark = P.mark()
    tp_banks = Ring([P.bank_bf(0), P.bank_bf(1)], P)
    mm_banks = Ring([P.bank(2), P.bank(3), P.bank(4), P.bank(5)], P)
    RW = 2048 if layer_kind == 0 else 2064
    h = [P.tile([128, D], F32, f"h{t}") for t in range(NTT)]
    rsr = P.ring([128, RW], F32, 2, "rs")
    for t in range(NTT):
        P.dma("sp", out=h[t][:], in_=io["hres"][t * 128:(t + 1) * 128, :])
        rs = rsr.next()
        P.dma("sp", out=rs[:], in_=io["rs"][t * 128:(t + 1) * 128, :])
        if layer_kind == 0:
            P.dve.tensor_tensor(out=h[t][:], in0=rs[:, 0:D], in1=h[t][:], op=ALU.add)
        else:
            ss = cx.small.next()
            P.dve.tensor_scalar(out=ss[:, 1:2], in0=rs[:, D:D + 1], scalar1=1.0 / 4096, scalar2=EPS, op0=ALU.mult, op1=ALU.add)
            P.act.activation(out=ss[:, 2:3], in_=ss[:, 1:2], func=AF.Ln)
            P.act.activation(out=ss[:, 2:3], in_=ss[:, 2:3], func=AF.Exp, scale=-0.5)
            P.dve.scalar_tensor_tensor(out=h[t][:], in0=rs[:, 0:D], scalar=ss[:, 2:3], in1=h[t][:], op0=ALU.mult, op1=ALU.add)
    rsr.free()
    xn_ring = P.ring([128, D], BF16, 2, "xn2")
    hnT = P.tile([128, 16, NT], BF16, "hnT")
    wbc = bcast_load(P, io["vec_xattn"], D, "wbc_xa")
    norm_T(cx, [h[t][:] for t in range(NTT)], D, wbc[:], hnT, 0, xn_ring, tp_banks)
    mnT = P.tile([128, 16, MEM], BF16, "mnT")
    wbm = bcast_load(P, io["vec_mem"], D, "wbc_mem")
    ld = P.ring([128, D], F32, 2, "memld")
    for t in range(MEM // 128):
        lt = ld.next()
        P.dma("sp", out=lt[:], in_=io["mem"][t * 128:(t + 1) * 128, :])
        norm_T(cx, [lt[:]], D, wbm[:], mnT, t * 128, xn_ring, tp_banks)
    ld.free()
    P.free(wbc, wbm)
    wq = P.tile([128, 16, 512], BF16, "wq")
    wk = P.tile([128, 16, 512], BF16, "wk")
    wv = P.tile([128, 16, 512], BF16, "wv")
    dma_w(P, wk, io["wk"], 16, 512)
    dma_w(P, wv, io["wv"], 16, 512)
    dma_w(P, wq, io["wq"], 16, 512)
    kT = P.tile([128, 4, MEM], BF16, "kT")
    vtm = P.tile([128, 2, 512], BF16, "vtm")
    qT = P.tile([128, 4, NT], BF16, "qT")
    for hd in range(4):
        pb = mm_banks.next()
        for kc in range(16):
            P.pe.matmul(out=pb[:, :MEM], lhsT=wk[:, kc, hd * 128:(hd + 1) * 128], rhs=mnT[:, kc, :], start=(kc == 0), stop=(kc == 15))
        P.act.activation(out=kT[:, hd, :], in_=pb[:, :MEM], func=AF.Copy)
    for mt in range(2):
        pb = mm_banks.next()
        for kc in range(16):
            P.pe.matmul(out=pb[:, :512], lhsT=mnT[:, kc, mt * 128:(mt + 1) * 128], rhs=wv[:, kc, :], start=(kc == 0), stop=(kc == 15))
        P.act.activation(out=vtm[:, mt, :], in_=pb[:, :512], func=AF.Copy)
    for hd in range(4):
        for th in range(NT // 512):
            pb = mm_banks.next()
            for kc in range(16):
                P.pe.matmul(out=pb[:, :512], lhsT=wq[:, kc, hd * 128:(hd + 1) * 128], rhs=hnT[:, kc, th * 512:(th + 1) * 512],
                            start=(kc == 0), stop=(kc == 15))
            P.act.activation(out=qT[:, hd, th * 512:(th + 1) * 512], in_=pb[:, :512], func=AF.Copy)
    P.free(wq, wk, wv, mnT, hnT)
    oT = P.tile([128, 4, NT], BF16, "oT")
    sc = 128 ** -0.5
    shr = P.ring([128, 4, MEM], F32, 2, "xsh")
    pnr = P.ring([128, 4, MEM], BF16, 2, "xpn")
    pTr = P.ring([128, 8, 128], BF16, 2, "xpT")
    sc_banks = Ring([P.bank(4, 2), P.bank(6, 2)], P)
    ob_banks = Ring([P.bank(2), P.bank(3)], P)
    for t in range(NTT):
        psc = sc_banks.next()
        for hd in range(4):
            P.pe.matmul(out=psc[:, hd * MEM:(hd + 1) * MEM], lhsT=qT[:, hd, t * 128:(t + 1) * 128], rhs=kT[:, hd, :], start=True, stop=True)
        ps3 = psc[:, :].rearrange("p (h m) -> p h m", h=4)
        sm = cx.small.next()
        P.dve.tensor_reduce(out=sm[:, 0:4], in_=ps3, axis=AX.X, op=ALU.max)
        sh = shr.next()
        P.dve.tensor_tensor(out=sh[:, :, :], in0=ps3, in1=bc3(sm[:, 0:4], 4, MEM), op=ALU.subtract)
        P.act.activation(out=sh[:, :, :], in_=sh[:, :, :], func=AF.Exp, scale=sc)
        P.dve.tensor_reduce(out=sm[:, 4:8], in_=sh[:, :, :], axis=AX.X, op=ALU.add)
        sm2 = cx.small.next()
        P.dve.reciprocal(out=sm2[:, 0:4], in_=sm[:, 4:8])
        pn = pnr.next()
        P.dve.tensor_tensor(out=pn[:, :, :], in0=sh[:, :, :], in1=bc3(sm2[:, 0:4], 4, MEM), op=ALU.mult)
        tb = tp_banks.next()
        for hd in range(4):
            for mc in range(2):
                j = hd * 2 + mc
                P.pe.transpose(out=tb[:, j * 128:(j + 1) * 128], in_=pn[:, hd, mc * 128:(mc + 1) * 128], identity=cx.identb)
        pT = pTr.next()
        P.act.activation(out=pT[:, :, :], in_=tb[:, :1024].rearrange("p (c t) -> p c t", c=8), func=AF.Copy)
        ob = ob_banks.next()
        for hd in range(4):
            for mc in range(2):
                P.pe.matmul(out=ob[:, hd * 128:(hd + 1) * 128], lhsT=vtm[:, mc, hd * 128:(hd + 1) * 128], rhs=pT[:, hd * 2 + mc, :],
                            start=(mc == 0), stop=(mc == 1))
        P.act.activation(out=oT[:, :, t * 128:(t + 1) * 128], in_=ob[:, :512].rearrange("p (h s) -> p h s", h=4), func=AF.Copy)
    for r_ in (shr, pnr, pTr):
        r_.free()
    P.free(qT, kT, vtm)
    wo = P.tile([128, 4, D], BF16, "wo")
    for c in range(4):
        P.dma("pool", out=wo[:, c, :], in_=io["wo"][c * 128:(c + 1) * 128, :])
    for t in range(NTT):
        for cg in range(4):
            pb = mm_banks.next()
            for hd in range(4):
                P.pe.matmul(out=pb[:, :512], lhsT=oT[:, hd, t * 128:(t + 1) * 128], rhs=wo[:, hd, cg * 512:(cg + 1) * 512],
                            start=(hd == 0), stop=(hd == 3))
            P.dve.tensor_tensor(out=h[t][:, cg * 512:(cg + 1) * 512], in0=pb[:, :512], in1=h[t][:, cg * 512:(cg + 1) * 512], op=ALU.add)
    P.free(wo, oT)
    hnT = P.tile([128, 16, NT], BF16, "hn2T")
    wbc = bcast_load(P, io["vec_ffn"], D, "wbc_ffn")
    norm_T(cx, [h[t][:] for t in range(NTT)], D, wbc[:], hnT, 0, xn_ring, tp_banks)
    P.free(wbc)
    FB = 256
    wgr = P.ring([128, 16, FB], BF16, 2, "wg")
    wur = P.ring([128, 16, FB], BF16, 2, "wu")
    wdr = P.ring([128, FB // 128, D], BF16, 2, "wd")
    actr = P.ring([128, FB // 128, NT], BF16, 2, "act")
    sgr = P.ring([128, 512], F32, 2, "sg")
    for fb in range(DFF // FB):
        wg = wgr.next()
        wu = wur.next()
        wd = wdr.next()
        dma_w(P, wg, io["w_gate"][:, fb * FB:(fb + 1) * FB], 16, FB)
        dma_w(P, wu, io["w_up"][:, fb * FB:(fb + 1) * FB], 16, FB)
        for s in range(FB // 128):
            P.dma("pool", out=wd[:, s, :], in_=io["w_down"][fb * FB + s * 128:fb * FB + (s + 1) * 128, :])
        act = actr.next()
        for s in range(FB // 128):
            for th in range(NT // 512):
                pg = mm_banks.next()
                for kc in range(16):
                    P.pe.matmul(out=pg[:, :512], lhsT=wg[:, kc, s * 128:(s + 1) * 128], rhs=hnT[:, kc, th * 512:(th + 1) * 512],
                                start=(kc == 0), stop=(kc == 15))
                pu = mm_banks.next()
                for kc in range(16):
                    P.pe.matmul(out=pu[:, :512], lhsT=wu[:, kc, s * 128:(s + 1) * 128], rhs=hnT[:, kc, th * 512:(th + 1) * 512],
                                start=(kc == 0), stop=(kc == 15))
                sg = sgr.next()
                P.act.activation(out=sg[:], in_=pg[:, :512], func=AF.Silu)
                P.dve.tensor_tensor(out=act[:, s, th * 512:(th + 1) * 512], in0=pu[:, :512], in1=sg[:], op=ALU.mult)
        for t in range(NTT):
            for cg in range(4):
                pb = mm_banks.next()
                for s in range(FB // 128):
                    P.pe.matmul(out=pb[:, :512], lhsT=act[:, s, t * 128:(t + 1) * 128], rhs=wd[:, s, cg * 512:(cg + 1) * 512],
                                start=(s == 0), stop=(s == FB // 128 - 1))
                P.dve.tensor_tensor(out=h[t][:, cg * 512:(cg + 1) * 512], in0=pb[:, :512], in1=h[t][:, cg * 512:(cg + 1) * 512], op=ALU.add)
    for r in (wgr, wur, wdr, actr, sgr):
        r.free()
    P.free(hnT)
    outs = []
    if final:
        wbc = bcast_load(P, io["vec_final"], D, "wbc_fin")
        orr = P.ring([128, D], F32, 2, "fin")
        for t in range(NTT):
            ss = cx.small.next()
            ot = orr.next()
            P.act.activation(out=ot[:], in_=h[t][:], func=AF.Square, accum_out=ss[:, 0:1])
            P.dve.tensor_scalar(out=ss[:, 1:2], in0=ss[:, 0:1], scalar1=1.0 / D, scalar2=EPS, op0=ALU.mult, op1=ALU.add)
            P.act.activation(out=ss[:, 2:3], in_=ss[:, 1:2], func=AF.Ln)
            P.act.activation(out=ss[:, 2:3], in_=ss[:, 2:3], func=AF.Exp, scale=-0.5)
            P.dve.scalar_tensor_tensor(out=ot[:], in0=h[t][:], scalar=ss[:, 2:3], in1=wbc[:], op0=ALU.mult, op1=ALU.mult)
            outs.append(P.dma("sp", out=io["hout"][t * 128:(t + 1) * 128, :], in_=ot[:]))
        orr.free()
        P.free(wbc)
    else:
        wbc = bcast_load(P, io["vec_mix_next"], D, "wbc_nx")
        ur = P.ring([128, D], BF16, 2, "ub")
        for t in range(NTT):
            outs.append(P.dma("sp", out=io["hout"][t * 128:(t + 1) * 128, :], in_=h[t][:]))
            ss = cx.small.next()
            ub = ur.next()
            P.act.activation(out=ub[:], in_=h[t][:], func=AF.Square, accum_out=ss[:, 0:1])
            P.dve.tensor_scalar(out=ss[:, 1:2], in0=ss[:, 0:1], scalar1=1.0 / D, scalar2=EPS, op0=ALU.mult, op1=ALU.add)
            P.act.activation(out=ss[:, 2:3], in_=ss[:, 1:2], func=AF.Ln)
            P.act.activation(out=ss[:, 2:3], in_=ss[:, 2:3], func=AF.Exp, scale=-0.5)
            P.dve.scalar_tensor_tensor(out=ub[:], in0=h[t][:], scalar=ss[:, 2:3], in1=wbc[:], op0=ALU.mult, op1=ALU.mult)
            outs.append(P.dma("sp", out=io["u_half"][t // 2][(t % 2) * 128:(t % 2 + 1) * 128, :], in_=ub[:]))
    P.release(mark)
    return outs


SEQ = 2048
NTS = SEQ // 128
NC2 = 512


def softplus_inplace(P, x, tmp):
    P.dve.tensor_scalar(out=tmp, in0=x, scalar1=-1.0, scalar2=None, op0=ALU.mult)
    P.dve.tensor_tensor(out=tmp, in0=tmp, in1=x, op=ALU.max)
    P.act.activation(out=tmp, in_=tmp, func=AF.Exp, scale=-1.0)
    P.act.activation(out=tmp, in_=tmp, func=AF.Ln, bias=1.0, scale=1.0)
    P.dve.tensor_scalar(out=x, in0=x, scalar1=0.0, scalar2=None, op0=ALU.max)
    P.dve.tensor_tensor(out=x, in0=x, in1=tmp, op=ALU.add)


def conv_silu_fm(P, cx, w_t, uT, col0, xpad, acc, cw, cb, dst_bf, mm_banks, silu=True):
    for th in range(SEQ // 512):
        pb = mm_banks.next()
        for kc in range(16):
            P.pe.matmul(out=pb[:, :512], lhsT=w_t[:, kc, col0:col0 + 128], rhs=uT[:, kc, th * 512:(th + 1) * 512],
                        start=(kc == 0), stop=(kc == 15))
        P.act.activation(out=xpad[:, 3 + th * 512:3 + (th + 1) * 512], in_=pb[:, :512], func=AF.Copy)
    if cb is not None:
        P.dve.tensor_scalar(out=acc[:, :], in0=xpad[:, 3:3 + SEQ], scalar1=cw[:, 3:4], scalar2=cb, op0=ALU.mult, op1=ALU.add)
    else:
        P.dve.tensor_scalar(out=acc[:, :], in0=xpad[:, 3:3 + SEQ], scalar1=cw[:, 3:4], scalar2=None, op0=ALU.mult)
    for k in range(3):
        P.dve.scalar_tensor_tensor(out=acc[:, :], in0=xpad[:, k:k + SEQ], scalar=cw[:, k:k + 1], in1=acc[:, :], op0=ALU.mult, op1=ALU.add)
    P.act.activation(out=dst_bf, in_=acc[:, :], func=AF.Silu if silu else AF.Copy)


def fm_to_tm(P, cx, src_list, dst, tp_banks, parity=[0]):
    n = len(src_list)
    for t in range(NTS):
        for c0 in range(0, n, 8):
            m = min(8, n - c0)
            pb = tp_banks.next()
            for c in range(m):
                P.pe.transpose(out=pb[:, c * 128:(c + 1) * 128], in_=src_list[c0 + c][:, t * 128:(t + 1) * 128], identity=cx.identb)
            parity[0] ^= 1
            if parity[0]:
                P.dve.tensor_copy(out=dst[:, t, c0 * 128:(c0 + m) * 128], in_=pb[:, :m * 128])
            else:
                P.act.activation(out=dst[:, t, c0 * 128:(c0 + m) * 128], in_=pb[:, :m * 128], func=AF.Copy)


def partial_proj(P, cx, src, K, w_dram, part, norm_vec=None, sumsq=False):
    mark = P.mark()
    KC = K // 128
    tp_banks = Ring([P.bank_bf(0), P.bank_bf(1)], P)
    mm_banks = Ring([P.bank(2), P.bank(3), P.bank(4), P.bank(5)], P)
    srcT = P.tile([128, KC, SEQ], BF16, "srcT")
    ld = P.ring([128, K], F32, 2, "ppld")
    xnr = P.ring([128, K], BF16, 2, "ppxn")
    wbc = bcast_load(P, norm_vec, K, "ppw") if norm_vec is not None else None
    sqr = P.ring([128, 16], F32, 2, "ppsq") if sumsq else None
    outs = []
    for t in range(NTS):
        lt = ld.next()
        P.dma("sp", out=lt[:], in_=src[t * 128:(t + 1) * 128, :])
        xn = xnr.next()
        if sumsq:
            sq = sqr.next()
            P.dve.memset(sq[:, :], 0.0)
            P.act.activation(out=xn[:, :], in_=lt[:, :], func=AF.Square, accum_out=sq[:, 0:1])
            outs.append(P.dma("sp", out=part[t * 128:(t + 1) * 128, 2048:2064], in_=sq[:, :]))
        if wbc is not None:
            P.dve.tensor_tensor(out=xn[:, :], in0=lt[:, :], in1=wbc[:, :], op=ALU.mult)
        else:
            P.dve.tensor_copy(out=xn[:, :], in_=lt[:, :])
        for j, c0 in enumerate(range(0, KC, 8)):
            pb = tp_banks.next()
            n = min(8, KC - c0)
            for c in range(n):
                P.pe.transpose(out=pb[:, c * 128:(c + 1) * 128], in_=xn[:, (c0 + c) * 128:(c0 + c + 1) * 128], identity=cx.identb)
            P.act.activation(out=srcT[:, c0:c0 + n, t * 128:(t + 1) * 128], in_=pb[:, :n * 128].rearrange("p (c t) -> p c t", c=n), func=AF.Copy)
    ld.free()
    xnr.free()
    wring = P.ring([128, KC, 512], BF16, 2, "ppwr")
    stg = P.ring([128, 512], F32, 3, "ppst")
    for cg in range(4):
        wt = wring.next()
        dma_w(P, wt, w_dram[:, cg * 512:(cg + 1) * 512], KC, 512)
        for t in range(NTS):
            pb = mm_banks.next()
            for kc in range(KC):
                P.pe.matmul(out=pb[:, :512], lhsT=srcT[:, kc, t * 128:(t + 1) * 128], rhs=wt[:, kc, :], start=(kc == 0), stop=(kc == KC - 1))
            st = stg.next()
            if t % 2 == 0:
                P.dve.tensor_copy(out=st[:, :], in_=pb[:, :512])
            else:
                P.act.activation(out=st[:, :], in_=pb[:, :512], func=AF.Copy)
            outs.append(P.dma("sp", out=part[t * 128:(t + 1) * 128, cg * 512:(cg + 1) * 512], in_=st[:, :]))
    P.release(mark)
    return outs


def stage_ssd(P, cx, io):
    GW = 1288
    mark = P.mark()
    tp_banks = Ring([P.bank_bf(0), P.bank_bf(1)], P)
    mm_banks = Ring([P.bank(2), P.bank(3)], P)
    c2 = P.tile([128, NC2], F32, "c2")
    P.dma("sp", out=c2[:], in_=io["c2"][:, :])
    U2, SL2, HA, HB = c2[:, 0:128], c2[:, 128:256], c2[:, 256:384], c2[:, 384:512]
    uT = P.tile([128, 16, SEQ], BF16, "uT")
    ld = P.ring([128, D], BF16, 2, "ld")
    for t in range(NTS):
        lt = ld.next()
        r0_ = (t // 8) * 256 + (t % 2) * 128
        P.dma("sp", out=lt[:], in_=io["u_full"][(t % 8) // 2][r0_:r0_ + 128, :])
        for j, c0 in enumerate((0, 8)):
            pb = tp_banks.next()
            for c in range(8):
                P.pe.transpose(out=pb[:, c * 128:(c + 1) * 128], in_=lt[:, (c0 + c) * 128:(c0 + c + 1) * 128], identity=cx.identb)
            o = uT[:, c0:c0 + 8, t * 128:(t + 1) * 128]
            sv = pb[:, :1024].rearrange("p (c t) -> p c t", c=8)
            if j == 0:
                P.dve.tensor_copy(out=o, in_=sv)
            else:
                P.act.activation(out=o, in_=sv, func=AF.Copy)
    ld.free()
    cwx = P.tile([128, 16, 4], F32, "cwx"); P.dma("sp", out=cwx[:], in_=io["cwx"][:, :, :])
    cbx = P.tile([128, 16], F32, "cbx"); P.dma("sp", out=cbx[:], in_=io["cbx"][:, :])
    cwbc = P.tile([128, 8, 4], F32, "cwbc"); P.dma("sp", out=cwbc[:], in_=io["cwbc"][:, :, :])
    cbbc = P.tile([128, 8], F32, "cbbc"); P.dma("sp", out=cbbc[:], in_=io["cbbc"][:, :])
    dtb = bcast_load(P, io["dtb"], 32, "dtb")
    aneg = bcast_load(P, io["alog"], 32, "aneg")
    P.act.activation(out=aneg[:], in_=aneg[:], func=AF.Exp)
    P.dve.tensor_scalar(out=aneg[:], in0=aneg[:], scalar1=-1.0, scalar2=None, op0=ALU.mult)
    outs = []
    wAr = P.ring([128, 16, 512], BF16, 2, "wA")
    wSr = P.ring([128, 16, 264], BF16, 2, "wS")
    for g in range(4):
        wA = wAr.next()
        wS = wSr.next()
        wZ = wAr.next()
        dma_w(P, wA, io["w_in"][:, g * GW:g * GW + 512], 16, 512)
        dma_w(P, wS, io["w_in"][:, g * GW + 1024:g * GW + 1288], 16, 264)
        dma_w(P, wZ, io["w_in"][:, g * GW + 512:g * GW + 1024], 16, 512)
        xpad = P.tile([128, SEQ + 3], F32, "xpad")
        P.dve.memset(xpad[:, 0:3], 0.0)
        acc = P.tile([128, SEQ], F32, "acc")
        fm = [P.tile([128, SEQ], BF16, f"fm{i}") for i in range(6)]
        for cc in range(4):
            conv_silu_fm(P, cx, wA, uT, cc * 128, xpad, acc, cwx[:, g * 4 + cc, :], cbx[:, g * 4 + cc:g * 4 + cc + 1], fm[cc][:, :], mm_banks)
        conv_silu_fm(P, cx, wS, uT, 0, xpad, acc, cwbc[:, g, :], cbbc[:, g:g + 1], fm[4][:, :], mm_banks)
        conv_silu_fm(P, cx, wS, uT, 128, xpad, acc, cwbc[:, 4 + g, :], cbbc[:, 4 + g:5 + g], fm[5][:, :], mm_banks)
        BT, CT = fm[4], fm[5]
        dt = P.tile([128, NTS, 8], F32, "dt")
        for t in range(NTS):
            pb = mm_banks.next()
            for kc in range(16):
                P.pe.matmul(out=pb[:, :8], lhsT=uT[:, kc, t * 128:(t + 1) * 128], rhs=wS[:, kc, 256:264], start=(kc == 0), stop=(kc == 15))
            P.dve.tensor_tensor(out=dt[:, t, :], in0=pb[:, :8], in1=dtb[:, g * 8:(g + 1) * 8], op=ALU.add)
        tmp = P.tile([128, NTS, 8], F32, "dttmp")
        softplus_inplace(P, dt[:, :, :], tmp[:, :, :])
        av = P.tile([128, NTS, 8], F32, "av")
        P.dve.tensor_tensor(out=av[:, :, :], in0=dt[:, :, :], in1=aneg[:, g * 8:(g + 1) * 8].unsqueeze(1).to_broadcast([128, NTS, 8]), op=ALU.mult)
        P.free(tmp, xpad, acc)
        sz = P.tile([128, NTS, 512], BF16, "sz")
        for t in range(NTS):
            pb = mm_banks.next()
            for kc in range(16):
                P.pe.matmul(out=pb[:, :512], lhsT=uT[:, kc, t * 128:(t + 1) * 128], rhs=wZ[:, kc, :], start=(kc == 0), stop=(kc == 15))
            P.act.activation(out=sz[:, t, :], in_=pb[:, :512], func=AF.Silu)
        x_tm = P.tile([128, NTS, 512], BF16, "x_tm")
        fm_to_tm(P, cx, [fm[i][:, :] for i in range(4)], x_tm, tp_banks)
        B_tm = P.tile([128, NTS, 128], BF16, "B_tm")
        fm_to_tm(P, cx, [BT[:, :]], B_tm, tp_banks)
        P.free(*fm[:4])
        dfull = bcast_load(P, io["dfull"][:, g * 512:(g + 1) * 512], 512, "dfull")
        S = P.tile([128, 512], F32, "S")
        P.dve.memset(S[:, :], 0.0)
        Sb = P.ring([128, 512], BF16, 3, "Sb")
        S0b = Sb.next()
        P.dve.memset(S0b[:, :], 0.0)
        aUr = P.ring([128, 8, 128], F32, 2, "aU")
        Lr = P.ring([128, 8, 128], F32, 2, "L")
        Mr = P.ring([128, 8, 128], BF16, 2, "M")
        mcbr = P.ring([128, 128], F32, 2, "mcb")
        xcr = P.ring([128, 512], BF16, 2, "xc")
        xdr = P.ring([128, 512], BF16, 2, "xdec")
        yor = P.ring([128, 512], F32, 2, "yoff")
        ygr = P.ring([128, 512], F32, 2, "yg")
        ps_E = P.bank(4, 2)
        ps_sm = P.bank(6)
        ps_y = P.bank(7)
        def ssd_f1(t):
            a_t = av[:, t, :]
            P.pe.matmul(out=ps_sm[:, 0:8], lhsT=U2, rhs=a_t, start=True, stop=True)
            P.pe.matmul(out=ps_sm[:, 8:16], lhsT=SL2, rhs=a_t, start=True, stop=True)
            P.pe.matmul(out=ps_sm[:, 16:24], lhsT=HA, rhs=a_t, start=True, stop=True)
            P.pe.matmul(out=ps_sm[:, 24:32], lhsT=HB, rhs=a_t, start=True, stop=True)
            P.pe.matmul(out=ps_sm[:, 128:256], lhsT=BT[:, t * 128:(t + 1) * 128], rhs=CT[:, t * 128:(t + 1) * 128], start=True, stop=True)
            sm = cx.small32.next()
            P.act.activation(out=sm[:, 0:32], in_=ps_sm[:, 0:32], func=AF.Exp)
            mcb = mcbr.next()
            P.dve.tensor_tensor(out=mcb[:, :], in0=ps_sm[:, 128:256], in1=U2, op=ALU.mult)
            aU = aUr.next()
            P.dve.tensor_tensor(out=aU[:, :, :], in0=U2.unsqueeze(1).to_broadcast([128, 8, 128]),
                                in1=a_t.unsqueeze(2).to_broadcast([128, 8, 128]), op=ALU.mult)
            return sm, mcb, aU

        def ssd_f2(t, sm, mcb, aU):
            for q in range(2):
                P.pe.matmul(out=ps_E[:, q * 512:(q + 1) * 512], lhsT=SL2, rhs=aU[:, q * 4:(q + 1) * 4, :], start=True, stop=True)
            L = Lr.next()
            P.act.activation(out=L[:, :, :], in_=ps_E[:, :].rearrange("p (h i) -> p h i", h=8), func=AF.Exp)
            M = Mr.next()
            P.dve.tensor_tensor(out=M[:, :, :], in0=L[:, :, :], in1=mcb[:, :].unsqueeze(1).to_broadcast([128, 8, 128]), op=ALU.mult)
            xc = xcr.next()
            P.dve.tensor_tensor(out=xc[:, :].rearrange("p (h q) -> p h q", h=8), in0=x_tm[:, t, :].rearrange("p (h q) -> p h q", h=8),
                                in1=dt[:, t, :].unsqueeze(2).to_broadcast([128, 8, 64]), op=ALU.mult)
            xd = xdr.next()
            P.dve.tensor_tensor(out=xd[:, :].rearrange("p (h q) -> p h q", h=8), in0=xc[:, :].rearrange("p (h q) -> p h q", h=8),
                                in1=sm[:, 8:16].unsqueeze(2).to_broadcast([128, 8, 64]), op=ALU.mult)
            return sm, M, xc, xd

        def ssd_back(t, S0b, sm, M, xc, xd):
            ps_st = mm_banks.next()
            P.pe.matmul(out=ps_st[:, :512], lhsT=B_tm[0:64, t, :], rhs=xd[0:64, :], start=True, stop=True)
            S1b = Sb.next()
            P.dve.tensor_tensor(out=S[:, :].rearrange("p (h q) -> p h q", h=8), in0=S[:, :].rearrange("p (h q) -> p h q", h=8),
                                in1=sm[:, 16:24].unsqueeze(2).to_broadcast([128, 8, 64]), op=ALU.mult)
            P.dve.tensor_tensor(out=S[:, :], in0=ps_st[:, :512], in1=S[:, :], op=ALU.add)
            P.act.activation(out=S1b[:, :], in_=S[:, :], func=AF.Copy)
            ps_st2 = mm_banks.next()
            P.pe.matmul(out=ps_st2[:, :512], lhsT=B_tm[64:128, t, :], rhs=xd[64:128, :], start=True, stop=True)
            S2b = Sb.next()
            P.dve.tensor_tensor(out=S[:, :].rearrange("p (h q) -> p h q", h=8), in0=S[:, :].rearrange("p (h q) -> p h q", h=8),
                                in1=sm[:, 24:32].unsqueeze(2).to_broadcast([128, 8, 64]), op=ALU.mult)
            P.dve.tensor_tensor(out=S[:, :], in0=ps_st2[:, :512], in1=S[:, :], op=ALU.add)
            P.act.activation(out=S2b[:, :], in_=S[:, :], func=AF.Copy)
            ps_o = mm_banks.next()
            P.pe.matmul(out=ps_o[0:64, :512], lhsT=CT[:, t * 128:t * 128 + 64], rhs=S0b[:, :], start=True, stop=True)
            P.pe.matmul(out=ps_o[64:128, :512], lhsT=CT[:, t * 128 + 64:(t + 1) * 128], rhs=S1b[:, :], start=True, stop=True)
            yo = yor.next()
            P.dve.tensor_tensor(out=yo[:, :].rearrange("p (h q) -> p h q", h=8), in0=ps_o[:, :512].rearrange("p (h q) -> p h q", h=8),
                                in1=sm[:, 0:8].unsqueeze(2).to_broadcast([128, 8, 64]), op=ALU.mult)
            for hh in range(8):
                P.pe.matmul(out=ps_y[:, hh * 64:(hh + 1) * 64], lhsT=M[:, hh, :], rhs=xc[:, hh * 64:(hh + 1) * 64], start=True, stop=True)
            P.dve.tensor_tensor(out=yo[:, :], in0=ps_y[:, :512], in1=yo[:, :], op=ALU.add)
            yg = ygr.next()
            P.dve.tensor_tensor(out=yg[:, :], in0=x_tm[:, t, :], in1=dfull[:, :], op=ALU.mult)
            P.dve.tensor_tensor(out=yg[:, :], in0=yg[:, :], in1=yo[:, :], op=ALU.add)
            P.dve.tensor_tensor(out=yg[:, :], in0=yg[:, :], in1=sz[:, t, :], op=ALU.mult)
            outs.append(P.dma("sp", out=io["yg"][t * 128:(t + 1) * 128, g * 512:(g + 1) * 512], in_=yg[:, :]))
            return S2b

        st1, st2 = {}, {}
        for step in range(NTS + 2):
            if step < NTS:
                st1[step] = ssd_f1(step)
            if 0 <= step - 1 < NTS:
                st2[step - 1] = ssd_f2(step - 1, *st1.pop(step - 1))
            if 0 <= step - 2 < NTS:
                S0b = ssd_back(step - 2, S0b, *st2.pop(step - 2))
        for r in (Sb, aUr, Lr, Mr, mcbr, xcr, xdr, yor, ygr):
            r.free()
        P.free(S, dfull, x_tm, B_tm, BT, CT, sz, dt, av)
    P.release(mark)
    outs += partial_proj(P, cx, io["yg"], 2048, io["w_out"], io["part"], norm_vec=io["vec_ssdnorm"], sumsq=True)
    return outs


L2EPS = 1e-6
def proj_fm(P, w_t, uT, col0, dst, mm_banks, func=None):
    for th in range(SEQ // 512):
        pb = mm_banks.next()
        for kc in range(16):
            P.pe.matmul(out=pb[:, :512], lhsT=w_t[:, kc, col0:col0 + 128], rhs=uT[:, kc, th * 512:(th + 1) * 512],
                        start=(kc == 0), stop=(kc == 15))
        P.act.activation(out=dst[:, th * 512:(th + 1) * 512], in_=pb[:, :512], func=func or AF.Copy)


def proj_tm(P, w_t, uT, col0, ncols, dst, mm_banks, func=None):
    for t in range(NTS):
        pb = mm_banks.next()
        for kc in range(16):
            P.pe.matmul(out=pb[:, :ncols], lhsT=uT[:, kc, t * 128:(t + 1) * 128], rhs=w_t[:, kc, col0:col0 + ncols],
                        start=(kc == 0), stop=(kc == 15))
        P.act.activation(out=dst[:, t, :], in_=pb[:, :ncols], func=func or AF.Copy)


def bc3(v, n_mid, n_last):
    return v.unsqueeze(2).to_broadcast([v.shape[0], n_mid, n_last])


def bcm(v, n_mid):
    return v.unsqueeze(1).to_broadcast([v.shape[0], n_mid, v.shape[1]])


def head_norm_out(P, cx, o, gate_t, wn, dst_dram, nh, rings):
    sq = rings["sq"].next()
    P.dve.tensor_tensor(out=sq[:, :, :], in0=o, in1=o, op=ALU.mult)
    sm = cx.small.next()
    P.dve.tensor_reduce(out=sm[:, 0:nh], in_=sq[:, :, :], axis=AX.X, op=ALU.add)
    P.dve.tensor_scalar(out=sm[:, 0:nh], in0=sm[:, 0:nh], scalar1=1.0 / 128, scalar2=EPS, op0=ALU.mult, op1=ALU.add)
    P.act.activation(out=sm[:, 4:4 + nh], in_=sm[:, 0:nh], func=AF.Ln)
    P.act.activation(out=sm[:, 4:4 + nh], in_=sm[:, 4:4 + nh], func=AF.Exp, scale=-0.5)
    y = rings["y"].next()
    P.dve.tensor_tensor(out=y[:, :, :], in0=o, in1=bc3(sm[:, 4:4 + nh], nh, 128), op=ALU.mult)
    yf = y[:, :, :].rearrange("p h d -> p (h d)")
    P.dve.tensor_tensor(out=yf, in0=yf, in1=wn, op=ALU.mult)
    P.dve.tensor_tensor(out=yf, in0=yf, in1=gate_t, op=ALU.mult)
    return P.dma("sp", out=dst_dram, in_=yf)


def stage_hy(P, cx, io):
    mark = P.mark()
    tp_banks = Ring([P.bank_bf(0), P.bank_bf(1)], P)
    mm_banks = Ring([P.bank(2), P.bank(3)], P)
    c2 = P.tile([128, NC2], F32, "c2")
    P.dma("sp", out=c2[:], in_=io["c2"][:, :])
    U2, SL2, HA, HB = c2[:, 0:128], c2[:, 128:256], c2[:, 256:384], c2[:, 384:512]
    onesb = P.tile([128, 128], BF16, "onesb")
    P.dve.memset(onesb[:, :], 1.0)
    uT = P.tile([128, 16, SEQ], BF16, "uT")
    wbc = bcast_load(P, io["vec_mix"], D, "wbc")
    ld = P.ring([128, D], F32, 2, "ld")
    xn_ring = P.ring([128, D], BF16, 2, "xn")
    for t in range(NTS):
        lt = ld.next()
        P.dma("sp", out=lt[:], in_=io["xfull"][t * 128:(t + 1) * 128, :])
        norm_T(cx, [lt[:]], D, wbc[:], uT, t * 128, xn_ring, tp_banks)
    ld.free()
    xn_ring.free()
    P.free(wbc)
    outs = []
    lbl = P.tile([128, 8], F32, "lbl")
    P.dma("sp", out=lbl[:], in_=io["lbl"][:, :])
    lb = P.tile([128, 16], F32, "lb")
    P.dve.tensor_tensor(out=lb[:, 12:16], in0=lbl[:, 0:4], in1=lbl[:, 4:8], op=ALU.subtract)
    P.act.activation(out=lb[:, 0:4], in_=lb[:, 12:16], func=AF.Sigmoid)
    P.dve.tensor_scalar(out=lb[:, 4:8], in0=lb[:, 0:4], scalar1=-1.0, scalar2=1.0, op0=ALU.mult, op1=ALU.add)
    P.dve.tensor_scalar(out=lb[:, 8:12], in0=lb[:, 4:8], scalar1=-1.0, scalar2=None, op0=ALU.mult)
    rmask = P.tile([128, SEQ], BF16, "rmask")
    P.dve.memset(rmask[:, :], 1.0)
    P.dve.memset(rmask[:, :].rearrange("p (c j) -> p c j", j=64)[:, :, 0:1], 0.0)
    wtm = P.tile([128, 16, 512], BF16, "wtm")
    wtm2 = P.tile([128, 16, 512], BF16, "wtmb")
    dma_w(P, wtm, io["w_in"][:, 1024:1536], 16, 512)
    dma_w(P, wtm2, io["w_in"][:, 1536:2048], 16, 512)
    v_tm = P.tile([128, NTS, 512], BF16, "v_tm")
    proj_tm(P, wtm, uT, 0, 512, v_tm, mm_banks)
    sg_tm = P.tile([128, NTS, 512], BF16, "sg_tm")
    proj_tm(P, wtm2, uT, 0, 512, sg_tm, mm_banks, func=AF.Silu)
    P.free(wtm, wtm2)
    wn = bcast_load(P, io["hgrn_norm"], 512, "wn")
    rings1 = None
    qt_l, kt_l, kd_l, egl_l = [], [], [], []
    whr = P.ring([128, 16, 256], BF16, 2, "wh")
    for h in range(4):
        wh = whr.next()
        dma_w(P, wh, io["w_in"][:, h * 256:(h + 1) * 256], 16, 256)
        qf = P.tile([128, SEQ], BF16, "qf")
        ff = P.tile([128, SEQ], F32, "ff")
        proj_fm(P, wh, uT, 0, qf[:, :], mm_banks, func=AF.Silu)
        proj_fm(P, wh, uT, 128, ff[:, :], mm_banks, func=AF.Sigmoid)
        lf = P.tile([128, SEQ], F32, "lf")
        P.dve.tensor_scalar(out=lf[:, :], in0=ff[:, :], scalar1=lb[:, 4 + h:5 + h], scalar2=lb[:, h:h + 1], op0=ALU.mult, op1=ALU.add)
        P.act.activation(out=lf[:, :], in_=lf[:, :], func=AF.Ln)
        kk = P.tile([128, SEQ], F32, "kk")
        P.dve.tensor_scalar(out=kk[:, :], in0=ff[:, :], scalar1=lb[:, 8 + h:9 + h], scalar2=lb[:, 4 + h:5 + h], op0=ALU.mult, op1=ALU.add)
        g = ff
        P.dve.tensor_tensor_scan(out=g[:, :], data0=rmask[:, :], data1=lf[:, :], initial=0.0, op0=ALU.mult, op1=ALU.add)
        eg = lf
        P.act.activation(out=eg[:, :], in_=g[:, :], func=AF.Exp)
        qt = P.tile([128, SEQ], BF16, "qt")
        P.dve.tensor_tensor(out=qt[:, :], in0=qf[:, :], in1=eg[:, :], op=ALU.mult)
        eng = qf
        P.act.activation(out=eng[:, :], in_=g[:, :], func=AF.Exp, scale=-1.0)
        kt = P.tile([128, SEQ], BF16, "kt")
        P.dve.tensor_tensor(out=kk[:, :], in0=kk[:, :], in1=eng[:, :], op=ALU.mult)
        P.act.activation(out=kt[:, :], in_=kk[:, :], func=AF.Copy)
        egl = P.tile([128, 32], F32, "egl")
        P.dve.tensor_copy(out=egl[:, :], in_=eg[:, :].rearrange("p (c j) -> p c j", j=64)[:, :, 63])
        kd = qf
        P.dve.tensor_tensor(out=kd[:, :].rearrange("p (c j) -> p c j", j=64), in0=kk[:, :].rearrange("p (c j) -> p c j", j=64),
                            in1=bc3(egl[:, :], 32, 64), op=ALU.mult)
        kd_tm = P.tile([128, NTS, 128], BF16, "kd_tm")
        fm_to_tm(P, cx, [kd[:, :]], kd_tm, tp_banks)
        P.free(qf, ff, lf, kk)
        qt_l.append(qt); kt_l.append(kt); kd_l.append(kd_tm); egl_l.append(egl)
    whr.free()
    qb = Ring([P.psum_slice(bk, 0, 128) for bk in range(2, 8)], P)
    S_l, Sb_l, S0_l = [], [], []
    for h in range(4):
        S = P.tile([128, 128], F32, "S")
        P.dve.memset(S[:, :], 0.0)
        Sb = P.ring([128, 128], BF16, 3, "Sb")
        S0b = Sb.next()
        P.dve.memset(S0b[:, :], 0.0)
        S_l.append(S); Sb_l.append(Sb); S0_l.append(S0b)
    Amr = P.ring([128, 128], BF16, 8, "Am")
    oir = P.ring([128, 128], F32, 4, "oi")
    o1r = P.ring([128, 1, 128], F32, 1, "o1")
    o4r = P.ring([128, 4, 128], F32, 2, "o4")
    rings4 = {"sq": P.ring([128, 4, 128], F32, 2, "sq4"), "y": P.ring([128, 4, 128], F32, 2, "y4")}
    for t in range(NTS):
        ts = slice(t * 128, (t + 1) * 128)
        keep = {}
        for h in range(4):
            qt, kt, kd_tm, egl, S, Sb, S0b = qt_l[h], kt_l[h], kd_l[h], egl_l[h], S_l[h], Sb_l[h], S0_l[h]
            pa = qb.next()
            P.pe.matmul(out=pa[:, :], lhsT=kt[:, ts], rhs=qt[:, ts], start=True, stop=True)
            Am = Amr.next()
            P.dve.tensor_tensor(out=Am[:, :], in0=pa[:, :], in1=U2, op=ALU.mult)
            ps1 = qb.next()
            P.pe.matmul(out=ps1[:, :], lhsT=kd_tm[0:64, t, :], rhs=v_tm[0:64, t, h * 128:(h + 1) * 128], start=True, stop=True)
            P.dve.scalar_tensor_tensor(out=S[:, :], in0=S[:, :], scalar=egl[:, 2 * t:2 * t + 1], in1=ps1[:, :], op0=ALU.mult, op1=ALU.add)
            S1b = Sb.next()
            P.act.activation(out=S1b[:, :], in_=S[:, :], func=AF.Copy)
            ps2 = qb.next()
            P.pe.matmul(out=ps2[:, :], lhsT=kd_tm[64:128, t, :], rhs=v_tm[64:128, t, h * 128:(h + 1) * 128], start=True, stop=True)
            P.dve.scalar_tensor_tensor(out=S[:, :], in0=S[:, :], scalar=egl[:, 2 * t + 1:2 * t + 2], in1=ps2[:, :], op0=ALU.mult, op1=ALU.add)
            S2b = Sb.next()
            P.act.activation(out=S2b[:, :], in_=S[:, :], func=AF.Copy)
            keep[h] = (Am, S1b, S2b)
        o4 = o4r.next()
        for h in range(4):
            qt, S0b = qt_l[h], S0_l[h]
            Am, S1b, S2b = keep[h]
            pi = qb.next()
            P.pe.matmul(out=pi[0:64, :], lhsT=qt[:, t * 128:t * 128 + 64], rhs=S0b[:, :], start=True, stop=True)
            P.pe.matmul(out=pi[64:128, :], lhsT=qt[:, t * 128 + 64:(t + 1) * 128], rhs=S1b[:, :], start=True, stop=True)
            oi = oir.next()
            P.act.activation(out=oi[:, :], in_=pi[:, :], func=AF.Copy)
            po = qb.next()
            P.pe.matmul(out=po[:, :], lhsT=Am[:, :], rhs=v_tm[:, t, h * 128:(h + 1) * 128], start=True, stop=True)
            P.dve.tensor_tensor(out=o4[:, h, :], in0=po[:, :], in1=oi[:, :], op=ALU.add)
            S0_l[h] = S2b
        outs.append(head_norm_out(P, cx, o4[:, :, :], sg_tm[:, t, :], wn[:, :], io["omix"][t * 128:(t + 1) * 128, 0:512], 4, rings4))
    for r in Sb_l + [Amr, oir, o1r, o4r, rings4["sq"], rings4["y"]]:
        r.free()
    P.free(*(S_l + qt_l + kt_l + kd_l + egl_l))
    P.free(v_tm, sg_tm, wn, rmask, lb, lbl)
    rings = {"sq": P.ring([128, 4, 128], F32, 2, "sq"), "y": P.ring([128, 4, 128], F32, 2, "y")}
    cw = P.tile([128, 12, 4], F32, "cw")
    P.dma("sp", out=cw[:], in_=io["gcw"][:, :, :])
    qn = [P.tile([128, SEQ], BF16, f"qn{h}") for h in range(4)]
    kn = [P.tile([128, SEQ], BF16, f"kn{h}") for h in range(4)]
    kv_tm = [P.tile([128, NTS, 256], BF16, f"kv{h}") for h in range(4)]
    whgr = P.ring([128, 16, 384], BF16, 2, "whg")
    for h in range(4):
        wh = whgr.next()
        dma_w(P, wh, io["w_in"][:, 2048 + h * 384:2048 + (h + 1) * 384], 16, 384)
        xpad = P.tile([128, SEQ + 3], F32, "xpad")
        P.dve.memset(xpad[:, 0:3], 0.0)
        acc = P.tile([128, SEQ], F32, "acc")
        sq = P.tile([128, SEQ], BF16, "sqb")
        rn = P.tile([128, SEQ], F32, "rn")
        vb = P.tile([128, SEQ], BF16, "vb")
        for j, dst in enumerate((qn[h], kn[h])):
            conv_silu_fm(P, cx, wh, uT, j * 128, xpad, acc, cw[:, h * 3 + j, :], None, acc[:, :], mm_banks)
            P.act.activation(out=sq[:, :], in_=acc[:, :], func=AF.Square)
            for th in range(4):
                pb = mm_banks.next()
                P.pe.matmul(out=pb[:, :512], lhsT=onesb[:, :], rhs=sq[:, th * 512:(th + 1) * 512], start=True, stop=True)
                P.dve.tensor_scalar(out=rn[:, th * 512:(th + 1) * 512], in0=pb[:, :512], scalar1=L2EPS, scalar2=None, op0=ALU.add)
                P.act.activation(out=rn[:, th * 512:(th + 1) * 512], in_=rn[:, th * 512:(th + 1) * 512], func=AF.Ln)
                P.act.activation(out=rn[:, th * 512:(th + 1) * 512], in_=rn[:, th * 512:(th + 1) * 512], func=AF.Exp, scale=-0.5)
            if j == 0:
                P.dve.scalar_tensor_tensor(out=dst[:, :], in0=acc[:, :], scalar=128 ** -0.5, in1=rn[:, :], op0=ALU.mult, op1=ALU.mult)
            else:
                P.dve.tensor_tensor(out=dst[:, :], in0=acc[:, :], in1=rn[:, :], op=ALU.mult)
        conv_silu_fm(P, cx, wh, uT, 256, xpad, acc, cw[:, h * 3 + 2, :], None, vb[:, :], mm_banks)
        fm_to_tm(P, cx, [kn[h][:, :], vb[:, :]], kv_tm[h], tp_banks)
        P.free(xpad, acc, sq, rn, vb)
    whgr.free()
    wtm = P.tile([128, 16, 520], BF16, "wtm2")
    dma_w(P, wtm, io["w_in"][:, 3584:4104], 16, 520)
    sg_tm = P.tile([128, NTS, 512], BF16, "sg2")
    proj_tm(P, wtm, uT, 0, 512, sg_tm, mm_banks, func=AF.Silu)
    bd = P.tile([128, NTS, 8], F32, "bd")
    proj_tm(P, wtm, uT, 512, 8, bd, mm_banks)
    P.free(wtm, uT)
    gp = P.tile([128, 12], F32, "gp")
    P.dma("sp", out=gp[:, 0:8], in_=io["gdn_p"][:, :].to_broadcast([128, 8]))
    P.act.activation(out=gp[:, 0:4], in_=gp[:, 0:4], func=AF.Exp)
    P.dve.tensor_scalar(out=gp[:, 0:4], in0=gp[:, 0:4], scalar1=-1.0, scalar2=None, op0=ALU.mult)
    beta = P.tile([128, NTS, 4], F32, "beta")
    nbeta = P.tile([128, NTS, 4], F32, "nbeta")
    gg = P.tile([128, NTS, 4], F32, "gg")
    tmp = P.tile([128, NTS, 4], F32, "tmpg")
    P.act.activation(out=beta[:, :, :], in_=bd[:, :, 0:4], func=AF.Sigmoid)
    P.dve.tensor_scalar(out=nbeta[:, :, :], in0=beta[:, :, :], scalar1=-1.0, scalar2=None, op0=ALU.mult)
    P.dve.tensor_tensor(out=gg[:, :, :], in0=bd[:, :, 4:8], in1=gp[:, 4:8].unsqueeze(1).to_broadcast([128, NTS, 4]), op=ALU.add)
    P.dve.tensor_scalar(out=tmp[:, :, :], in0=gg[:, :, :], scalar1=-1.0, scalar2=None, op0=ALU.mult)
    P.dve.tensor_tensor(out=tmp[:, :, :], in0=tmp[:, :, :], in1=gg[:, :, :], op=ALU.max)
    P.act.activation(out=tmp[:, :, :], in_=tmp[:, :, :], func=AF.Exp, scale=-1.0)
    P.act.activation(out=tmp[:, :, :], in_=tmp[:, :, :], func=AF.Ln, bias=1.0, scale=1.0)
    P.dve.tensor_scalar(out=gg[:, :, :], in0=gg[:, :, :], scalar1=0.0, scalar2=None, op0=ALU.max)
    P.dve.tensor_tensor(out=gg[:, :, :], in0=gg[:, :, :], in1=tmp[:, :, :], op=ALU.add)
    P.dve.tensor_tensor(out=gg[:, :, :], in0=gg[:, :, :], in1=gp[:, 0:4].unsqueeze(1).to_broadcast([128, NTS, 4]), op=ALU.mult)
    P.free(tmp, bd)
    wn = bcast_load(P, io["gdn_norm"], 512, "wn2")
    S = P.tile([128, 4, 128], F32, "Sg")
    P.dve.memset(S[:, :, :], 0.0)
    Sbr = P.ring([128, 4, 128], BF16, 3, "Sgb")
    Sb_cur = Sbr.next()
    P.dve.memset(Sb_cur[:, :, :], 0.0)
    R3 = lambda nm, dt=F32, n=2: P.ring([128, 4, 128], dt, n, nm)
    gSLr, gUr, Dr, DTr, Zr, Yr, Pr, qkr, bvr, kdr, Rr, vnr, otr, TTr = (R3("gSL"), R3("gU"), R3("D"), R3("DT"), R3("Z", F32, 3), R3("Y", F32, 3), R3("P"),
                                                                   R3("qk", BF16), R3("bv"), R3("kdc", BF16), R3("R", BF16), R3("vn", BF16), R3("ot"), R3("TT", BF16))
    rfr = R3("rf")
    identf = cx.identf
    bE, bET, bKK, bKQ, bSM = P.bank(4), P.bank(5), P.bank(6), P.bank(7), P.bank(2)
    for t in range(NTS):
        ts = slice(t * 128, (t + 1) * 128)
        g_t = gg[:, t, :]
        P.pe.matmul(out=bSM[:, 0:4], lhsT=U2, rhs=g_t, start=True, stop=True)
        P.pe.matmul(out=bSM[:, 4:8], lhsT=SL2, rhs=g_t, start=True, stop=True)
        P.pe.matmul(out=bSM[:, 8:12], lhsT=HA, rhs=g_t, start=True, stop=True)
        P.pe.matmul(out=bSM[:, 12:16], lhsT=HB, rhs=g_t, start=True, stop=True)
        sm = cx.small32.next()
        P.act.activation(out=sm[:, 0:16], in_=bSM[:, 0:16], func=AF.Exp)
        P.dve.tensor_tensor(out=sm[:, 16:20], in0=sm[:, 0:4], in1=nbeta[:, t, :], op=ALU.mult)
        gSL, gU = gSLr.next(), gUr.next()
        P.dve.tensor_tensor(out=gSL[:, :, :], in0=bcm(SL2, 4), in1=bc3(g_t, 4, 128), op=ALU.mult)
        P.dve.tensor_tensor(out=gU[:, :, :], in0=bcm(U2, 4), in1=bc3(g_t, 4, 128), op=ALU.mult)
        P.pe.matmul(out=bE[:, :512], lhsT=U2, rhs=gSL[:, :, :], start=True, stop=True)
        P.pe.matmul(out=bET[:, :512], lhsT=SL2, rhs=gU[:, :, :], start=True, stop=True)
        for h in range(4):
            P.pe.matmul(out=bKK[:, h * 128:(h + 1) * 128], lhsT=kn[h][:, ts], rhs=kn[h][:, ts], start=True, stop=True)
            P.pe.matmul(out=bKQ[:, h * 128:(h + 1) * 128], lhsT=kn[h][:, ts], rhs=qn[h][:, ts], start=True, stop=True)
        Dm, DT = Dr.next(), DTr.next()
        P.act.activation(out=Dm[:, :, :], in_=bE[:, :512].rearrange("p (h j) -> p h j", h=4), func=AF.Exp)
        P.act.activation(out=DT[:, :, :], in_=bET[:, :512].rearrange("p (h j) -> p h j", h=4), func=AF.Exp)
        Z = Zr.next()
        P.dve.tensor_tensor(out=Z[:, :, :], in0=bKK[:, :512].rearrange("p (h j) -> p h j", h=4), in1=Dm[:, :, :], op=ALU.mult)
        P.dve.tensor_tensor(out=Z[:, :, :], in0=Z[:, :, :], in1=bcm(SL2, 4), op=ALU.mult)
        P.dve.tensor_tensor(out=Z[:, :, :], in0=Z[:, :, :], in1=bc3(nbeta[:, t, :], 4, 128), op=ALU.mult)
        qk = qkr.next()
        P.dve.tensor_tensor(out=DT[:, :, :], in0=bKQ[:, :512].rearrange("p (h j) -> p h j", h=4), in1=DT[:, :, :], op=ALU.mult)
        P.dve.tensor_tensor(out=qk[:, :, :], in0=DT[:, :, :], in1=bcm(U2, 4), op=ALU.mult)
        pT = mm_banks.next()
        for h in range(4):
            P.pe.transpose(out=pT[:, h * 128:(h + 1) * 128], in_=Z[:, h, :], identity=identf)
        Y = Yr.next()
        P.act.activation(out=Y[:, :, :], in_=pT[:, :512].rearrange("p (h j) -> p h j", h=4), func=AF.Copy)
        Pm = Pr.next()
        P.dve.tensor_tensor(out=Pm[:, :, :], in0=Y[:, :, :], in1=bcm(identf, 4), op=ALU.add)
        for m in range(1, 6):
            Zn = Zr.next()
            pz = bE
            for h in range(4):
                P.pe.matmul(out=pz[:, h * 128:(h + 1) * 128], lhsT=Y[:, h, :], rhs=Z[:, h, :], start=True, stop=True)
            P.act.activation(out=Zn[:, :, :], in_=pz[:, :512].rearrange("p (h j) -> p h j", h=4), func=AF.Copy)
            if m < 5:
                Yn = Yr.next()
                py = bET
                for h in range(4):
                    P.pe.matmul(out=py[:, h * 128:(h + 1) * 128], lhsT=Z[:, h, :], rhs=Y[:, h, :], start=True, stop=True)
                P.dve.tensor_copy(out=Yn[:, :, :], in_=py[:, :512].rearrange("p (h j) -> p h j", h=4))
            pp = bKK
            for h in range(4):
                P.pe.matmul(out=pp[:, h * 128:(h + 1) * 128], lhsT=Zn[:, h, :], rhs=Pm[:, h, :], start=True, stop=True)
            Pn = Pr.next()
            P.dve.tensor_tensor(out=Pn[:, :, :], in0=pp[:, :512].rearrange("p (h j) -> p h j", h=4), in1=Pm[:, :, :], op=ALU.add)
            Z, Pm = Zn, Pn
            if m < 5:
                Y = Yn
        TT = TTr.next()
        P.act.activation(out=TT[:, :, :], in_=Pm[:, :, :], func=AF.Copy)
        bv, kdc = bvr.next(), kdr.next()
        for h in range(4):
            P.dve.tensor_scalar(out=bv[:, h, :], in0=kv_tm[h][:, t, 128:256], scalar1=beta[:, t, h:h + 1], scalar2=None, op0=ALU.mult)
            P.dve.tensor_scalar(out=kdc[:, h, :], in0=kv_tm[h][:, t, 0:128], scalar1=sm[:, 4 + h:5 + h], scalar2=None, op0=ALU.mult)
        ot = otr.next()
        for c in range(2):
            r = slice(c * 64, (c + 1) * 64)
            tr = slice(t * 128 + c * 64, t * 128 + (c + 1) * 64)
            pks, pqs = mm_banks.next(), mm_banks.next()
            for h in range(4):
                P.pe.matmul(out=pks[r, h * 128:(h + 1) * 128], lhsT=kn[h][:, tr], rhs=Sb_cur[:, h, :], start=True, stop=True)
                P.pe.matmul(out=pqs[r, h * 128:(h + 1) * 128], lhsT=qn[h][:, tr], rhs=Sb_cur[:, h, :], start=True, stop=True)
            Rt = Rr.next()
            rf = rfr.next()
            P.dve.tensor_tensor(out=rf[r, :, :], in0=pks[r, :512].rearrange("p (h j) -> p h j", h=4), in1=bc3(sm[r, 16:20], 4, 128), op=ALU.mult)
            P.dve.tensor_tensor(out=Rt[r, :, :], in0=rf[r, :, :], in1=bv[r, :, :], op=ALU.add)
            pvn = bKQ
            for h in range(4):
                P.pe.matmul(out=pvn[r, h * 128:(h + 1) * 128], lhsT=TT[r, h, c * 64:(c + 1) * 64], rhs=Rt[r, h, :], start=True, stop=True)
            vn = vnr.next()
            P.act.activation(out=vn[r, :, :], in_=pvn[r, :512].rearrange("p (h j) -> p h j", h=4), func=AF.Copy)
            poi = bE
            for h in range(4):
                P.pe.matmul(out=poi[r, h * 128:(h + 1) * 128], lhsT=qk[r, h, c * 64:(c + 1) * 64], rhs=vn[r, h, :], start=True, stop=True)
            P.dve.tensor_tensor(out=rf[r, :, :], in0=pqs[r, :512].rearrange("p (h j) -> p h j", h=4), in1=bc3(sm[r, 0:4], 4, 128), op=ALU.mult)
            P.dve.tensor_tensor(out=ot[r, :, :], in0=poi[r, :512].rearrange("p (h j) -> p h j", h=4), in1=rf[r, :, :], op=ALU.add)
            pst = bET
            for h in range(4):
                P.pe.matmul(out=pst[:, h * 128:(h + 1) * 128], lhsT=kdc[r, h, :], rhs=vn[r, h, :], start=True, stop=True)
            cd = sm[:, 8 + 4 * c:12 + 4 * c]
            P.dve.tensor_tensor(out=S[:, :, :], in0=S[:, :, :], in1=bc3(cd, 4, 128), op=ALU.mult)
            P.dve.tensor_tensor(out=S[:, :, :], in0=pst[:, :512].rearrange("p (h j) -> p h j", h=4), in1=S[:, :, :], op=ALU.add)
            Sb_cur = Sbr.next()
            P.act.activation(out=Sb_cur[:, :, :], in_=S[:, :, :], func=AF.Copy)
        outs.append(head_norm_out(P, cx, ot[:, :, :], sg_tm[:, t, :], wn[:, :], io["omix"][t * 128:(t + 1) * 128, 512:1024], 4, rings))
    P.release(mark)
    outs += partial_proj(P, cx, io["omix"], 1024, io["w_out"], io["part"])
    return outs


def make_consts():
    c = np.zeros((128, NCONST), np.float32)
    c[:, 0:128] = np.eye(128)
    t = np.arange(128)[:, None] % 64
    i = np.arange(64)[None, :]
    c[:, 128:192] = (t <= i)
    c[:, 192:256] = (t > i)
    c[:, 256:320] = (t < i)
    c[:, 320:448] = 1.0
    return c


def make_c2():
    c = np.zeros((128, NC2), np.float32)
    t = np.arange(128)[:, None]; i = np.arange(128)[None, :]
    same = (t // 64) == (i // 64)
    c[:, 0:128] = (t <= i) & same
    c[:, 128:256] = (t > i) & same
    c[:, 256:384] = (t < 64)
    c[:, 384:512] = (t >= 64)
    return c


def ssd_host(d, half):
    w = d["ssd_w_in"][0]
    cols = []
    for g in range(4):
        gg = half * 4 + g
        cols += list(range(4096 + gg * 512, 4096 + (gg + 1) * 512))
        cols += list(range(gg * 512, (gg + 1) * 512))
        cols += list(range(8192 + gg * 128, 8192 + (gg + 1) * 128))
        cols += list(range(9216 + gg * 128, 9216 + (gg + 1) * 128))
        cols += list(range(10240 + gg * 8, 10240 + (gg + 1) * 8))
    w_in = np.ascontiguousarray(w[:, cols])
    cw = d["ssd_conv_w"][0]; cb = d["ssd_conv_b"][0]
    xs = slice(half * 2048, (half + 1) * 2048)
    cwx = cw[:, xs].T.reshape(16, 128, 4).transpose(1, 0, 2)
    cbx = cb[xs].reshape(16, 128).T
    bcols = np.concatenate([np.arange(4096 + (half * 4 + g) * 128, 4096 + (half * 4 + g + 1) * 128) for g in range(4)] +
                           [np.arange(5120 + (half * 4 + g) * 128, 5120 + (half * 4 + g + 1) * 128) for g in range(4)])
    cwbc = cw[:, bcols].T.reshape(8, 128, 4).transpose(1, 0, 2)
    cbbc = cb[bcols].reshape(8, 128).T
    hs = slice(half * 32, (half + 1) * 32)
    m = dict(w_in=w_in, cwx=cwx, cbx=cbx, cwbc=cwbc, cbbc=cbbc, dtb=d["ssd_dt_bias"][0][hs][None], alog=d["ssd_a_log"][0][hs][None],
             dfull=np.repeat(d["ssd_d"][0][hs], 64)[None], vec_mix=d["norm_mix"][1][None], c2=make_c2(), consts=make_consts())
    return {k: np.ascontiguousarray(v, dtype=np.float32) for k, v in m.items()}


def hy_host(d, half):
    w = d["hy_w_in"][0]
    hs = [half * 4 + h for h in range(4)]
    cols = []
    for h in hs:
        cols += list(range(h * 128, (h + 1) * 128)) + list(range(1024 + h * 128, 1024 + (h + 1) * 128))
    cols += list(range(2048 + hs[0] * 128, 2048 + (hs[-1] + 1) * 128))
    cols += list(range(3072 + hs[0] * 128, 3072 + (hs[-1] + 1) * 128))
    for h in hs:
        cols += list(range(4096 + h * 128, 4096 + (h + 1) * 128)) + list(range(5120 + h * 128, 5120 + (h + 1) * 128)) + \
                list(range(6144 + h * 128, 6144 + (h + 1) * 128))
    cols += list(range(7168 + hs[0] * 128, 7168 + (hs[-1] + 1) * 128))
    cols += [8192 + h for h in hs] + [8200 + h for h in hs]
    assert len(cols) == 4104
    w_in = w[:, cols]
    lg = d["hgrn_lb_logits"]
    lbl = np.concatenate([lg[0, hs[0] * 128:(hs[-1] + 1) * 128].reshape(4, 128).T, lg[1, hs[0] * 128:(hs[-1] + 1) * 128].reshape(4, 128).T], axis=1)
    cw = d["gdn_conv_w"][0]
    ccols = []
    for h in hs:
        ccols += list(range(h * 128, (h + 1) * 128)) + list(range(1024 + h * 128, 1024 + (h + 1) * 128)) + list(range(2048 + h * 128, 2048 + (h + 1) * 128))
    gcw = cw[:, ccols].T.reshape(12, 128, 4).transpose(1, 0, 2)
    gdn_p = np.concatenate([d["gdn_a_log"][0][hs], d["gdn_dt_bias"][0][hs]])[None]
    sl = slice(hs[0] * 128, (hs[-1] + 1) * 128)
    m = dict(w_in=w_in, lbl=lbl, gcw=gcw, gdn_p=gdn_p, hgrn_norm=d["hgrn_norm"][0][sl][None], gdn_norm=d["gdn_norm"][0][sl][None],
             vec_mix=d["norm_mix"][0][None], c2=make_c2(), consts=make_consts())
    return {k: np.ascontiguousarray(v, dtype=np.float32) for k, v in m.items()}


GROUPS = [[0, 1], [2, 3], [4, 5], [6, 7]]
NCU = 8

EXT_INPUTS = [
    ("consts", [128, NCONST]), ("c2", [128, NC2]), ("xfull", [SEQ, D]), ("x_own", [NT, D]), ("mem", [MEM, D]),
    ("w_in_hy", [D, 4104]), ("vec_mix0", [1, D]), ("lbl", [128, 8]), ("gcw", [128, 12, 4]), ("gdn_p", [1, 8]),
    ("hgrn_norm", [1, 512]), ("gdn_norm", [1, 512]), ("w_out0", [1024, D]),
    ("w_in_ssd", [D, 4 * 1288]), ("cwx", [128, 16, 4]), ("cbx", [128, 16]), ("cwbc", [128, 8, 4]), ("cbbc", [128, 8]),
    ("dtb", [1, 32]), ("alog", [1, 32]), ("dfull", [1, 2048]), ("w_out1", [2048, D]), ("vec_ssdnorm", [1, 2048]), ("vec_mix1", [1, D]),
    ("vec_final", [1, D]),
]
for _l in range(2):
    EXT_INPUTS += [(f"wq{_l}", [D, 512]), (f"wk{_l}", [D, 512]), (f"wv{_l}", [D, 512]), (f"wo{_l}", [512, D]),
                   (f"w_gate{_l}", [D, DFF]), (f"w_up{_l}", [D, DFF]), (f"w_down{_l}", [DFF, D]),
                   (f"vec_xattn{_l}", [1, D]), (f"vec_mem{_l}", [1, D]), (f"vec_ffn{_l}", [1, D])]


def build_all(upto=9):
    nc = bass.Bass("TRN2", target_bir_lowering=False)
    P = Prog(nc, same_engine_sync=SES)
    t = {}
    for nm, shp in EXT_INPUTS:
        t[nm] = P.dram(nm, shp, F32, "ExternalInput")
    t["out"] = P.dram("out", [NT, D], F32, "ExternalOutput")
    for nm, shp, dt in [("omix", [SEQ, 1024], F32), ("part0", [SEQ, D], F32), ("rs0", [NT, D], F32), ("h1_own", [NT, D], F32),
                        ("yg", [SEQ, 2048], F32),
                        ("part1", [SEQ, 2064], F32), ("rs1", [NT, 2064], F32)]:
        t[nm] = P.dram(nm, shp, dt, "Internal")
    for k_ in range(4):
        t[f"u_half{k_}"] = P.dram(f"u_half{k_}", [256, D], BF16, "Internal")
        t[f"u_full{k_}"] = P.dram(f"u_full{k_}", [512, D], BF16, "Internal")
    u_half = [t[f"u_half{k_}"] for k_ in range(4)]
    u_full = [t[f"u_full{k_}"] for k_ in range(4)]
    cx = setup_ctx(P, t["consts"])
    hy_io = dict(c2=t["c2"], xfull=t["xfull"], w_in=t["w_in_hy"], vec_mix=t["vec_mix0"], lbl=t["lbl"], gcw=t["gcw"], gdn_p=t["gdn_p"],
                 hgrn_norm=t["hgrn_norm"], gdn_norm=t["gdn_norm"], omix=t["omix"], w_out=t["w_out0"], part=t["part0"])
    if upto != 4:
        stage_hy(P, cx, hy_io)
    if upto == 5:
        stage_hy(P, cx, hy_io)
    P.collective("ReduceScatter", ALU.add, GROUPS[:NCU // 2], t["part0"].ap, t["rs0"].ap)
    if upto in (1, 5):
        o = P.dma("sp", out=t["out"][:, :], in_=t["rs0"][:, :])
        P.emit(final_wait_ops=[o])
        return nc

    def b_io(l):
        d_ = dict(mem=t["mem"], wq=t[f"wq{l}"], wk=t[f"wk{l}"], wv=t[f"wv{l}"], wo=t[f"wo{l}"], w_gate=t[f"w_gate{l}"], w_up=t[f"w_up{l}"],
                  w_down=t[f"w_down{l}"], vec_xattn=t[f"vec_xattn{l}"], vec_mem=t[f"vec_mem{l}"], vec_ffn=t[f"vec_ffn{l}"],
                  vec_final=t["vec_final"], vec_mix_next=t["vec_mix1"])
        return d_
    io0 = b_io(0)
    io0.update(hres=t["x_own"], rs=t["rs0"], hout=t["h1_own"], u_half=u_half)
    stage_b(P, cx, io0, 0, False)
    if upto in (3, 4):
        o = P.dma("sp", out=t["out"][:, :], in_=t["h1_own"][:, :])
        P.emit(final_wait_ops=[o])
        return nc
    for k_ in range(4):
        P.collective("AllGather", ALU.bypass, GROUPS[:NCU // 2], u_half[k_].ap, u_full[k_].ap)
    if upto == 2:
        o = P.dma("sp", out=t["out"][:, :], in_=t["h1_own"][:, :])
        P.emit(final_wait_ops=[o])
        return nc
    ssd_io = dict(c2=t["c2"], u_full=u_full, w_in=t["w_in_ssd"], cwx=t["cwx"], cbx=t["cbx"], cwbc=t["cwbc"], cbbc=t["cbbc"], dtb=t["dtb"],
                  alog=t["alog"], dfull=t["dfull"], yg=t["yg"], w_out=t["w_out1"], vec_ssdnorm=t["vec_ssdnorm"], part=t["part1"])
    stage_ssd(P, cx, ssd_io)
    P.collective("ReduceScatter", ALU.add, GROUPS[:NCU // 2], t["part1"].ap, t["rs1"].ap)
    io1 = b_io(1)
    io1.update(hres=t["h1_own"], rs=t["rs1"], hout=t["out"])
    outs = stage_b(P, cx, io1, 1, True)
    P.emit(final_wait_ops=outs)
    print("ops", len(P.ops), "waits", P.n_waits, "sb_peak", P.sb_peak)
    return nc


_CACHE = {}
UPTO = 9
TRACE = False
SES = True


def _f32(a):
    return np.ascontiguousarray(a, dtype=np.float32)


def kernel(**inputs):
    d = {k: np.asarray(v) for k, v in inputs.items()}
    x = d["x"]
    B = x.shape[0]
    cores = list(range(NCU))
    if "nc" not in _CACHE:
        _CACHE["nc"] = build_all(UPTO)
    nc = _CACHE["nc"]
    shared = dict(consts=make_consts(), c2=make_c2(), vec_mix0=_f32(d["norm_mix"][0][None]), vec_mix1=_f32(d["norm_mix"][1][None]),
                  vec_final=_f32(d["norm_final"][None]))
    for l in range(2):
        shared.update({f"wq{l}": _f32(d["xa_wq"][l]), f"wk{l}": _f32(d["xa_wk"][l]), f"wv{l}": _f32(d["xa_wv"][l]), f"wo{l}": _f32(d["xa_wo"][l]),
                       f"w_gate{l}": _f32(d["ffn_w_gate"][l]), f"w_up{l}": _f32(d["ffn_w_up"][l]), f"w_down{l}": _f32(d["ffn_w_down"][l]),
                       f"vec_xattn{l}": _f32(d["norm_xattn"][l][None]), f"vec_mem{l}": _f32(d["norm_mem"][l][None]),
                       f"vec_ffn{l}": _f32(d["norm_ffn"][l][None])})
    per_half = []
    for half in range(2):
        hh = hy_host(d, half)
        sh = ssd_host(d, half)
        m = dict(w_in_hy=hh["w_in"], lbl=hh["lbl"], gcw=hh["gcw"], gdn_p=hh["gdn_p"], hgrn_norm=hh["hgrn_norm"], gdn_norm=hh["gdn_norm"],
                 w_out0=_f32(np.concatenate([d["hy_w_out"][0][half * 512:(half + 1) * 512], d["hy_w_out"][0][1024 + half * 512:1024 + (half + 1) * 512]], axis=0)),
                 w_in_ssd=sh["w_in"], cwx=sh["cwx"], cbx=sh["cbx"], cwbc=sh["cwbc"], cbbc=sh["cbbc"], dtb=sh["dtb"], alog=sh["alog"], dfull=sh["dfull"],
                 w_out1=_f32(d["ssd_w_out"][0][half * 2048:(half + 1) * 2048]), vec_ssdnorm=_f32(d["ssd_norm"][0][half * 2048:(half + 1) * 2048][None]))
        per_half.append(m)
    maps = []
    for c in cores:
        b, half = c // 2, c % 2
        m = dict(shared)
        m.update(per_half[half])
        m["xfull"] = _f32(x[b])
        m["x_own"] = _f32(x[b, half * NT:(half + 1) * NT])
        m["mem"] = _f32(d["mem"][b])
        maps.append(m)
    res = run_bass_kernel_spmd(nc, maps, core_ids=cores, **({'trace': True} if TRACE else {}))
    if TRACE:
        print('EXEC_NS', getattr(res, 'exec_time_ns', None))
    out = np.empty((B, SEQ, D), np.float32)
    for c in cores:
        b, half = c // 2, c % 2
        out[b, half * NT:(half + 1) * NT] = res.results[c]["out"]
    return out
```

```python
import numpy as np
import concourse.bass as bass
import concourse.mybir as mybir
from concourse.bass_utils import run_bass_kernel_spmd

F32 = mybir.dt.float32
BF16 = mybir.dt.bfloat16
AF = mybir.ActivationFunctionType
ALU = mybir.AluOpType
AX = mybir.AxisListType

_DT_SIZE = {F32: 4, BF16: 2}


class T:
    def __init__(self, base, space, lo, hi, name):
        self.h = base
        self.space = space
        self.lo = lo
        self.hi = hi
        self.name = name

    def __getitem__(self, idx):
        v = V(self.h[idx], self)
        if self.space == "dram":
            i0 = idx[0] if isinstance(idx, tuple) else idx
            if isinstance(i0, slice) and (i0.start is None or isinstance(i0.start, int)) and (i0.stop is None or isinstance(i0.stop, int)) \
                    and i0.step in (None, 1):
                r0 = 0 if i0.start is None else i0.start
                r1 = (self.hi - self.lo) if i0.stop is None else i0.stop
                v.lo, v.hi = self.lo + r0, self.lo + r1
        return v

    @property
    def ap(self):
        return V(self.h, self)


class V:
    def __init__(self, ap, t, lo=None, hi=None):
        self.ap = ap
        self.t = t
        self.lo = t.lo if lo is None else lo
        self.hi = t.hi if hi is None else hi

    def __getitem__(self, idx):
        return V(self.ap[idx], self.t, self.lo, self.hi)

    def __getattr__(self, name):
        if name in ("ap", "t", "lo", "hi"):
            raise AttributeError(name)
        a = getattr(self.ap, name)
        if callable(a):
            def f(*args, **kw):
                r = a(*args, **kw)
                if isinstance(r, bass.AP):
                    return V(r, self.t, self.lo, self.hi)
                return r
            return f
        return a


class CCAP:
    def __init__(self, v):
        self.v = v


class Ring:
    def __init__(self, tiles, prog):
        self.tiles = tiles
        self.i = 0
        self.p = prog

    def next(self):
        t = self.tiles[self.i % len(self.tiles)]
        self.i += 1
        return t

    def free(self):
        self.p.free(*self.tiles)


class Op:
    __slots__ = ("eng", "name", "args", "kw", "deps", "is_dma", "is_cc", "sem", "val", "has_dep", "idx", "pre")

    def __init__(self, eng, name, args, kw):
        self.eng = eng
        self.name = name
        self.args = args
        self.kw = kw
        self.deps = []
        self.is_dma = name == "dma_start"
        self.is_cc = name == "collective_compute"
        self.sem = None
        self.val = None
        self.has_dep = False
        self.pre = []


class EngProxy:
    def __init__(self, prog, eng):
        self.p = prog
        self.e = eng

    def __getattr__(self, name):
        def f(*args, **kw):
            return self.p.record(self.e, name, args, kw)
        return f


WRITE_KEYS = ("out", "accum_out")
CC_BLOCK = True


class Prog:
    ENGS = ("pe", "act", "dve", "pool", "sp")

    def __init__(self, nc, same_engine_sync=True, n_dma_sems=(12, 6, 6)):
        self.nc = nc
        self.ops = []
        self.same_engine_sync = same_engine_sync
        self.recs = {"sb": [], "ps": [], "dram": []}
        self.pe = EngProxy(self, "pe")
        self.act = EngProxy(self, "act")
        self.dve = EngProxy(self, "dve")
        self.pool = EngProxy(self, "pool")
        self.sp = EngProxy(self, "sp")
        self.sb_lo = (nc.sbuf_base + 63) // 64 * 64
        self.sb_hi = nc.sbuf_top // 64 * 64
        self.sb_free = [(self.sb_lo, self.sb_hi)]
        self.sb_peak = 0
        self.n_tiles = 0
        self.psum = []
        self.dram_next = 0
        self.live = []
        self.last_cc = None
        self.n_dma_sems = dict(zip(("sp", "act", "pool"), n_dma_sems))

    def tile(self, shape, dtype=F32, name=None):
        nbytes = int(np.prod(shape[1:])) * _DT_SIZE[dtype]
        nbytes_al = (nbytes + 31) // 32 * 32
        for i, (lo, hi) in enumerate(self.sb_free):
            if hi - lo >= nbytes_al:
                if hi - lo == nbytes_al:
                    self.sb_free.pop(i)
                else:
                    self.sb_free[i] = (lo + nbytes_al, hi)
                self.n_tiles += 1
                nm = f"{name or 't'}_{self.n_tiles}"
                h = self.nc.alloc_sbuf_tensor_at(nm, list(shape), dtype, offset=lo)
                self.sb_peak = max(self.sb_peak, lo + nbytes_al)
                t = T(h[:], "sb", lo, lo + nbytes_al, nm)
                self.live.append(t)
                return t
        raise RuntimeError(f"SBUF arena full allocating {shape} {name}; free={self.sb_free}")

    def mark(self):
        return list(self.live)

    def release(self, mark):
        keep = set(id(t) for t in mark)
        self.free(*[t for t in self.live if id(t) not in keep])

    def free(self, *tiles):
        for t in tiles:
            if not any(t is x for x in self.live):
                continue
            self.live = [x for x in self.live if x is not t]
            self.sb_free.append((t.lo, t.hi))
        self.sb_free.sort()
        merged = []
        for lo, hi in self.sb_free:
            if merged and merged[-1][1] == lo:
                merged[-1] = (merged[-1][0], hi)
            else:
                merged.append((lo, hi))
        self.sb_free = merged

    def psum_init(self):
        self.ps_h = self.nc.alloc_psum_tensor("psall", [128, 4096], F32)

    def bank(self, b0, nb=1):
        return T(self.ps_h[:, b0 * 512:(b0 + nb) * 512], "ps", b0 * 2048, (b0 + nb) * 2048, f"ps{b0}_{nb}")

    def psum_slice(self, b, col0, ncols):
        c0 = b * 512 + col0
        return T(self.ps_h[:, c0:c0 + ncols], "ps", c0 * 4, (c0 + ncols) * 4, f"pss{b}_{col0}")

    def bank_bf(self, b0, nb=1):
        return T(self.ps_h[:, b0 * 512:(b0 + nb) * 512].bitcast(BF16), "ps", b0 * 2048, (b0 + nb) * 2048, f"psb{b0}_{nb}")

    def ring(self, shape, dtype, n, name=None):
        return Ring([self.tile(shape, dtype, name) for _ in range(n)], self)

    def dram(self, name, shape, dtype, kind):
        if kind == "Internal":
            h = self.nc.dram_tensor(name, list(shape), dtype)
        else:
            h = self.nc.dram_tensor(name, list(shape), dtype, kind=kind)
        base = self.dram_next
        self.dram_next += int(shape[0]) + 8
        return T(h.ap(), "dram", base, base + int(shape[0]), name)

    def _access(self, op, t, lo, hi, is_write):
        recs = self.recs[t.space]
        keep = []
        for r in recs:
            rlo, rhi, rop, rw = r
            if rlo < hi and lo < rhi:
                if (is_write or rw) and rop is not op:
                    op.deps.append(rop)
                if is_write and rlo >= lo and rhi <= hi:
                    continue
                if (not is_write) and (not rw) and rop.eng == op.eng and not rop.is_dma and not op.is_dma and not rop.is_cc \
                        and rlo >= lo and rhi <= hi:
                    continue
            keep.append(r)
        keep.append((lo, hi, op, is_write))
        self.recs[t.space] = keep

    def record(self, eng, name, args, kw, extra_reads=(), extra_writes=()):
        op = Op(eng, name, args, kw)
        reads, writes = [], []
        for k, v in kw.items():
            if isinstance(v, V):
                (writes if k in WRITE_KEYS else reads).append(v)
        for i, v in enumerate(args):
            if isinstance(v, V):
                (writes if i == 0 else reads).append(v)
        reads += list(extra_reads)
        writes += list(extra_writes)
        for v in reads:
            t = v.t if isinstance(v, V) else v
            self._access(op, t, v.lo, v.hi, False)
        for v in writes:
            t = v.t if isinstance(v, V) else v
            self._access(op, t, v.lo, v.hi, True)
        if op.is_dma and self.last_cc is not None:
            op.deps.append(self.last_cc)
        if op.is_cc:
            self.last_cc = op
        op.idx = len(self.ops)
        self.ops.append(op)
        return op

    def collective(self, kind, op, groups, src, dst):
        return self.record("pool", "collective_compute", (kind, op), dict(replica_groups=groups, ins=[CCAP(src)], outs=[CCAP(dst)]),
                           extra_reads=[src], extra_writes=[dst])

    def dma(self, q, out, in_, **kw):
        return self.record(q, "dma_start", (), dict(out=out, in_=in_, **kw))

    def emit(self, final_wait_ops=()):
        nc = self.nc
        ops = self.ops
        for op in ops:
            seen = set()
            d2 = []
            for d in op.deps:
                if id(d) in seen:
                    continue
                seen.add(id(d))
                if d.eng == op.eng and not d.is_dma and not d.is_cc:
                    if d.eng == "pe" or not self.same_engine_sync:
                        continue
                d2.append(d)
            op.deps = d2
            for d in d2:
                d.has_dep = True
        for op in final_wait_ops:
            op.has_dep = True
        eng_sem = {}
        eng_cnt = {}
        SEM_MAX = 30000

        def new_sem(nm):
            return nc.alloc_semaphore(nm)

        for e in ("pe", "act", "dve", "pool"):
            eng_sem[e] = new_sem(f"c_{e}_0")
            eng_cnt[e] = 0
        n_epoch = 0
        dma_sems = {q: [[new_sem(f"d_{q}_{i}"), 0, None] for i in range(n)] for q, n in self.n_dma_sems.items()}
        dma_rr = {q: 0 for q in dma_sems}
        for op in ops:
            if op.is_cc:
                op.sem, op.val = new_sem(f"cc_{op.idx}"), 1
                for q_, pool_ in dma_sems.items():
                    for sl_ in pool_:
                        if sl_[2] is not None:
                            op.pre.append((sl_[0], sl_[1]))
            elif op.is_dma:
                pool = dma_sems[op.eng]
                slot = pool[dma_rr[op.eng] % len(pool)]
                dma_rr[op.eng] += 1
                if slot[2] is not None:
                    op.pre.append((slot[0], slot[1]))
                slot[1] += 16
                slot[2] = op
                op.sem, op.val = slot[0], slot[1]
            elif op.has_dep:
                e = op.eng
                if eng_cnt[e] >= SEM_MAX:
                    n_epoch += 1
                    eng_sem[e] = new_sem(f"c_{e}_{n_epoch}")
                    eng_cnt[e] = 0
                eng_cnt[e] += 1
                op.sem, op.val = eng_sem[e], eng_cnt[e]
        self.eng_cnt_final = dict(eng_cnt)
        print("sem counts", eng_cnt, "epochs", n_epoch, "dma", {q: [x[1] for x in v] for q, v in dma_sems.items()})
        per_eng = {e: [] for e in self.ENGS}
        for op in ops:
            per_eng[op.eng].append(op)
        handles = {"pe": "tensor", "act": "scalar", "dve": "vector", "pool": "gpsimd", "sp": "sync"}
        print("per-engine instr", {e: len(v) for e, v in per_eng.items()})
        self.n_waits = 0

        def unwrap(x):
            if isinstance(x, V):
                return x.ap
            if isinstance(x, CCAP):
                return x.v.ap.opt()
            if isinstance(x, list):
                return [unwrap(y) for y in x]
            return x

        def emit_engine(e, eng):
            waited = {}
            for op in per_eng[e]:
                need = {}
                for (s, v) in op.pre:
                    need[id(s)] = (s, max(v, need.get(id(s), (s, 0))[1]))
                for d in op.deps:
                    k = id(d.sem)
                    if k not in need or need[k][1] < d.val:
                        need[k] = (d.sem, d.val)
                for k, (s, v) in need.items():
                    if waited.get(k, 0) >= v:
                        continue
                    eng.wait_ge(s, v)
                    self.n_waits += 1
                    waited[k] = v
                args = [unwrap(a) for a in op.args]
                kw = {k: unwrap(v) for k, v in op.kw.items()}
                ins = getattr(eng, op.name)(*args, **kw)
                if op.sem is not None:
                    ins.then_inc(op.sem, 16 if op.is_dma else 1)
                if op.is_cc and CC_BLOCK:
                    eng.wait_ge(op.sem, op.val)
            if e in final_eng:
                for op in final_wait_ops:
                    eng.wait_ge(op.sem, op.val)

        final_eng = {"sp"}
        with nc.Block() as block:
            for e in self.ENGS:
                fn = getattr(block, handles[e])
                fn(lambda eng, e=e: emit_engine(e, eng))
        return nc


D = 2048
NT = 1024
NTT = NT // 128
EPS = 1e-6
DFF = 5632
MEM = 256
NCONST = 512


class Ctx:
    pass


def setup_ctx(P, cdram):
    cx = Ctx()
    cx.P = P
    P.psum_init()
    cx.consts = P.tile([128, NCONST], F32, "consts")
    P.dma("sp", out=cx.consts[:], in_=cdram[:])
    cx.identf = cx.consts[:, 0:128]
    cx.identb_t = P.tile([128, 128], BF16, "identb")
    P.dve.tensor_copy(out=cx.identb_t[:], in_=cx.consts[:, 0:128])
    cx.identb = cx.identb_t[:]
    cx.ones_f = cx.consts[:, 320:448]
    cx.small = P.ring([128, 8], F32, 8, "small")
    cx.small32 = P.ring([128, 32], F32, 6, "small32")
    return cx


def bcast_load(P, dram_row, K, name):
    t = P.tile([128, K], F32, name)
    P.dma("sp", out=t[:], in_=dram_row[:, :].to_broadcast([128, K]))
    return t


def dma_w(P, out_tile, w_view, KC, ncols, q="pool"):
    step = 16
    for k0 in range(0, KC, step):
        k1 = min(KC, k0 + step)
        P.dma(q, out=out_tile[:, k0:k1, :ncols],
              in_=w_view[k0 * 128:k1 * 128, :].rearrange("(c p) n -> p c n", p=128))


def norm_T(cx, srcs, K, wbc, dstT, tok0, xn_ring, tp_banks, do_norm=True):
    P = cx.P
    KC = K // 128
    for i, src in enumerate(srcs):
        xn = xn_ring.next()
        if do_norm:
            ss = cx.small.next()
            P.act.activation(out=xn[:, :K], in_=src, func=AF.Square, accum_out=ss[:, 0:1])
            P.dve.tensor_scalar(out=ss[:, 1:2], in0=ss[:, 0:1], scalar1=1.0 / K, scalar2=EPS, op0=ALU.mult, op1=ALU.add)
            P.act.activation(out=ss[:, 2:3], in_=ss[:, 1:2], func=AF.Ln)
            P.act.activation(out=ss[:, 2:3], in_=ss[:, 2:3], func=AF.Exp, scale=-0.5)
            P.dve.scalar_tensor_tensor(out=xn[:, :K], in0=src, scalar=ss[:, 2:3], in1=wbc, op0=ALU.mult, op1=ALU.mult)
        else:
            P.act.activation(out=xn[:, :K], in_=src, func=AF.Copy)
        for j, c0 in enumerate(range(0, KC, 8)):
            pb = tp_banks.next()
            n = min(8, KC - c0)
            for c in range(n):
                P.pe.transpose(out=pb[:, c * 128:(c + 1) * 128], in_=xn[:, (c0 + c) * 128:(c0 + c + 1) * 128],
                               identity=cx.identb)
            o = dstT[:, c0:c0 + n, tok0 + i * 128:tok0 + (i + 1) * 128]
            s = pb[:, :n * 128].rearrange("p (c t) -> p c t", c=n)
            if j % 2 == 0:
                P.dve.tensor_copy(out=o, in_=s)
            else:
                P.act.activation(out=o, in_=s, func=AF.Copy)


def stage_b(P, cx, io, layer_kind, final):
    mark = P.mark()
    tp_banks = Ring([P.bank_bf(0), P.bank_bf(1)], P)
    mm_banks = Ring([P.bank(2), P.bank(3), P.bank(4), P.bank(5)], P)
    RW = 2048 if layer_kind == 0 else 2064
    h = [P.tile([128, D], F32, f"h{t}") for t in range(NTT)]
    rsr = P.ring([128, RW], F32, 2, "rs")
    for t in range(NTT):
        P.dma("sp", out=h[t][:], in_=io["hres"][t * 128:(t + 1) * 128, :])
        rs = rsr.next()
        P.dma("sp", out=rs[:], in_=io["rs"][t * 128:(t + 1) * 128, :])
        if layer_kind == 0:
            P.dve.tensor_tensor(out=h[t][:], in0=rs[:, 0:D], in1=h[t][:], op=ALU.add)
        else:
            ss = cx.small.next()
            P.dve.tensor_scalar(out=ss[:, 1:2], in0=rs[:, D:D + 1], scalar1=1.0 / 4096, scalar2=EPS, op0=ALU.mult, op1=ALU.add)
            P.act.activation(out=ss[:, 2:3], in_=ss[:, 1:2], func=AF.Ln)
            P.act.activation(out=ss[:, 2:3], in_=ss[:, 2:3], func=AF.Exp, scale=-0.5)
            P.dve.scalar_tensor_tensor(out=h[t][:], in0=rs[:, 0:D], scalar=ss[:, 2:3], in1=h[t][:], op0=ALU.mult, op1=ALU.add)
    rsr.free()
    xn_ring = P.ring([128, D], BF16, 2, "xn2")
    hnT = P.tile([128, 16, NT], BF16, "hnT")
    wbc = bcast_load(P, io["vec_xattn"], D, "wbc_xa")
    norm_T(cx, [h[t][:] for t in range(NTT)], D, wbc[:], hnT, 0, xn_ring, tp_banks)
    mnT = P.tile([128, 16, MEM], BF16, "mnT")
    wbm = bcast_load(P, io["vec_mem"], D, "wbc_mem")
    ld = P.ring([128, D], F32, 2, "memld")
    for t in range(MEM // 128):
        lt = ld.next()
        P.dma("sp", out=lt[:], in_=io["mem"][t * 128:(t + 1) * 128, :])
        norm_T(cx, [lt[:]], D, wbm[:], mnT, t * 128, xn_ring, tp_banks)
    ld.free()
    P.free(wbc, wbm)
    wq = P.tile([128, 16, 512], BF16, "wq")
    wk = P.tile([128, 16, 512], BF16, "wk")
    wv = P.tile([128, 16, 512], BF16, "wv")
    dma_w(P, wk, io["wk"], 16, 512)
    dma_w(P, wv, io["wv"], 16, 512)
    dma_w(P, wq, io["wq"], 16, 512)
    kT = P.tile([128, 4, MEM], BF16, "kT")
    vtm = P.tile([128, 2, 512], BF16, "vtm")
    qT = P.tile([128, 4, NT], BF16, "qT")
    for hd in range(4):
        pb = mm_banks.next()
        for kc in range(16):
            P.pe.matmul(out=pb[:, :MEM], lhsT=wk[:, kc, hd * 128:(hd + 1) * 128], rhs=mnT[:, kc, :], start=(kc == 0), stop=(kc == 15))
        P.act.activation(out=kT[:, hd, :], in_=pb[:, :MEM], func=AF.Copy)
    for mt in range(2):
        pb = mm_banks.next()
        for kc in range(16):
            P.pe.matmul(out=pb[:, :512], lhsT=mnT[:, kc, mt * 128:(mt + 1) * 128], rhs=wv[:, kc, :], start=(kc == 0), stop=(kc == 15))
        P.act.activation(out=vtm[:, mt, :], in_=pb[:, :512], func=AF.Copy)
    for hd in range(4):
        for th in range(NT // 512):
            pb = mm_banks.next()
            for kc in range(16):
                P.pe.matmul(out=pb[:, :512], lhsT=wq[:, kc, hd * 128:(hd + 1) * 128], rhs=hnT[:, kc, th * 512:(th + 1) * 512],
                            start=(kc == 0), stop=(kc == 15))
            P.act.activation(out=qT[:, hd, th * 512:(th + 1) * 512], in_=pb[:, :512], func=AF.Copy)
    P.free(wq, wk, wv, mnT, hnT)
    oT = P.tile([128, 4, NT], BF16, "oT")
    sc = 128 ** -0.5
    shr = P.ring([128, 4, MEM], F32, 2, "xsh")
    pnr = P.ring([128, 4, MEM], BF16, 2, "xpn")
    pTr = P.ring([128, 8, 128], BF16, 2, "xpT")
    sc_banks = Ring([P.bank(4, 2), P.bank(6, 2)], P)
    ob_banks = Ring([P.bank(2), P.bank(3)], P)
    for t in range(NTT):
        psc = sc_banks.next()
        for hd in range(4):
            P.pe.matmul(out=psc[:, hd * MEM:(hd + 1) * MEM], lhsT=qT[:, hd, t * 128:(t + 1) * 128], rhs=kT[:, hd, :], start=True, stop=True)
        ps3 = psc[:, :].rearrange("p (h m) -> p h m", h=4)
        sm = cx.small.next()
        P.dve.tensor_reduce(out=sm[:, 0:4], in_=ps3, axis=AX.X, op=ALU.max)
        sh = shr.next()
        P.dve.tensor_tensor(out=sh[:, :, :], in0=ps3, in1=bc3(sm[:, 0:4], 4, MEM), op=ALU.subtract)
        P.act.activation(out=sh[:, :, :], in_=sh[:, :, :], func=AF.Exp, scale=sc)
        P.dve.tensor_reduce(out=sm[:, 4:8], in_=sh[:, :, :], axis=AX.X, op=ALU.add)
        sm2 = cx.small.next()
        P.dve.reciprocal(out=sm2[:, 0:4], in_=sm[:, 4:8])
        pn = pnr.next()
        P.dve.tensor_tensor(out=pn[:, :, :], in0=sh[:, :, :], in1=bc3(sm2[:, 0:4], 4, MEM), op=ALU.mult)
        tb = tp_banks.next()
        for hd in range(4):
            for mc in range(2):
                j = hd * 2 + mc
                P.pe.transpose(out=tb[:, j * 128:(j + 1) * 128], in_=pn[:, hd, mc * 128:(mc + 1) * 128], identity=cx.identb)
        pT = pTr.next()
        P.act.activation(out=pT[:, :, :], in_=tb[:, :1024].rearrange("p (c t) -> p c t", c=8), func=AF.Copy)
        ob = ob_banks.next()
        for hd in range(4):
            for mc in range(2):
                P.pe.matmul(out=ob[:, hd * 128:(hd + 1) * 128], lhsT=vtm[:, mc, hd * 128:(hd + 1) * 128], rhs=pT[:, hd * 2 + mc, :],
                            start=(mc == 0), stop=(mc == 1))
        P.act.activation(out=oT[:, :, t * 128:(t + 1) * 128], in_=ob[:, :512].rearrange("p (h s) -> p h s", h=4), func=AF.Copy)
    for r_ in (shr, pnr, pTr):
        r_.free()
    P.free(qT, kT, vtm)
    wo = P.tile([128, 4, D], BF16, "wo")
    for c in range(4):
        P.dma("pool", out=wo[:, c, :], in_=io["wo"][c * 128:(c + 1) * 128, :])
    for t in range(NTT):
        for cg in range(4):
            pb = mm_banks.next()
            for hd in range(4):
                P.pe.matmul(out=pb[:, :512], lhsT=oT[:, hd, t * 128:(t + 1) * 128], rhs=wo[:, hd, cg * 512:(cg + 1) * 512],
                            start=(hd == 0), stop=(hd == 3))
            P.dve.tensor_tensor(out=h[t][:, cg * 512:(cg + 1) * 512], in0=pb[:, :512], in1=h[t][:, cg * 512:(cg + 1) * 512], op=ALU.add)
    P.free(wo, oT)
    hnT = P.tile([128, 16, NT], BF16, "hn2T")
    wbc = bcast_load(P, io["vec_ffn"], D, "wbc_ffn")
    norm_T(cx, [h[t][:] for t in range(NTT)], D, wbc[:], hnT, 0, xn_ring, tp_banks)
    P.free(wbc)
    FB = 256
    ffn_banks = Ring([P.bank(b_) for b_ in (2, 3, 4, 5, 6, 7, 0, 1)], P)
    wgr = P.ring([128, 16, FB], BF16, 2, "wg")
    wur = P.ring([128, 16, FB], BF16, 2, "wu")
    wdr = P.ring([128, FB // 128, D], BF16, 2, "wd")
    actr = P.ring([128, FB // 128, NT], BF16, 2, "act")
    sgr = P.ring([128, 512], F32, 2, "sg")
    for fb in range(DFF // FB):
        wg = wgr.next()
        wu = wur.next()
        wd = wdr.next()
        dma_w(P, wg, io["w_gate"][:, fb * FB:(fb + 1) * FB], 16, FB)
        dma_w(P, wu, io["w_up"][:, fb * FB:(fb + 1) * FB], 16, FB)
        for s in range(FB // 128):
            P.dma("pool", out=wd[:, s, :], in_=io["w_down"][fb * FB + s * 128:fb * FB + (s + 1) * 128, :])
        act = actr.next()
        for s in range(FB // 128):
            for th in range(NT // 512):
                pg = ffn_banks.next()
                for kc in range(16):
                    P.pe.matmul(out=pg[:, :512], lhsT=wg[:, kc, s * 128:(s + 1) * 128], rhs=hnT[:, kc, th * 512:(th + 1) * 512],
                                start=(kc == 0), stop=(kc == 15))
                pu = ffn_banks.next()
                for kc in range(16):
                    P.pe.matmul(out=pu[:, :512], lhsT=wu[:, kc, s * 128:(s + 1) * 128], rhs=hnT[:, kc, th * 512:(th + 1) * 512],
                                start=(kc == 0), stop=(kc == 15))
                sg = sgr.next()
                P.act.activation(out=sg[:], in_=pg[:, :512], func=AF.Silu)
                P.dve.tensor_tensor(out=act[:, s, th * 512:(th + 1) * 512], in0=pu[:, :512], in1=sg[:], op=ALU.mult)
        for t in range(NTT):
            for cg in range(4):
                pb = ffn_banks.next()
                for s in range(FB // 128):
                    P.pe.matmul(out=pb[:, :512], lhsT=act[:, s, t * 128:(t + 1) * 128], rhs=wd[:, s, cg * 512:(cg + 1) * 512],
                                start=(s == 0), stop=(s == FB // 128 - 1))
                P.dve.tensor_tensor(out=h[t][:, cg * 512:(cg + 1) * 512], in0=pb[:, :512], in1=h[t][:, cg * 512:(cg + 1) * 512], op=ALU.add)
    for r in (wgr, wur, wdr, actr, sgr):
        r.free()
    P.free(hnT)
    outs = []
    if final:
        wbc = bcast_load(P, io["vec_final"], D, "wbc_fin")
        orr = P.ring([128, D], F32, 2, "fin")
        for t in range(NTT):
            ss = cx.small.next()
            ot = orr.next()
            P.act.activation(out=ot[:], in_=h[t][:], func=AF.Square, accum_out=ss[:, 0:1])
            P.dve.tensor_scalar(out=ss[:, 1:2], in0=ss[:, 0:1], scalar1=1.0 / D, scalar2=EPS, op0=ALU.mult, op1=ALU.add)
            P.act.activation(out=ss[:, 2:3], in_=ss[:, 1:2], func=AF.Ln)
            P.act.activation(out=ss[:, 2:3], in_=ss[:, 2:3], func=AF.Exp, scale=-0.5)
            P.dve.scalar_tensor_tensor(out=ot[:], in0=h[t][:], scalar=ss[:, 2:3], in1=wbc[:], op0=ALU.mult, op1=ALU.mult)
            outs.append(P.dma("sp", out=io["hout"][t * 128:(t + 1) * 128, :], in_=ot[:]))
        orr.free()
        P.free(wbc)
    else:
        wbc = bcast_load(P, io["vec_mix_next"], D, "wbc_nx")
        ur = P.ring([128, D], BF16, 2, "ub")
        for t in range(NTT):
            outs.append(P.dma("sp", out=io["hout"][t * 128:(t + 1) * 128, :], in_=h[t][:]))
            ss = cx.small.next()
            ub = ur.next()
            P.act.activation(out=ub[:], in_=h[t][:], func=AF.Square, accum_out=ss[:, 0:1])
            P.dve.tensor_scalar(out=ss[:, 1:2], in0=ss[:, 0:1], scalar1=1.0 / D, scalar2=EPS, op0=ALU.mult, op1=ALU.add)
            P.act.activation(out=ss[:, 2:3], in_=ss[:, 1:2], func=AF.Ln)
            P.act.activation(out=ss[:, 2:3], in_=ss[:, 2:3], func=AF.Exp, scale=-0.5)
            P.dve.scalar_tensor_tensor(out=ub[:], in0=h[t][:], scalar=ss[:, 2:3], in1=wbc[:], op0=ALU.mult, op1=ALU.mult)
            outs.append(P.dma("sp", out=io["u_half"][t // 2][(t % 2) * 128:(t % 2 + 1) * 128, :], in_=ub[:]))
    P.release(mark)
    return outs


SEQ = 2048
NTS = SEQ // 128
NC2 = 512


def softplus_inplace(P, x, tmp):
    P.dve.tensor_scalar(out=tmp, in0=x, scalar1=-1.0, scalar2=None, op0=ALU.mult)
    P.dve.tensor_tensor(out=tmp, in0=tmp, in1=x, op=ALU.max)
    P.act.activation(out=tmp, in_=tmp, func=AF.Exp, scale=-1.0)
    P.act.activation(out=tmp, in_=tmp, func=AF.Ln, bias=1.0, scale=1.0)
    P.dve.tensor_scalar(out=x, in0=x, scalar1=0.0, scalar2=None, op0=ALU.max)
    P.dve.tensor_tensor(out=x, in0=x, in1=tmp, op=ALU.add)


def conv_silu_fm(P, cx, w_t, uT, col0, xpad, acc, cw, cb, dst_bf, mm_banks, silu=True):
    for th in range(SEQ // 512):
        pb = mm_banks.next()
        for kc in range(16):
            P.pe.matmul(out=pb[:, :512], lhsT=w_t[:, kc, col0:col0 + 128], rhs=uT[:, kc, th * 512:(th + 1) * 512],
                        start=(kc == 0), stop=(kc == 15))
        P.act.activation(out=xpad[:, 3 + th * 512:3 + (th + 1) * 512], in_=pb[:, :512], func=AF.Copy)
    if cb is not None:
        P.dve.tensor_scalar(out=acc[:, :], in0=xpad[:, 3:3 + SEQ], scalar1=cw[:, 3:4], scalar2=cb, op0=ALU.mult, op1=ALU.add)
    else:
        P.dve.tensor_scalar(out=acc[:, :], in0=xpad[:, 3:3 + SEQ], scalar1=cw[:, 3:4], scalar2=None, op0=ALU.mult)
    for k in range(3):
        P.dve.scalar_tensor_tensor(out=acc[:, :], in0=xpad[:, k:k + SEQ], scalar=cw[:, k:k + 1], in1=acc[:, :], op0=ALU.mult, op1=ALU.add)
    P.act.activation(out=dst_bf, in_=acc[:, :], func=AF.Silu if silu else AF.Copy)


def fm_to_tm(P, cx, src_list, dst, tp_banks, parity=[0]):
    n = len(src_list)
    for t in range(NTS):
        for c0 in range(0, n, 8):
            m = min(8, n - c0)
            pb = tp_banks.next()
            for c in range(m):
                P.pe.transpose(out=pb[:, c * 128:(c + 1) * 128], in_=src_list[c0 + c][:, t * 128:(t + 1) * 128], identity=cx.identb)
            parity[0] ^= 1
            if parity[0]:
                P.dve.tensor_copy(out=dst[:, t, c0 * 128:(c0 + m) * 128], in_=pb[:, :m * 128])
            else:
                P.act.activation(out=dst[:, t, c0 * 128:(c0 + m) * 128], in_=pb[:, :m * 128], func=AF.Copy)


def partial_proj(P, cx, src, K, w_dram, part, norm_vec=None, sumsq=False):
    mark = P.mark()
    KC = K // 128
    tp_banks = Ring([P.bank_bf(0), P.bank_bf(1)], P)
    mm_banks = Ring([P.bank(2), P.bank(3), P.bank(4), P.bank(5)], P)
    srcT = P.tile([128, KC, SEQ], BF16, "srcT")
    ld = P.ring([128, K], F32, 2, "ppld")
    xnr = P.ring([128, K], BF16, 2, "ppxn")
    wbc = bcast_load(P, norm_vec, K, "ppw") if norm_vec is not None else None
    sqr = P.ring([128, 16], F32, 2, "ppsq") if sumsq else None
    outs = []
    for t in range(NTS):
        lt = ld.next()
        P.dma("sp", out=lt[:], in_=src[t * 128:(t + 1) * 128, :])
        xn = xnr.next()
        if sumsq:
            sq = sqr.next()
            P.dve.memset(sq[:, :], 0.0)
            P.act.activation(out=xn[:, :], in_=lt[:, :], func=AF.Square, accum_out=sq[:, 0:1])
            outs.append(P.dma("sp", out=part[t * 128:(t + 1) * 128, 2048:2064], in_=sq[:, :]))
        if wbc is not None:
            P.dve.tensor_tensor(out=xn[:, :], in0=lt[:, :], in1=wbc[:, :], op=ALU.mult)
        else:
            P.dve.tensor_copy(out=xn[:, :], in_=lt[:, :])
        for j, c0 in enumerate(range(0, KC, 8)):
            pb = tp_banks.next()
            n = min(8, KC - c0)
            for c in range(n):
                P.pe.transpose(out=pb[:, c * 128:(c + 1) * 128], in_=xn[:, (c0 + c) * 128:(c0 + c + 1) * 128], identity=cx.identb)
            P.act.activation(out=srcT[:, c0:c0 + n, t * 128:(t + 1) * 128], in_=pb[:, :n * 128].rearrange("p (c t) -> p c t", c=n), func=AF.Copy)
    ld.free()
    xnr.free()
    wring = P.ring([128, KC, 512], BF16, 2, "ppwr")
    stg = P.ring([128, 512], F32, 3, "ppst")
    for cg in range(4):
        wt = wring.next()
        dma_w(P, wt, w_dram[:, cg * 512:(cg + 1) * 512], KC, 512)
        for t in range(NTS):
            pb = mm_banks.next()
            for kc in range(KC):
                P.pe.matmul(out=pb[:, :512], lhsT=srcT[:, kc, t * 128:(t + 1) * 128], rhs=wt[:, kc, :], start=(kc == 0), stop=(kc == KC - 1))
            st = stg.next()
            if t % 2 == 0:
                P.dve.tensor_copy(out=st[:, :], in_=pb[:, :512])
            else:
                P.act.activation(out=st[:, :], in_=pb[:, :512], func=AF.Copy)
            outs.append(P.dma("sp", out=part[t * 128:(t + 1) * 128, cg * 512:(cg + 1) * 512], in_=st[:, :]))
    P.release(mark)
    return outs


def stage_ssd(P, cx, io):
    GW = 1288
    mark = P.mark()
    tp_banks = Ring([P.bank_bf(0), P.bank_bf(1)], P)
    mm_banks = Ring([P.bank(2), P.bank(3)], P)
    c2 = P.tile([128, NC2], F32, "c2")
    P.dma("sp", out=c2[:], in_=io["c2"][:, :])
    U2, SL2, HA, HB = c2[:, 0:128], c2[:, 128:256], c2[:, 256:384], c2[:, 384:512]
    uT = P.tile([128, 16, SEQ], BF16, "uT")
    ld = P.ring([128, D], BF16, 2, "ld")
    for t in range(NTS):
        lt = ld.next()
        r0_ = (t // 8) * 256 + (t % 2) * 128
        P.dma("sp", out=lt[:], in_=io["u_full"][(t % 8) // 2][r0_:r0_ + 128, :])
        for j, c0 in enumerate((0, 8)):
            pb = tp_banks.next()
            for c in range(8):
                P.pe.transpose(out=pb[:, c * 128:(c + 1) * 128], in_=lt[:, (c0 + c) * 128:(c0 + c + 1) * 128], identity=cx.identb)
            o = uT[:, c0:c0 + 8, t * 128:(t + 1) * 128]
            sv = pb[:, :1024].rearrange("p (c t) -> p c t", c=8)
            if j == 0:
                P.dve.tensor_copy(out=o, in_=sv)
            else:
                P.act.activation(out=o, in_=sv, func=AF.Copy)
    ld.free()
    cwx = P.tile([128, 16, 4], F32, "cwx"); P.dma("sp", out=cwx[:], in_=io["cwx"][:, :, :])
    cbx = P.tile([128, 16], F32, "cbx"); P.dma("sp", out=cbx[:], in_=io["cbx"][:, :])
    cwbc = P.tile([128, 8, 4], F32, "cwbc"); P.dma("sp", out=cwbc[:], in_=io["cwbc"][:, :, :])
    cbbc = P.tile([128, 8], F32, "cbbc"); P.dma("sp", out=cbbc[:], in_=io["cbbc"][:, :])
    dtb = bcast_load(P, io["dtb"], 32, "dtb")
    aneg = bcast_load(P, io["alog"], 32, "aneg")
    P.act.activation(out=aneg[:], in_=aneg[:], func=AF.Exp)
    P.dve.tensor_scalar(out=aneg[:], in0=aneg[:], scalar1=-1.0, scalar2=None, op0=ALU.mult)
    outs = []
    wAr = P.ring([128, 16, 512], BF16, 2, "wA")
    wSr = P.ring([128, 16, 264], BF16, 2, "wS")
    for g in range(4):
        wA = wAr.next()
        wS = wSr.next()
        wZ = wAr.next()
        dma_w(P, wA, io["w_in"][:, g * GW:g * GW + 512], 16, 512)
        dma_w(P, wS, io["w_in"][:, g * GW + 1024:g * GW + 1288], 16, 264)
        dma_w(P, wZ, io["w_in"][:, g * GW + 512:g * GW + 1024], 16, 512)
        xpad = P.tile([128, SEQ + 3], F32, "xpad")
        P.dve.memset(xpad[:, 0:3], 0.0)
        acc = P.tile([128, SEQ], F32, "acc")
        fm = [P.tile([128, SEQ], BF16, f"fm{i}") for i in range(6)]
        for cc in range(4):
            conv_silu_fm(P, cx, wA, uT, cc * 128, xpad, acc, cwx[:, g * 4 + cc, :], cbx[:, g * 4 + cc:g * 4 + cc + 1], fm[cc][:, :], mm_banks)
        conv_silu_fm(P, cx, wS, uT, 0, xpad, acc, cwbc[:, g, :], cbbc[:, g:g + 1], fm[4][:, :], mm_banks)
        conv_silu_fm(P, cx, wS, uT, 128, xpad, acc, cwbc[:, 4 + g, :], cbbc[:, 4 + g:5 + g], fm[5][:, :], mm_banks)
        BT, CT = fm[4], fm[5]
        dt = P.tile([128, NTS, 8], F32, "dt")
        for t in range(NTS):
            pb = mm_banks.next()
            for kc in range(16):
                P.pe.matmul(out=pb[:, :8], lhsT=uT[:, kc, t * 128:(t + 1) * 128], rhs=wS[:, kc, 256:264], start=(kc == 0), stop=(kc == 15))
            P.dve.tensor_tensor(out=dt[:, t, :], in0=pb[:, :8], in1=dtb[:, g * 8:(g + 1) * 8], op=ALU.add)
        tmp = P.tile([128, NTS, 8], F32, "dttmp")
        softplus_inplace(P, dt[:, :, :], tmp[:, :, :])
        av = P.tile([128, NTS, 8], F32, "av")
        P.dve.tensor_tensor(out=av[:, :, :], in0=dt[:, :, :], in1=aneg[:, g * 8:(g + 1) * 8].unsqueeze(1).to_broadcast([128, NTS, 8]), op=ALU.mult)
        P.free(tmp, xpad, acc)
        sz = P.tile([128, NTS, 512], BF16, "sz")
        for t in range(NTS):
            pb = mm_banks.next()
            for kc in range(16):
                P.pe.matmul(out=pb[:, :512], lhsT=uT[:, kc, t * 128:(t + 1) * 128], rhs=wZ[:, kc, :], start=(kc == 0), stop=(kc == 15))
            P.act.activation(out=sz[:, t, :], in_=pb[:, :512], func=AF.Silu)
        x_tm = P.tile([128, NTS, 512], BF16, "x_tm")
        fm_to_tm(P, cx, [fm[i][:, :] for i in range(4)], x_tm, tp_banks)
        B_tm = P.tile([128, NTS, 128], BF16, "B_tm")
        fm_to_tm(P, cx, [BT[:, :]], B_tm, tp_banks)
        P.free(*fm[:4])
        dfull = bcast_load(P, io["dfull"][:, g * 512:(g + 1) * 512], 512, "dfull")
        S = P.tile([128, 512], F32, "S")
        P.dve.memset(S[:, :], 0.0)
        Sb = P.ring([128, 512], BF16, 3, "Sb")
        S0b = Sb.next()
        P.dve.memset(S0b[:, :], 0.0)
        aUr = P.ring([128, 8, 128], F32, 2, "aU")
        Lr = P.ring([128, 8, 128], F32, 2, "L")
        Mr = P.ring([128, 8, 128], BF16, 2, "M")
        mcbr = P.ring([128, 128], F32, 2, "mcb")
        xcr = P.ring([128, 512], BF16, 2, "xc")
        xdr = P.ring([128, 512], BF16, 2, "xdec")
        yor = P.ring([128, 512], F32, 2, "yoff")
        ygr = P.ring([128, 512], F32, 2, "yg")
        ps_E = P.bank(4, 2)
        ps_sm = P.bank(6)
        ps_y = P.bank(7)
        def ssd_f1(t):
            a_t = av[:, t, :]
            P.pe.matmul(out=ps_sm[:, 0:8], lhsT=U2, rhs=a_t, start=True, stop=True)
            P.pe.matmul(out=ps_sm[:, 8:16], lhsT=SL2, rhs=a_t, start=True, stop=True)
            P.pe.matmul(out=ps_sm[:, 16:24], lhsT=HA, rhs=a_t, start=True, stop=True)
            P.pe.matmul(out=ps_sm[:, 24:32], lhsT=HB, rhs=a_t, start=True, stop=True)
            P.pe.matmul(out=ps_sm[:, 128:256], lhsT=BT[:, t * 128:(t + 1) * 128], rhs=CT[:, t * 128:(t + 1) * 128], start=True, stop=True)
            sm = cx.small32.next()
            P.act.activation(out=sm[:, 0:32], in_=ps_sm[:, 0:32], func=AF.Exp)
            mcb = mcbr.next()
            P.dve.tensor_tensor(out=mcb[:, :], in0=ps_sm[:, 128:256], in1=U2, op=ALU.mult)
            aU = aUr.next()
            P.dve.tensor_tensor(out=aU[:, :, :], in0=U2.unsqueeze(1).to_broadcast([128, 8, 128]),
                                in1=a_t.unsqueeze(2).to_broadcast([128, 8, 128]), op=ALU.mult)
            return sm, mcb, aU

        def ssd_f2(t, sm, mcb, aU):
            for q in range(2):
                P.pe.matmul(out=ps_E[:, q * 512:(q + 1) * 512], lhsT=SL2, rhs=aU[:, q * 4:(q + 1) * 4, :], start=True, stop=True)
            L = Lr.next()
            P.act.activation(out=L[:, :, :], in_=ps_E[:, :].rearrange("p (h i) -> p h i", h=8), func=AF.Exp)
            M = Mr.next()
            P.dve.tensor_tensor(out=M[:, :, :], in0=L[:, :, :], in1=mcb[:, :].unsqueeze(1).to_broadcast([128, 8, 128]), op=ALU.mult)
            xc = xcr.next()
            P.dve.tensor_tensor(out=xc[:, :].rearrange("p (h q) -> p h q", h=8), in0=x_tm[:, t, :].rearrange("p (h q) -> p h q", h=8),
                                in1=dt[:, t, :].unsqueeze(2).to_broadcast([128, 8, 64]), op=ALU.mult)
            xd = xdr.next()
            P.dve.tensor_tensor(out=xd[:, :].rearrange("p (h q) -> p h q", h=8), in0=xc[:, :].rearrange("p (h q) -> p h q", h=8),
                                in1=sm[:, 8:16].unsqueeze(2).to_broadcast([128, 8, 64]), op=ALU.mult)
            return sm, M, xc, xd

        def ssd_back(t, S0b, sm, M, xc, xd):
            ps_st = mm_banks.next()
            P.pe.matmul(out=ps_st[:, :512], lhsT=B_tm[0:64, t, :], rhs=xd[0:64, :], start=True, stop=True)
            S1b = Sb.next()
            P.dve.tensor_tensor(out=S[:, :].rearrange("p (h q) -> p h q", h=8), in0=S[:, :].rearrange("p (h q) -> p h q", h=8),
                                in1=sm[:, 16:24].unsqueeze(2).to_broadcast([128, 8, 64]), op=ALU.mult)
            P.dve.tensor_tensor(out=S[:, :], in0=ps_st[:, :512], in1=S[:, :], op=ALU.add)
            P.act.activation(out=S1b[:, :], in_=S[:, :], func=AF.Copy)
            ps_st2 = mm_banks.next()
            P.pe.matmul(out=ps_st2[:, :512], lhsT=B_tm[64:128, t, :], rhs=xd[64:128, :], start=True, stop=True)
            S2b = Sb.next()
            P.dve.tensor_tensor(out=S[:, :].rearrange("p (h q) -> p h q", h=8), in0=S[:, :].rearrange("p (h q) -> p h q", h=8),
                                in1=sm[:, 24:32].unsqueeze(2).to_broadcast([128, 8, 64]), op=ALU.mult)
            P.dve.tensor_tensor(out=S[:, :], in0=ps_st2[:, :512], in1=S[:, :], op=ALU.add)
            P.act.activation(out=S2b[:, :], in_=S[:, :], func=AF.Copy)
            ps_o = mm_banks.next()
            P.pe.matmul(out=ps_o[0:64, :512], lhsT=CT[:, t * 128:t * 128 + 64], rhs=S0b[:, :], start=True, stop=True)
            P.pe.matmul(out=ps_o[64:128, :512], lhsT=CT[:, t * 128 + 64:(t + 1) * 128], rhs=S1b[:, :], start=True, stop=True)
            yo = yor.next()
            P.dve.tensor_tensor(out=yo[:, :].rearrange("p (h q) -> p h q", h=8), in0=ps_o[:, :512].rearrange("p (h q) -> p h q", h=8),
                                in1=sm[:, 0:8].unsqueeze(2).to_broadcast([128, 8, 64]), op=ALU.mult)
            for hh in range(8):
                P.pe.matmul(out=ps_y[:, hh * 64:(hh + 1) * 64], lhsT=M[:, hh, :], rhs=xc[:, hh * 64:(hh + 1) * 64], start=True, stop=True)
            P.dve.tensor_tensor(out=yo[:, :], in0=ps_y[:, :512], in1=yo[:, :], op=ALU.add)
            yg = ygr.next()
            P.dve.tensor_tensor(out=yg[:, :], in0=x_tm[:, t, :], in1=dfull[:, :], op=ALU.mult)
            P.dve.tensor_tensor(out=yg[:, :], in0=yg[:, :], in1=yo[:, :], op=ALU.add)
            P.dve.tensor_tensor(out=yg[:, :], in0=yg[:, :], in1=sz[:, t, :], op=ALU.mult)
            outs.append(P.dma("sp", out=io["yg"][t * 128:(t + 1) * 128, g * 512:(g + 1) * 512], in_=yg[:, :]))
            return S2b

        st1, st2 = {}, {}
        for step in range(NTS + 2):
            if step < NTS:
                st1[step] = ssd_f1(step)
            if 0 <= step - 1 < NTS:
                st2[step - 1] = ssd_f2(step - 1, *st1.pop(step - 1))
            if 0 <= step - 2 < NTS:
                S0b = ssd_back(step - 2, S0b, *st2.pop(step - 2))
        for r in (Sb, aUr, Lr, Mr, mcbr, xcr, xdr, yor, ygr):
            r.free()
        P.free(S, dfull, x_tm, B_tm, BT, CT, sz, dt, av)
    P.release(mark)
    outs += partial_proj(P, cx, io["yg"], 2048, io["w_out"], io["part"], norm_vec=io["vec_ssdnorm"], sumsq=True)
    return outs


L2EPS = 1e-6
def proj_fm(P, w_t, uT, col0, dst, mm_banks, func=None):
    for th in range(SEQ // 512):
        pb = mm_banks.next()
        for kc in range(16):
            P.pe.matmul(out=pb[:, :512], lhsT=w_t[:, kc, col0:col0 + 128], rhs=uT[:, kc, th * 512:(th + 1) * 512],
                        start=(kc == 0), stop=(kc == 15))
        P.act.activation(out=dst[:, th * 512:(th + 1) * 512], in_=pb[:, :512], func=func or AF.Copy)


def proj_tm(P, w_t, uT, col0, ncols, dst, mm_banks, func=None):
    for t in range(NTS):
        pb = mm_banks.next()
        for kc in range(16):
            P.pe.matmul(out=pb[:, :ncols], lhsT=uT[:, kc, t * 128:(t + 1) * 128], rhs=w_t[:, kc, col0:col0 + ncols],
                        start=(kc == 0), stop=(kc == 15))
        P.act.activation(out=dst[:, t, :], in_=pb[:, :ncols], func=func or AF.Copy)


def bc3(v, n_mid, n_last):
    return v.unsqueeze(2).to_broadcast([v.shape[0], n_mid, n_last])


def bcm(v, n_mid):
    return v.unsqueeze(1).to_broadcast([v.shape[0], n_mid, v.shape[1]])


def head_norm_out(P, cx, o, gate_t, wn, dst_dram, nh, rings):
    sq = rings["sq"].next()
    P.dve.tensor_tensor(out=sq[:, :, :], in0=o, in1=o, op=ALU.mult)
    sm = cx.small.next()
    P.dve.tensor_reduce(out=sm[:, 0:nh], in_=sq[:, :, :], axis=AX.X, op=ALU.add)
    P.dve.tensor_scalar(out=sm[:, 0:nh], in0=sm[:, 0:nh], scalar1=1.0 / 128, scalar2=EPS, op0=ALU.mult, op1=ALU.add)
    P.act.activation(out=sm[:, 4:4 + nh], in_=sm[:, 0:nh], func=AF.Ln)
    P.act.activation(out=sm[:, 4:4 + nh], in_=sm[:, 4:4 + nh], func=AF.Exp, scale=-0.5)
    y = rings["y"].next()
    P.dve.tensor_tensor(out=y[:, :, :], in0=o, in1=bc3(sm[:, 4:4 + nh], nh, 128), op=ALU.mult)
    yf = y[:, :, :].rearrange("p h d -> p (h d)")
    P.dve.tensor_tensor(out=yf, in0=yf, in1=wn, op=ALU.mult)
    P.dve.tensor_tensor(out=yf, in0=yf, in1=gate_t, op=ALU.mult)
    return P.dma("sp", out=dst_dram, in_=yf)


def stage_hy(P, cx, io):
    mark = P.mark()
    tp_banks = Ring([P.bank_bf(0), P.bank_bf(1)], P)
    mm_banks = Ring([P.bank(2), P.bank(3)], P)
    c2 = P.tile([128, NC2], F32, "c2")
    P.dma("sp", out=c2[:], in_=io["c2"][:, :])
    U2, SL2, HA, HB = c2[:, 0:128], c2[:, 128:256], c2[:, 256:384], c2[:, 384:512]
    onesb = P.tile([128, 128], BF16, "onesb")
    P.dve.memset(onesb[:, :], 1.0)
    uT = P.tile([128, 16, SEQ], BF16, "uT")
    wbc = bcast_load(P, io["vec_mix"], D, "wbc")
    ld = P.ring([128, D], F32, 2, "ld")
    xn_ring = P.ring([128, D], BF16, 2, "xn")
    for t in range(NTS):
        lt = ld.next()
        P.dma("sp", out=lt[:], in_=io["xfull"][t * 128:(t + 1) * 128, :])
        norm_T(cx, [lt[:]], D, wbc[:], uT, t * 128, xn_ring, tp_banks)
    ld.free()
    xn_ring.free()
    P.free(wbc)
    outs = []
    lbl = P.tile([128, 8], F32, "lbl")
    P.dma("sp", out=lbl[:], in_=io["lbl"][:, :])
    lb = P.tile([128, 16], F32, "lb")
    P.dve.tensor_tensor(out=lb[:, 12:16], in0=lbl[:, 0:4], in1=lbl[:, 4:8], op=ALU.subtract)
    P.act.activation(out=lb[:, 0:4], in_=lb[:, 12:16], func=AF.Sigmoid)
    P.dve.tensor_scalar(out=lb[:, 4:8], in0=lb[:, 0:4], scalar1=-1.0, scalar2=1.0, op0=ALU.mult, op1=ALU.add)
    P.dve.tensor_scalar(out=lb[:, 8:12], in0=lb[:, 4:8], scalar1=-1.0, scalar2=None, op0=ALU.mult)
    rmask = P.tile([128, SEQ], BF16, "rmask")
    P.dve.memset(rmask[:, :], 1.0)
    P.dve.memset(rmask[:, :].rearrange("p (c j) -> p c j", j=64)[:, :, 0:1], 0.0)
    wtm = P.tile([128, 16, 512], BF16, "wtm")
    wtm2 = P.tile([128, 16, 512], BF16, "wtmb")
    dma_w(P, wtm, io["w_in"][:, 1024:1536], 16, 512)
    dma_w(P, wtm2, io["w_in"][:, 1536:2048], 16, 512)
    v_tm = P.tile([128, NTS, 512], BF16, "v_tm")
    proj_tm(P, wtm, uT, 0, 512, v_tm, mm_banks)
    sg_tm = P.tile([128, NTS, 512], BF16, "sg_tm")
    proj_tm(P, wtm2, uT, 0, 512, sg_tm, mm_banks, func=AF.Silu)
    P.free(wtm, wtm2)
    wn = bcast_load(P, io["hgrn_norm"], 512, "wn")
    rings1 = None
    qt_l, kt_l, kd_l, egl_l = [], [], [], []
    whr = P.ring([128, 16, 256], BF16, 2, "wh")
    for h in range(4):
        wh = whr.next()
        dma_w(P, wh, io["w_in"][:, h * 256:(h + 1) * 256], 16, 256)
        qf = P.tile([128, SEQ], BF16, "qf")
        ff = P.tile([128, SEQ], F32, "ff")
        proj_fm(P, wh, uT, 0, qf[:, :], mm_banks, func=AF.Silu)
        proj_fm(P, wh, uT, 128, ff[:, :], mm_banks, func=AF.Sigmoid)
        lf = P.tile([128, SEQ], F32, "lf")
        P.dve.tensor_scalar(out=lf[:, :], in0=ff[:, :], scalar1=lb[:, 4 + h:5 + h], scalar2=lb[:, h:h + 1], op0=ALU.mult, op1=ALU.add)
        P.act.activation(out=lf[:, :], in_=lf[:, :], func=AF.Ln)
        kk = P.tile([128, SEQ], F32, "kk")
        P.dve.tensor_scalar(out=kk[:, :], in0=ff[:, :], scalar1=lb[:, 8 + h:9 + h], scalar2=lb[:, 4 + h:5 + h], op0=ALU.mult, op1=ALU.add)
        g = ff
        P.dve.tensor_tensor_scan(out=g[:, :], data0=rmask[:, :], data1=lf[:, :], initial=0.0, op0=ALU.mult, op1=ALU.add)
        eg = lf
        P.act.activation(out=eg[:, :], in_=g[:, :], func=AF.Exp)
        qt = P.tile([128, SEQ], BF16, "qt")
        P.dve.tensor_tensor(out=qt[:, :], in0=qf[:, :], in1=eg[:, :], op=ALU.mult)
        eng = qf
        P.act.activation(out=eng[:, :], in_=g[:, :], func=AF.Exp, scale=-1.0)
        kt = P.tile([128, SEQ], BF16, "kt")
        P.dve.tensor_tensor(out=kk[:, :], in0=kk[:, :], in1=eng[:, :], op=ALU.mult)
        P.act.activation(out=kt[:, :], in_=kk[:, :], func=AF.Copy)
        egl = P.tile([128, 32], F32, "egl")
        P.dve.tensor_copy(out=egl[:, :], in_=eg[:, :].rearrange("p (c j) -> p c j", j=64)[:, :, 63])
        kd = qf
        P.dve.tensor_tensor(out=kd[:, :].rearrange("p (c j) -> p c j", j=64), in0=kk[:, :].rearrange("p (c j) -> p c j", j=64),
                            in1=bc3(egl[:, :], 32, 64), op=ALU.mult)
        kd_tm = P.tile([128, NTS, 128], BF16, "kd_tm")
        fm_to_tm(P, cx, [kd[:, :]], kd_tm, tp_banks)
        P.free(qf, ff, lf, kk)
        qt_l.append(qt); kt_l.append(kt); kd_l.append(kd_tm); egl_l.append(egl)
    whr.free()
    qb = Ring([P.psum_slice(bk, 0, 128) for bk in range(2, 8)], P)
    S_l, Sb_l, S0_l = [], [], []
    for h in range(4):
        S = P.tile([128, 128], F32, "S")
        P.dve.memset(S[:, :], 0.0)
        Sb = P.ring([128, 128], BF16, 3, "Sb")
        S0b = Sb.next()
        P.dve.memset(S0b[:, :], 0.0)
        S_l.append(S); Sb_l.append(Sb); S0_l.append(S0b)
    Amr = P.ring([128, 128], BF16, 8, "Am")
    oir = P.ring([128, 128], F32, 4, "oi")
    o1r = P.ring([128, 1, 128], F32, 1, "o1")
    o4r = P.ring([128, 4, 128], F32, 2, "o4")
    rings4 = {"sq": P.ring([128, 4, 128], F32, 2, "sq4"), "y": P.ring([128, 4, 128], F32, 2, "y4")}
    for t in range(NTS):
        ts = slice(t * 128, (t + 1) * 128)
        keep = {}
        for h in range(4):
            qt, kt, kd_tm, egl, S, Sb, S0b = qt_l[h], kt_l[h], kd_l[h], egl_l[h], S_l[h], Sb_l[h], S0_l[h]
            pa = qb.next()
            P.pe.matmul(out=pa[:, :], lhsT=kt[:, ts], rhs=qt[:, ts], start=True, stop=True)
            Am = Amr.next()
            P.dve.tensor_tensor(out=Am[:, :], in0=pa[:, :], in1=U2, op=ALU.mult)
            ps1 = qb.next()
            P.pe.matmul(out=ps1[:, :], lhsT=kd_tm[0:64, t, :], rhs=v_tm[0:64, t, h * 128:(h + 1) * 128], start=True, stop=True)
            P.dve.scalar_tensor_tensor(out=S[:, :], in0=S[:, :], scalar=egl[:, 2 * t:2 * t + 1], in1=ps1[:, :], op0=ALU.mult, op1=ALU.add)
            S1b = Sb.next()
            P.act.activation(out=S1b[:, :], in_=S[:, :], func=AF.Copy)
            ps2 = qb.next()
            P.pe.matmul(out=ps2[:, :], lhsT=kd_tm[64:128, t, :], rhs=v_tm[64:128, t, h * 128:(h + 1) * 128], start=True, stop=True)
            P.dve.scalar_tensor_tensor(out=S[:, :], in0=S[:, :], scalar=egl[:, 2 * t + 1:2 * t + 2], in1=ps2[:, :], op0=ALU.mult, op1=ALU.add)
            S2b = Sb.next()
            P.act.activation(out=S2b[:, :], in_=S[:, :], func=AF.Copy)
            keep[h] = (Am, S1b, S2b)
        o4 = o4r.next()
        for h in range(4):
            qt, S0b = qt_l[h], S0_l[h]
            Am, S1b, S2b = keep[h]
            pi = qb.next()
            P.pe.matmul(out=pi[0:64, :], lhsT=qt[:, t * 128:t * 128 + 64], rhs=S0b[:, :], start=True, stop=True)
            P.pe.matmul(out=pi[64:128, :], lhsT=qt[:, t * 128 + 64:(t + 1) * 128], rhs=S1b[:, :], start=True, stop=True)
            oi = oir.next()
            P.act.activation(out=oi[:, :], in_=pi[:, :], func=AF.Copy)
            po = qb.next()
            P.pe.matmul(out=po[:, :], lhsT=Am[:, :], rhs=v_tm[:, t, h * 128:(h + 1) * 128], start=True, stop=True)
            P.dve.tensor_tensor(out=o4[:, h, :], in0=po[:, :], in1=oi[:, :], op=ALU.add)
            S0_l[h] = S2b
        outs.append(head_norm_out(P, cx, o4[:, :, :], sg_tm[:, t, :], wn[:, :], io["omix"][t * 128:(t + 1) * 128, 0:512], 4, rings4))
    for r in Sb_l + [Amr, oir, o1r, o4r, rings4["sq"], rings4["y"]]:
        r.free()
    P.free(*(S_l + qt_l + kt_l + kd_l + egl_l))
    P.free(v_tm, sg_tm, wn, rmask, lb, lbl)
    rings = {"sq": P.ring([128, 4, 128], F32, 2, "sq"), "y": P.ring([128, 4, 128], F32, 2, "y")}
    cw = P.tile([128, 12, 4], F32, "cw")
    P.dma("sp", out=cw[:], in_=io["gcw"][:, :, :])
    qn = [P.tile([128, SEQ], BF16, f"qn{h}") for h in range(4)]
    kn = [P.tile([128, SEQ], BF16, f"kn{h}") for h in range(4)]
    kv_tm = [P.tile([128, NTS, 256], BF16, f"kv{h}") for h in range(4)]
    whgr = P.ring([128, 16, 384], BF16, 2, "whg")
    for h in range(4):
        wh = whgr.next()
        dma_w(P, wh, io["w_in"][:, 2048 + h * 384:2048 + (h + 1) * 384], 16, 384)
        xpad = P.tile([128, SEQ + 3], F32, "xpad")
        P.dve.memset(xpad[:, 0:3], 0.0)
        acc = P.tile([128, SEQ], F32, "acc")
        sq = P.tile([128, SEQ], BF16, "sqb")
        rn = P.tile([128, SEQ], F32, "rn")
        vb = P.tile([128, SEQ], BF16, "vb")
        for j, dst in enumerate((qn[h], kn[h])):
            conv_silu_fm(P, cx, wh, uT, j * 128, xpad, acc, cw[:, h * 3 + j, :], None, acc[:, :], mm_banks)
            P.act.activation(out=sq[:, :], in_=acc[:, :], func=AF.Square)
            for th in range(4):
                pb = mm_banks.next()
                P.pe.matmul(out=pb[:, :512], lhsT=onesb[:, :], rhs=sq[:, th * 512:(th + 1) * 512], start=True, stop=True)
                P.dve.tensor_scalar(out=rn[:, th * 512:(th + 1) * 512], in0=pb[:, :512], scalar1=L2EPS, scalar2=None, op0=ALU.add)
                P.act.activation(out=rn[:, th * 512:(th + 1) * 512], in_=rn[:, th * 512:(th + 1) * 512], func=AF.Ln)
                P.act.activation(out=rn[:, th * 512:(th + 1) * 512], in_=rn[:, th * 512:(th + 1) * 512], func=AF.Exp, scale=-0.5)
            if j == 0:
                P.dve.scalar_tensor_tensor(out=dst[:, :], in0=acc[:, :], scalar=128 ** -0.5, in1=rn[:, :], op0=ALU.mult, op1=ALU.mult)
            else:
                P.dve.tensor_tensor(out=dst[:, :], in0=acc[:, :], in1=rn[:, :], op=ALU.mult)
        conv_silu_fm(P, cx, wh, uT, 256, xpad, acc, cw[:, h * 3 + 2, :], None, vb[:, :], mm_banks)
        fm_to_tm(P, cx, [kn[h][:, :], vb[:, :]], kv_tm[h], tp_banks)
        P.free(xpad, acc, sq, rn, vb)
    whgr.free()
    wtm = P.tile([128, 16, 520], BF16, "wtm2")
    dma_w(P, wtm, io["w_in"][:, 3584:4104], 16, 520)
    sg_tm = P.tile([128, NTS, 512], BF16, "sg2")
    proj_tm(P, wtm, uT, 0, 512, sg_tm, mm_banks, func=AF.Silu)
    bd = P.tile([128, NTS, 8], F32, "bd")
    proj_tm(P, wtm, uT, 512, 8, bd, mm_banks)
    P.free(wtm, uT)
    gp = P.tile([128, 12], F32, "gp")
    P.dma("sp", out=gp[:, 0:8], in_=io["gdn_p"][:, :].to_broadcast([128, 8]))
    P.act.activation(out=gp[:, 0:4], in_=gp[:, 0:4], func=AF.Exp)
    P.dve.tensor_scalar(out=gp[:, 0:4], in0=gp[:, 0:4], scalar1=-1.0, scalar2=None, op0=ALU.mult)
    beta = P.tile([128, NTS, 4], F32, "beta")
    nbeta = P.tile([128, NTS, 4], F32, "nbeta")
    gg = P.tile([128, NTS, 4], F32, "gg")
    tmp = P.tile([128, NTS, 4], F32, "tmpg")
    P.act.activation(out=beta[:, :, :], in_=bd[:, :, 0:4], func=AF.Sigmoid)
    P.dve.tensor_scalar(out=nbeta[:, :, :], in0=beta[:, :, :], scalar1=-1.0, scalar2=None, op0=ALU.mult)
    P.dve.tensor_tensor(out=gg[:, :, :], in0=bd[:, :, 4:8], in1=gp[:, 4:8].unsqueeze(1).to_broadcast([128, NTS, 4]), op=ALU.add)
    P.dve.tensor_scalar(out=tmp[:, :, :], in0=gg[:, :, :], scalar1=-1.0, scalar2=None, op0=ALU.mult)
    P.dve.tensor_tensor(out=tmp[:, :, :], in0=tmp[:, :, :], in1=gg[:, :, :], op=ALU.max)
    P.act.activation(out=tmp[:, :, :], in_=tmp[:, :, :], func=AF.Exp, scale=-1.0)
    P.act.activation(out=tmp[:, :, :], in_=tmp[:, :, :], func=AF.Ln, bias=1.0, scale=1.0)
    P.dve.tensor_scalar(out=gg[:, :, :], in0=gg[:, :, :], scalar1=0.0, scalar2=None, op0=ALU.max)
    P.dve.tensor_tensor(out=gg[:, :, :], in0=gg[:, :, :], in1=tmp[:, :, :], op=ALU.add)
    P.dve.tensor_tensor(out=gg[:, :, :], in0=gg[:, :, :], in1=gp[:, 0:4].unsqueeze(1).to_broadcast([128, NTS, 4]), op=ALU.mult)
    P.free(tmp, bd)
    wn = bcast_load(P, io["gdn_norm"], 512, "wn2")
    S = P.tile([128, 4, 128], F32, "Sg")
    P.dve.memset(S[:, :, :], 0.0)
    Sbr = P.ring([128, 4, 128], BF16, 3, "Sgb")
    Sb_cur = Sbr.next()
    P.dve.memset(Sb_cur[:, :, :], 0.0)
    R3 = lambda nm, dt=F32, n=2: P.ring([128, 4, 128], dt, n, nm)
    gSLr, gUr, Dr, DTr, Zr, Yr, Pr, qkr, bvr, kdr, Rr, vnr, otr, TTr = (R3("gSL"), R3("gU"), R3("D"), R3("DT"), R3("Z", F32, 3), R3("Y", F32, 3), R3("P"),
                                                                   R3("qk", BF16), R3("bv"), R3("kdc", BF16), R3("R", BF16), R3("vn", BF16), R3("ot"), R3("TT", BF16))
    rfr = R3("rf")
    identf = cx.identf
    bE, bET, bKK, bKQ, bSM = P.bank(4), P.bank(5), P.bank(6), P.bank(7), P.bank(2)
    for t in range(NTS):
        ts = slice(t * 128, (t + 1) * 128)
        g_t = gg[:, t, :]
        P.pe.matmul(out=bSM[:, 0:4], lhsT=U2, rhs=g_t, start=True, stop=True)
        P.pe.matmul(out=bSM[:, 4:8], lhsT=SL2, rhs=g_t, start=True, stop=True)
        P.pe.matmul(out=bSM[:, 8:12], lhsT=HA, rhs=g_t, start=True, stop=True)
        P.pe.matmul(out=bSM[:, 12:16], lhsT=HB, rhs=g_t, start=True, stop=True)
        sm = cx.small32.next()
        P.act.activation(out=sm[:, 0:16], in_=bSM[:, 0:16], func=AF.Exp)
        P.dve.tensor_tensor(out=sm[:, 16:20], in0=sm[:, 0:4], in1=nbeta[:, t, :], op=ALU.mult)
        gSL, gU = gSLr.next(), gUr.next()
        P.dve.tensor_tensor(out=gSL[:, :, :], in0=bcm(SL2, 4), in1=bc3(g_t, 4, 128), op=ALU.mult)
        P.dve.tensor_tensor(out=gU[:, :, :], in0=bcm(U2, 4), in1=bc3(g_t, 4, 128), op=ALU.mult)
        P.pe.matmul(out=bE[:, :512], lhsT=U2, rhs=gSL[:, :, :], start=True, stop=True)
        P.pe.matmul(out=bET[:, :512], lhsT=SL2, rhs=gU[:, :, :], start=True, stop=True)
        for h in range(4):
            P.pe.matmul(out=bKK[:, h * 128:(h + 1) * 128], lhsT=kn[h][:, ts], rhs=kn[h][:, ts], start=True, stop=True)
            P.pe.matmul(out=bKQ[:, h * 128:(h + 1) * 128], lhsT=kn[h][:, ts], rhs=qn[h][:, ts], start=True, stop=True)
        Dm, DT = Dr.next(), DTr.next()
        P.act.activation(out=Dm[:, :, :], in_=bE[:, :512].rearrange("p (h j) -> p h j", h=4), func=AF.Exp)
        P.act.activation(out=DT[:, :, :], in_=bET[:, :512].rearrange("p (h j) -> p h j", h=4), func=AF.Exp)
        Z = Zr.next()
        P.dve.tensor_tensor(out=Z[:, :, :], in0=bKK[:, :512].rearrange("p (h j) -> p h j", h=4), in1=Dm[:, :, :], op=ALU.mult)
        P.dve.tensor_tensor(out=Z[:, :, :], in0=Z[:, :, :], in1=bcm(SL2, 4), op=ALU.mult)
        P.dve.tensor_tensor(out=Z[:, :, :], in0=Z[:, :, :], in1=bc3(nbeta[:, t, :], 4, 128), op=ALU.mult)
        qk = qkr.next()
        P.dve.tensor_tensor(out=DT[:, :, :], in0=bKQ[:, :512].rearrange("p (h j) -> p h j", h=4), in1=DT[:, :, :], op=ALU.mult)
        P.dve.tensor_tensor(out=qk[:, :, :], in0=DT[:, :, :], in1=bcm(U2, 4), op=ALU.mult)
        pT = mm_banks.next()
        for h in range(4):
            P.pe.transpose(out=pT[:, h * 128:(h + 1) * 128], in_=Z[:, h, :], identity=identf)
        Y = Yr.next()
        P.act.activation(out=Y[:, :, :], in_=pT[:, :512].rearrange("p (h j) -> p h j", h=4), func=AF.Copy)
        Pm = Pr.next()
        P.dve.tensor_tensor(out=Pm[:, :, :], in0=Y[:, :, :], in1=bcm(identf, 4), op=ALU.add)
        for m in range(1, 6):
            Zn = Zr.next()
            pz = bE
            for h in range(4):
                P.pe.matmul(out=pz[:, h * 128:(h + 1) * 128], lhsT=Y[:, h, :], rhs=Z[:, h, :], start=True, stop=True)
            P.act.activation(out=Zn[:, :, :], in_=pz[:, :512].rearrange("p (h j) -> p h j", h=4), func=AF.Copy)
            if m < 5:
                Yn = Yr.next()
                py = bET
                for h in range(4):
                    P.pe.matmul(out=py[:, h * 128:(h + 1) * 128], lhsT=Z[:, h, :], rhs=Y[:, h, :], start=True, stop=True)
                P.dve.tensor_copy(out=Yn[:, :, :], in_=py[:, :512].rearrange("p (h j) -> p h j", h=4))
            pp = bKK
            for h in range(4):
                P.pe.matmul(out=pp[:, h * 128:(h + 1) * 128], lhsT=Zn[:, h, :], rhs=Pm[:, h, :], start=True, stop=True)
            Pn = Pr.next()
            P.dve.tensor_tensor(out=Pn[:, :, :], in0=pp[:, :512].rearrange("p (h j) -> p h j", h=4), in1=Pm[:, :, :], op=ALU.add)
            Z, Pm = Zn, Pn
            if m < 5:
                Y = Yn
        TT = TTr.next()
        P.act.activation(out=TT[:, :, :], in_=Pm[:, :, :], func=AF.Copy)
        bv, kdc = bvr.next(), kdr.next()
        for h in range(4):
            P.dve.tensor_scalar(out=bv[:, h, :], in0=kv_tm[h][:, t, 128:256], scalar1=beta[:, t, h:h + 1], scalar2=None, op0=ALU.mult)
            P.dve.tensor_scalar(out=kdc[:, h, :], in0=kv_tm[h][:, t, 0:128], scalar1=sm[:, 4 + h:5 + h], scalar2=None, op0=ALU.mult)
        ot = otr.next()
        for c in range(2):
            r = slice(c * 64, (c + 1) * 64)
            tr = slice(t * 128 + c * 64, t * 128 + (c + 1) * 64)
            pks, pqs = mm_banks.next(), mm_banks.next()
            for h in range(4):
                P.pe.matmul(out=pks[r, h * 128:(h + 1) * 128], lhsT=kn[h][:, tr], rhs=Sb_cur[:, h, :], start=True, stop=True)
                P.pe.matmul(out=pqs[r, h * 128:(h + 1) * 128], lhsT=qn[h][:, tr], rhs=Sb_cur[:, h, :], start=True, stop=True)
            Rt = Rr.next()
            rf = rfr.next()
            P.dve.tensor_tensor(out=rf[r, :, :], in0=pks[r, :512].rearrange("p (h j) -> p h j", h=4), in1=bc3(sm[r, 16:20], 4, 128), op=ALU.mult)
            P.dve.tensor_tensor(out=Rt[r, :, :], in0=rf[r, :, :], in1=bv[r, :, :], op=ALU.add)
            pvn = bKQ
            for h in range(4):
                P.pe.matmul(out=pvn[r, h * 128:(h + 1) * 128], lhsT=TT[r, h, c * 64:(c + 1) * 64], rhs=Rt[r, h, :], start=True, stop=True)
            vn = vnr.next()
            P.act.activation(out=vn[r, :, :], in_=pvn[r, :512].rearrange("p (h j) -> p h j", h=4), func=AF.Copy)
            poi = bE
            for h in range(4):
                P.pe.matmul(out=poi[r, h * 128:(h + 1) * 128], lhsT=qk[r, h, c * 64:(c + 1) * 64], rhs=vn[r, h, :], start=True, stop=True)
            P.dve.tensor_tensor(out=rf[r, :, :], in0=pqs[r, :512].rearrange("p (h j) -> p h j", h=4), in1=bc3(sm[r, 0:4], 4, 128), op=ALU.mult)
            P.dve.tensor_tensor(out=ot[r, :, :], in0=poi[r, :512].rearrange("p (h j) -> p h j", h=4), in1=rf[r, :, :], op=ALU.add)
            pst = bET
            for h in range(4):
                P.pe.matmul(out=pst[:, h * 128:(h + 1) * 128], lhsT=kdc[r, h, :], rhs=vn[r, h, :], start=True, stop=True)
            cd = sm[:, 8 + 4 * c:12 + 4 * c]
            P.dve.tensor_tensor(out=S[:, :, :], in0=S[:, :, :], in1=bc3(cd, 4, 128), op=ALU.mult)
            P.dve.tensor_tensor(out=S[:, :, :], in0=pst[:, :512].rearrange("p (h j) -> p h j", h=4), in1=S[:, :, :], op=ALU.add)
            Sb_cur = Sbr.next()
            P.act.activation(out=Sb_cur[:, :, :], in_=S[:, :, :], func=AF.Copy)
        outs.append(head_norm_out(P, cx, ot[:, :, :], sg_tm[:, t, :], wn[:, :], io["omix"][t * 128:(t + 1) * 128, 512:1024], 4, rings))
    P.release(mark)
    outs += partial_proj(P, cx, io["omix"], 1024, io["w_out"], io["part"])
    return outs


def make_consts():
    c = np.zeros((128, NCONST), np.float32)
    c[:, 0:128] = np.eye(128)
    t = np.arange(128)[:, None] % 64
    i = np.arange(64)[None, :]
    c[:, 128:192] = (t <= i)
    c[:, 192:256] = (t > i)
    c[:, 256:320] = (t < i)
    c[:, 320:448] = 1.0
    return c


def make_c2():
    c = np.zeros((128, NC2), np.float32)
    t = np.arange(128)[:, None]; i = np.arange(128)[None, :]
    same = (t // 64) == (i // 64)
    c[:, 0:128] = (t <= i) & same
    c[:, 128:256] = (t > i) & same
    c[:, 256:384] = (t < 64)
    c[:, 384:512] = (t >= 64)
    return c


def ssd_host(d, half):
    w = d["ssd_w_in"][0]
    cols = []
    for g in range(4):
        gg = half * 4 + g
        cols += list(range(4096 + gg * 512, 4096 + (gg + 1) * 512))
        cols += list(range(gg * 512, (gg + 1) * 512))
        cols += list(range(8192 + gg * 128, 8192 + (gg + 1) * 128))
        cols += list(range(9216 + gg * 128, 9216 + (gg + 1) * 128))
        cols += list(range(10240 + gg * 8, 10240 + (gg + 1) * 8))
    w_in = np.ascontiguousarray(w[:, cols])
    cw = d["ssd_conv_w"][0]; cb = d["ssd_conv_b"][0]
    xs = slice(half * 2048, (half + 1) * 2048)
    cwx = cw[:, xs].T.reshape(16, 128, 4).transpose(1, 0, 2)
    cbx = cb[xs].reshape(16, 128).T
    bcols = np.concatenate([np.arange(4096 + (half * 4 + g) * 128, 4096 + (half * 4 + g + 1) * 128) for g in range(4)] +
                           [np.arange(5120 + (half * 4 + g) * 128, 5120 + (half * 4 + g + 1) * 128) for g in range(4)])
    cwbc = cw[:, bcols].T.reshape(8, 128, 4).transpose(1, 0, 2)
    cbbc = cb[bcols].reshape(8, 128).T
    hs = slice(half * 32, (half + 1) * 32)
    m = dict(w_in=w_in, cwx=cwx, cbx=cbx, cwbc=cwbc, cbbc=cbbc, dtb=d["ssd_dt_bias"][0][hs][None], alog=d["ssd_a_log"][0][hs][None],
             dfull=np.repeat(d["ssd_d"][0][hs], 64)[None], vec_mix=d["norm_mix"][1][None], c2=make_c2(), consts=make_consts())
    return {k: np.ascontiguousarray(v, dtype=np.float32) for k, v in m.items()}


def hy_host(d, half):
    w = d["hy_w_in"][0]
    hs = [half * 4 + h for h in range(4)]
    cols = []
    for h in hs:
        cols += list(range(h * 128, (h + 1) * 128)) + list(range(1024 + h * 128, 1024 + (h + 1) * 128))
    cols += list(range(2048 + hs[0] * 128, 2048 + (hs[-1] + 1) * 128))
    cols += list(range(3072 + hs[0] * 128, 3072 + (hs[-1] + 1) * 128))
    for h in hs:
        cols += list(range(4096 + h * 128, 4096 + (h + 1) * 128)) + list(range(5120 + h * 128, 5120 + (h + 1) * 128)) + \
                list(range(6144 + h * 128, 6144 + (h + 1) * 128))
    cols += list(range(7168 + hs[0] * 128, 7168 + (hs[-1] + 1) * 128))
    cols += [8192 + h for h in hs] + [8200 + h for h in hs]
    assert len(cols) == 4104
    w_in = w[:, cols]
    lg = d["hgrn_lb_logits"]
    lbl = np.concatenate([lg[0, hs[0] * 128:(hs[-1] + 1) * 128].reshape(4, 128).T, lg[1, hs[0] * 128:(hs[-1] + 1) * 128].reshape(4, 128).T], axis=1)
    cw = d["gdn_conv_w"][0]
    ccols = []
    for h in hs:
        ccols += list(range(h * 128, (h + 1) * 128)) + list(range(1024 + h * 128, 1024 + (h + 1) * 128)) + list(range(2048 + h * 128, 2048 + (h + 1) * 128))
    gcw = cw[:, ccols].T.reshape(12, 128, 4).transpose(1, 0, 2)
    gdn_p = np.concatenate([d["gdn_a_log"][0][hs], d["gdn_dt_bias"][0][hs]])[None]
    sl = slice(hs[0] * 128, (hs[-1] + 1) * 128)
    m = dict(w_in=w_in, lbl=lbl, gcw=gcw, gdn_p=gdn_p, hgrn_norm=d["hgrn_norm"][0][sl][None], gdn_norm=d["gdn_norm"][0][sl][None],
             vec_mix=d["norm_mix"][0][None], c2=make_c2(), consts=make_consts())
    return {k: np.ascontiguousarray(v, dtype=np.float32) for k, v in m.items()}


GROUPS = [[0, 1], [2, 3], [4, 5], [6, 7]]
NCU = 8

EXT_INPUTS = [
    ("consts", [128, NCONST]), ("c2", [128, NC2]), ("xfull", [SEQ, D]), ("x_own", [NT, D]), ("mem", [MEM, D]),
    ("w_in_hy", [D, 4104]), ("vec_mix0", [1, D]), ("lbl", [128, 8]), ("gcw", [128, 12, 4]), ("gdn_p", [1, 8]),
    ("hgrn_norm", [1, 512]), ("gdn_norm", [1, 512]), ("w_out0", [1024, D]),
    ("w_in_ssd", [D, 4 * 1288]), ("cwx", [128, 16, 4]), ("cbx", [128, 16]), ("cwbc", [128, 8, 4]), ("cbbc", [128, 8]),
    ("dtb", [1, 32]), ("alog", [1, 32]), ("dfull", [1, 2048]), ("w_out1", [2048, D]), ("vec_ssdnorm", [1, 2048]), ("vec_mix1", [1, D]),
    ("vec_final", [1, D]),
]
for _l in range(2):
    EXT_INPUTS += [(f"wq{_l}", [D, 512]), (f"wk{_l}", [D, 512]), (f"wv{_l}", [D, 512]), (f"wo{_l}", [512, D]),
                   (f"w_gate{_l}", [D, DFF]), (f"w_up{_l}", [D, DFF]), (f"w_down{_l}", [DFF, D]),
                   (f"vec_xattn{_l}", [1, D]), (f"vec_mem{_l}", [1, D]), (f"vec_ffn{_l}", [1, D])]


def build_all(upto=9):
    nc = bass.Bass("TRN2", target_bir_lowering=False)
    P = Prog(nc, same_engine_sync=SES)
    t = {}
    for nm, shp in EXT_INPUTS:
        t[nm] = P.dram(nm, shp, F32, "ExternalInput")
    t["out"] = P.dram("out", [NT, D], F32, "ExternalOutput")
    for nm, shp, dt in [("omix", [SEQ, 1024], F32), ("part0", [SEQ, D], F32), ("rs0", [NT, D], F32), ("h1_own", [NT, D], F32),
                        ("yg", [SEQ, 2048], F32),
                        ("part1", [SEQ, 2064], F32), ("rs1", [NT, 2064], F32)]:
        t[nm] = P.dram(nm, shp, dt, "Internal")
    for k_ in range(4):
        t[f"u_half{k_}"] = P.dram(f"u_half{k_}", [256, D], BF16, "Internal")
        t[f"u_full{k_}"] = P.dram(f"u_full{k_}", [512, D], BF16, "Internal")
    u_half = [t[f"u_half{k_}"] for k_ in range(4)]
    u_full = [t[f"u_full{k_}"] for k_ in range(4)]
    cx = setup_ctx(P, t["consts"])
    hy_io = dict(c2=t["c2"], xfull=t["xfull"], w_in=t["w_in_hy"], vec_mix=t["vec_mix0"], lbl=t["lbl"], gcw=t["gcw"], gdn_p=t["gdn_p"],
                 hgrn_norm=t["hgrn_norm"], gdn_norm=t["gdn_norm"], omix=t["omix"], w_out=t["w_out0"], part=t["part0"])
    if upto != 4:
        stage_hy(P, cx, hy_io)
    if upto == 5:
        stage_hy(P, cx, hy_io)
    P.collective("ReduceScatter", ALU.add, GROUPS[:NCU // 2], t["part0"].ap, t["rs0"].ap)
    if upto in (1, 5):
        o = P.dma("sp", out=t["out"][:, :], in_=t["rs0"][:, :])
        P.emit(final_wait_ops=[o])
        return nc

    def b_io(l):
        d_ = dict(mem=t["mem"], wq=t[f"wq{l}"], wk=t[f"wk{l}"], wv=t[f"wv{l}"], wo=t[f"wo{l}"], w_gate=t[f"w_gate{l}"], w_up=t[f"w_up{l}"],
                  w_down=t[f"w_down{l}"], vec_xattn=t[f"vec_xattn{l}"], vec_mem=t[f"vec_mem{l}"], vec_ffn=t[f"vec_ffn{l}"],
                  vec_final=t["vec_final"], vec_mix_next=t["vec_mix1"])
        return d_
    io0 = b_io(0)
    io0.update(hres=t["x_own"], rs=t["rs0"], hout=t["h1_own"], u_half=u_half)
    stage_b(P, cx, io0, 0, False)
    if upto in (3, 4):
        o = P.dma("sp", out=t["out"][:, :], in_=t["h1_own"][:, :])
        P.emit(final_wait_ops=[o])
        return nc
    for k_ in range(4):
        P.collective("AllGather", ALU.bypass, GROUPS[:NCU // 2], u_half[k_].ap, u_full[k_].ap)
    if upto == 2:
        o = P.dma("sp", out=t["out"][:, :], in_=t["h1_own"][:, :])
        P.emit(final_wait_ops=[o])
        return nc
    ssd_io = dict(c2=t["c2"], u_full=u_full, w_in=t["w_in_ssd"], cwx=t["cwx"], cbx=t["cbx"], cwbc=t["cwbc"], cbbc=t["cbbc"], dtb=t["dtb"],
                  alog=t["alog"], dfull=t["dfull"], yg=t["yg"], w_out=t["w_out1"], vec_ssdnorm=t["vec_ssdnorm"], part=t["part1"])
    stage_ssd(P, cx, ssd_io)
    P.collective("ReduceScatter", ALU.add, GROUPS[:NCU // 2], t["part1"].ap, t["rs1"].ap)
    io1 = b_io(1)
    io1.update(hres=t["h1_own"], rs=t["rs1"], hout=t["out"])
    outs = stage_b(P, cx, io1, 1, True)
    P.emit(final_wait_ops=outs)
    print("ops", len(P.ops), "waits", P.n_waits, "sb_peak", P.sb_peak)
    return nc


_CACHE = {}
UPTO = 9
TRACE = False
SES = True


def _f32(a):
    return np.ascontiguousarray(a, dtype=np.float32)


def kernel(**inputs):
    d = {k: np.asarray(v) for k, v in inputs.items()}
    x = d["x"]
    B = x.shape[0]
    cores = list(range(NCU))
    if "nc" not in _CACHE:
        _CACHE["nc"] = build_all(UPTO)
    nc = _CACHE["nc"]
    shared = dict(consts=make_consts(), c2=make_c2(), vec_mix0=_f32(d["norm_mix"][0][None]), vec_mix1=_f32(d["norm_mix"][1][None]),
                  vec_final=_f32(d["norm_final"][None]))
    for l in range(2):
        shared.update({f"wq{l}": _f32(d["xa_wq"][l]), f"wk{l}": _f32(d["xa_wk"][l]), f"wv{l}": _f32(d["xa_wv"][l]), f"wo{l}": _f32(d["xa_wo"][l]),
                       f"w_gate{l}": _f32(d["ffn_w_gate"][l]), f"w_up{l}": _f32(d["ffn_w_up"][l]), f"w_down{l}": _f32(d["ffn_w_down"][l]),
                       f"vec_xattn{l}": _f32(d["norm_xattn"][l][None]), f"vec_mem{l}": _f32(d["norm_mem"][l][None]),
                       f"vec_ffn{l}": _f32(d["norm_ffn"][l][None])})
    per_half = []
    for half in range(2):
        hh = hy_host(d, half)
        sh = ssd_host(d, half)
        m = dict(w_in_hy=hh["w_in"], lbl=hh["lbl"], gcw=hh["gcw"], gdn_p=hh["gdn_p"], hgrn_norm=hh["hgrn_norm"], gdn_norm=hh["gdn_norm"],
                 w_out0=_f32(np.concatenate([d["hy_w_out"][0][half * 512:(half + 1) * 512], d["hy_w_out"][0][1024 + half * 512:1024 + (half + 1) * 512]], axis=0)),
                 w_in_ssd=sh["w_in"], cwx=sh["cwx"], cbx=sh["cbx"], cwbc=sh["cwbc"], cbbc=sh["cbbc"], dtb=sh["dtb"], alog=sh["alog"], dfull=sh["dfull"],
                 w_out1=_f32(d["ssd_w_out"][0][half * 2048:(half + 1) * 2048]), vec_ssdnorm=_f32(d["ssd_norm"][0][half * 2048:(half + 1) * 2048][None]))
        per_half.append(m)
    maps = []
    for c in cores:
        b, half = c // 2, c % 2
        m = dict(shared)
        m.update(per_half[half])
        m["xfull"] = _f32(x[b])
        m["x_own"] = _f32(x[b, half * NT:(half + 1) * NT])
        m["mem"] = _f32(d["mem"][b])
        maps.append(m)
    res = run_bass_kernel_spmd(nc, maps, core_ids=cores, **({'trace': True} if TRACE else {}))
    if TRACE:
        print('EXEC_NS', getattr(res, 'exec_time_ns', None))
    out = np.empty((B, SEQ, D), np.float32)
    for c in cores:
        b, half = c // 2, c % 2
        out[b, half * NT:(half + 1) * NT] = res.results[c]["out"]
    return out
```

```python
import numpy as np
import concourse.bass as bass
import concourse.mybir as mybir
from concourse.bass_utils import run_bass_kernel_spmd

F32 = mybir.dt.float32
BF16 = mybir.dt.bfloat16
AF = mybir.ActivationFunctionType
ALU = mybir.AluOpType
AX = mybir.AxisListType

_DT_SIZE = {F32: 4, BF16: 2}


class T:
    def __init__(self, base, space, lo, hi, name):
        self.h = base
        self.space = space
        self.lo = lo
        self.hi = hi
        self.name = name

    def __getitem__(self, idx):
        v = V(self.h[idx], self)
        if self.space == "dram":
            i0 = idx[0] if isinstance(idx, tuple) else idx
            if isinstance(i0, slice) and (i0.start is None or isinstance(i0.start, int)) and (i0.stop is None or isinstance(i0.stop, int)) \
                    and i0.step in (None, 1):
                r0 = 0 if i0.start is None else i0.start
                r1 = (self.hi - self.lo) if i0.stop is None else i0.stop
                v.lo, v.hi = self.lo + r0, self.lo + r1
        return v

    @property
    def ap(self):
        return V(self.h, self)


class V:
    def __init__(self, ap, t, lo=None, hi=None):
        self.ap = ap
        self.t = t
        self.lo = t.lo if lo is None else lo
        self.hi = t.hi if hi is None else hi

    def __getitem__(self, idx):
        return V(self.ap[idx], self.t, self.lo, self.hi)

    def __getattr__(self, name):
        if name in ("ap", "t", "lo", "hi"):
            raise AttributeError(name)
        a = getattr(self.ap, name)
        if callable(a):
            def f(*args, **kw):
                r = a(*args, **kw)
                if isinstance(r, bass.AP):
                    return V(r, self.t, self.lo, self.hi)
                return r
            return f
        return a


class CCAP:
    def __init__(self, v):
        self.v = v


class Ring:
    def __init__(self, tiles, prog):
        self.tiles = tiles
        self.i = 0
        self.p = prog

    def next(self):
        t = self.tiles[self.i % len(self.tiles)]
        self.i += 1
        return t

    def free(self):
        self.p.free(*self.tiles)


class Op:
    __slots__ = ("eng", "name", "args", "kw", "deps", "is_dma", "is_cc", "sem", "val", "has_dep", "idx", "pre")

    def __init__(self, eng, name, args, kw):
        self.eng = eng
        self.name = name
        self.args = args
        self.kw = kw
        self.deps = []
        self.is_dma = name == "dma_start"
        self.is_cc = name == "collective_compute"
        self.sem = None
        self.val = None
        self.has_dep = False
        self.pre = []


class EngProxy:
    def __init__(self, prog, eng):
        self.p = prog
        self.e = eng

    def __getattr__(self, name):
        def f(*args, **kw):
            return self.p.record(self.e, name, args, kw)
        return f


WRITE_KEYS = ("out", "accum_out")
CC_BLOCK = True


class Prog:
    ENGS = ("pe", "act", "dve", "pool", "sp")

    def __init__(self, nc, same_engine_sync=True, n_dma_sems=(12, 6, 6)):
        self.nc = nc
        self.ops = []
        self.same_engine_sync = same_engine_sync
        self.recs = {"sb": [], "ps": [], "dram": []}
        self.pe = EngProxy(self, "pe")
        self.act = EngProxy(self, "act")
        self.dve = EngProxy(self, "dve")
        self.pool = EngProxy(self, "pool")
        self.sp = EngProxy(self, "sp")
        self.sb_lo = (nc.sbuf_base + 63) // 64 * 64
        self.sb_hi = nc.sbuf_top // 64 * 64
        self.sb_free = [(self.sb_lo, self.sb_hi)]
        self.sb_peak = 0
        self.n_tiles = 0
        self.psum = []
        self.dram_next = 0
        self.live = []
        self.last_cc = None
        self.n_dma_sems = dict(zip(("sp", "act", "pool"), n_dma_sems))

    def tile(self, shape, dtype=F32, name=None):
        nbytes = int(np.prod(shape[1:])) * _DT_SIZE[dtype]
        nbytes_al = (nbytes + 31) // 32 * 32
        for i, (lo, hi) in enumerate(self.sb_free):
            if hi - lo >= nbytes_al:
                if hi - lo == nbytes_al:
                    self.sb_free.pop(i)
                else:
                    self.sb_free[i] = (lo + nbytes_al, hi)
                self.n_tiles += 1
                nm = f"{name or 't'}_{self.n_tiles}"
                h = self.nc.alloc_sbuf_tensor_at(nm, list(shape), dtype, offset=lo)
                self.sb_peak = max(self.sb_peak, lo + nbytes_al)
                t = T(h[:], "sb", lo, lo + nbytes_al, nm)
                self.live.append(t)
                return t
        raise RuntimeError(f"SBUF arena full allocating {shape} {name}; free={self.sb_free}")

    def mark(self):
        return list(self.live)

    def release(self, mark):
        keep = set(id(t) for t in mark)
        self.free(*[t for t in self.live if id(t) not in keep])

    def free(self, *tiles):
        for t in tiles:
            if not any(t is x for x in self.live):
                continue
            self.live = [x for x in self.live if x is not t]
            self.sb_free.append((t.lo, t.hi))
        self.sb_free.sort()
        merged = []
        for lo, hi in self.sb_free:
            if merged and merged[-1][1] == lo:
                merged[-1] = (merged[-1][0], hi)
            else:
                merged.append((lo, hi))
        self.sb_free = merged

    def psum_init(self):
        self.ps_h = self.nc.alloc_psum_tensor("psall", [128, 4096], F32)

    def bank(self, b0, nb=1):
        return T(self.ps_h[:, b0 * 512:(b0 + nb) * 512], "ps", b0 * 2048, (b0 + nb) * 2048, f"ps{b0}_{nb}")

    def psum_slice(self, b, col0, ncols):
        c0 = b * 512 + col0
        return T(self.ps_h[:, c0:c0 + ncols], "ps", c0 * 4, (c0 + ncols) * 4, f"pss{b}_{col0}")

    def bank_bf(self, b0, nb=1):
        return T(self.ps_h[:, b0 * 512:(b0 + nb) * 512].bitcast(BF16), "ps", b0 * 2048, (b0 + nb) * 2048, f"psb{b0}_{nb}")

    def ring(self, shape, dtype, n, name=None):
        return Ring([self.tile(shape, dtype, name) for _ in range(n)], self)

    def dram(self, name, shape, dtype, kind):
        if kind == "Internal":
            h = self.nc.dram_tensor(name, list(shape), dtype)
        else:
            h = self.nc.dram_tensor(name, list(shape), dtype, kind=kind)
        base = self.dram_next
        self.dram_next += int(shape[0]) + 8
        return T(h.ap(), "dram", base, base + int(shape[0]), name)

    def _access(self, op, t, lo, hi, is_write):
        recs = self.recs[t.space]
        keep = []
        for r in recs:
            rlo, rhi, rop, rw = r
            if rlo < hi and lo < rhi:
                if (is_write or rw) and rop is not op:
                    op.deps.append(rop)
                if is_write and rlo >= lo and rhi <= hi:
                    continue
                if (not is_write) and (not rw) and rop.eng == op.eng and not rop.is_dma and not op.is_dma and not rop.is_cc \
                        and rlo >= lo and rhi <= hi:
                    continue
            keep.append(r)
        keep.append((lo, hi, op, is_write))
        self.recs[t.space] = keep

    def record(self, eng, name, args, kw, extra_reads=(), extra_writes=()):
        op = Op(eng, name, args, kw)
        reads, writes = [], []
        for k, v in kw.items():
            if isinstance(v, V):
                (writes if k in WRITE_KEYS else reads).append(v)
        for i, v in enumerate(args):
            if isinstance(v, V):
                (writes if i == 0 else reads).append(v)
        reads += list(extra_reads)
        writes += list(extra_writes)
        for v in reads:
            t = v.t if isinstance(v, V) else v
            self._access(op, t, v.lo, v.hi, False)
        for v in writes:
            t = v.t if isinstance(v, V) else v
            self._access(op, t, v.lo, v.hi, True)
        if op.is_dma and self.last_cc is not None:
            op.deps.append(self.last_cc)
        if op.is_cc:
            self.last_cc = op
        op.idx = len(self.ops)
        self.ops.append(op)
        return op

    def collective(self, kind, op, groups, src, dst):
        return self.record("pool", "collective_compute", (kind, op), dict(replica_groups=groups, ins=[CCAP(src)], outs=[CCAP(dst)]),
                           extra_reads=[src], extra_writes=[dst])

    def dma(self, q, out, in_, **kw):
        return self.record(q, "dma_start", (), dict(out=out, in_=in_, **kw))

    def emit(self, final_wait_ops=()):
        nc = self.nc
        ops = self.ops
        for op in ops:
            seen = set()
            d2 = []
            for d in op.deps:
                if id(d) in seen:
                    continue
                seen.add(id(d))
                if d.eng == op.eng and not d.is_dma and not d.is_cc:
                    if d.eng == "pe" or not self.same_engine_sync:
                        continue
                d2.append(d)
            op.deps = d2
            for d in d2:
                d.has_dep = True
        for op in final_wait_ops:
            op.has_dep = True
        eng_sem = {}
        eng_cnt = {}
        SEM_MAX = 30000

        def new_sem(nm):
            return nc.alloc_semaphore(nm)

        for e in ("pe", "act", "dve", "pool"):
            eng_sem[e] = new_sem(f"c_{e}_0")
            eng_cnt[e] = 0
        n_epoch = 0
        dma_sems = {q: [[new_sem(f"d_{q}_{i}"), 0, None] for i in range(n)] for q, n in self.n_dma_sems.items()}
        dma_rr = {q: 0 for q in dma_sems}
        for op in ops:
            if op.is_cc:
                op.sem, op.val = new_sem(f"cc_{op.idx}"), 1
                for q_, pool_ in dma_sems.items():
                    for sl_ in pool_:
                        if sl_[2] is not None:
                            op.pre.append((sl_[0], sl_[1]))
            elif op.is_dma:
                pool = dma_sems[op.eng]
                slot = pool[dma_rr[op.eng] % len(pool)]
                dma_rr[op.eng] += 1
                if slot[2] is not None:
                    op.pre.append((slot[0], slot[1]))
                slot[1] += 16
                slot[2] = op
                op.sem, op.val = slot[0], slot[1]
            elif op.has_dep:
                e = op.eng
                if eng_cnt[e] >= SEM_MAX:
                    n_epoch += 1
                    eng_sem[e] = new_sem(f"c_{e}_{n_epoch}")
                    eng_cnt[e] = 0
                eng_cnt[e] += 1
                op.sem, op.val = eng_sem[e], eng_cnt[e]
        self.eng_cnt_final = dict(eng_cnt)
        print("sem counts", eng_cnt, "epochs", n_epoch, "dma", {q: [x[1] for x in v] for q, v in dma_sems.items()})
        per_eng = {e: [] for e in self.ENGS}
        for op in ops:
            per_eng[op.eng].append(op)
        handles = {"pe": "tensor", "act": "scalar", "dve": "vector", "pool": "gpsimd", "sp": "sync"}
        print("per-engine instr", {e: len(v) for e, v in per_eng.items()})
        self.n_waits = 0

        def unwrap(x):
            if isinstance(x, V):
                return x.ap
            if isinstance(x, CCAP):
                return x.v.ap.opt()
            if isinstance(x, list):
                return [unwrap(y) for y in x]
            return x

        def emit_engine(e, eng):
            waited = {}
            for op in per_eng[e]:
                need = {}
                for (s, v) in op.pre:
                    need[id(s)] = (s, max(v, need.get(id(s), (s, 0))[1]))
                for d in op.deps:
                    k = id(d.sem)
                    if k not in need or need[k][1] < d.val:
                        need[k] = (d.sem, d.val)
                for k, (s, v) in need.items():
                    if waited.get(k, 0) >= v:
                        continue
                    eng.wait_ge(s, v)
                    self.n_waits += 1
                    waited[k] = v
                args = [unwrap(a) for a in op.args]
                kw = {k: unwrap(v) for k, v in op.kw.items()}
                ins = getattr(eng, op.name)(*args, **kw)
                if op.sem is not None:
                    ins.then_inc(op.sem, 16 if op.is_dma else 1)
                if op.is_cc and CC_BLOCK:
                    eng.wait_ge(op.sem, op.val)
            if e in final_eng:
                for op in final_wait_ops:
                    eng.wait_ge(op.sem, op.val)

        final_eng = {"sp"}
        with nc.Block() as block:
            for e in self.ENGS:
                fn = getattr(block, handles[e])
                fn(lambda eng, e=e: emit_engine(e, eng))
        return nc


D = 2048
NT = 1024
NTT = NT // 128
EPS = 1e-6
DFF = 5632
MEM = 256
NCONST = 512


class Ctx:
    pass


def setup_ctx(P, cdram):
    cx = Ctx()
    cx.P = P
    P.psum_init()
    cx.consts = P.tile([128, NCONST], F32, "consts")
    P.dma("sp", out=cx.consts[:], in_=cdram[:])
    cx.identf = cx.consts[:, 0:128]
    cx.identb_t = P.tile([128, 128], BF16, "identb")
    P.dve.tensor_copy(out=cx.identb_t[:], in_=cx.consts[:, 0:128])
    cx.identb = cx.identb_t[:]
    cx.ones_f = cx.consts[:, 320:448]
    cx.small = P.ring([128, 8], F32, 8, "small")
    cx.small32 = P.ring([128, 32], F32, 6, "small32")
    return cx


def bcast_load(P, dram_row, K, name):
    t = P.tile([128, K], F32, name)
    P.dma("sp", out=t[:], in_=dram_row[:, :].to_broadcast([128, K]))
    return t


def dma_w(P, out_tile, w_view, KC, ncols, q="pool"):
    step = 16
    for k0 in range(0, KC, step):
        k1 = min(KC, k0 + step)
        P.dma(q, out=out_tile[:, k0:k1, :ncols],
              in_=w_view[k0 * 128:k1 * 128, :].rearrange("(c p) n -> p c n", p=128))


def norm_T(cx, srcs, K, wbc, dstT, tok0, xn_ring, tp_banks, do_norm=True):
    P = cx.P
    KC = K // 128
    for i, src in enumerate(srcs):
        xn = xn_ring.next()
        if do_norm:
            ss = cx.small.next()
            P.act.activation(out=xn[:, :K], in_=src, func=AF.Square, accum_out=ss[:, 0:1])
            P.dve.tensor_scalar(out=ss[:, 1:2], in0=ss[:, 0:1], scalar1=1.0 / K, scalar2=EPS, op0=ALU.mult, op1=ALU.add)
            P.act.activation(out=ss[:, 2:3], in_=ss[:, 1:2], func=AF.Ln)
            P.act.activation(out=ss[:, 2:3], in_=ss[:, 2:3], func=AF.Exp, scale=-0.5)
            P.dve.scalar_tensor_tensor(out=xn[:, :K], in0=src, scalar=ss[:, 2:3], in1=wbc, op0=ALU.mult, op1=ALU.mult)
        else:
            P.act.activation(out=xn[:, :K], in_=src, func=AF.Copy)
        for j, c0 in enumerate(range(0, KC, 8)):
            pb = tp_banks.next()
            n = min(8, KC - c0)
            for c in range(n):
                P.pe.transpose(out=pb[:, c * 128:(c + 1) * 128], in_=xn[:, (c0 + c) * 128:(c0 + c + 1) * 128],
                               identity=cx.identb)
            o = dstT[:, c0:c0 + n, tok0 + i * 128:tok0 + (i + 1) * 128]
            s = pb[:, :n * 128].rearrange("p (c t) -> p c t", c=n)
            if j % 2 == 0:
                P.dve.tensor_copy(out=o, in_=s)
            else:
                P.act.activation(out=o, in_=s, func=AF.Copy)


def stage_b(P, cx, io, layer_kind, final):
    mark = P.mark()
    tp_banks = Ring([P.bank_bf(0), P.bank_bf(1)], P)
    mm_banks = Ring([P.bank(2), P.bank(3), P.bank(4), P.bank(5)], P)
    RW = 2048 if layer_kind == 0 else 2064
    h = [P.tile([128, D], F32, f"h{t}") for t in range(NTT)]
    rsr = P.ring([128, RW], F32, 2, "rs")
    for t in range(NTT):
        P.dma("sp", out=h[t][:], in_=io["hres"][t * 128:(t + 1) * 128, :])
        rs = rsr.next()
        P.dma("sp", out=rs[:], in_=io["rs"][t * 128:(t + 1) * 128, :])
        if layer_kind == 0:
            P.dve.tensor_tensor(out=h[t][:], in0=rs[:, 0:D], in1=h[t][:], op=ALU.add)
        else:
            ss = cx.small.next()
            P.dve.tensor_scalar(out=ss[:, 1:2], in0=rs[:, D:D + 1], scalar1=1.0 / 4096, scalar2=EPS, op0=ALU.mult, op1=ALU.add)
            P.act.activation(out=ss[:, 2:3], in_=ss[:, 1:2], func=AF.Ln)
            P.act.activation(out=ss[:, 2:3], in_=ss[:, 2:3], func=AF.Exp, scale=-0.5)
            P.dve.scalar_tensor_tensor(out=h[t][:], in0=rs[:, 0:D], scalar=ss[:, 2:3], in1=h[t][:], op0=ALU.mult, op1=ALU.add)
    rsr.free()
    xn_ring = P.ring([128, D], BF16, 2, "xn2")
    hnT = P.tile([128, 16, NT], BF16, "hnT")
    wbc = bcast_load(P, io["vec_xattn"], D, "wbc_xa")
    norm_T(cx, [h[t][:] for t in range(NTT)], D, wbc[:], hnT, 0, xn_ring, tp_banks)
    mnT = P.tile([128, 16, MEM], BF16, "mnT")
    wbm = bcast_load(P, io["vec_mem"], D, "wbc_mem")
    ld = P.ring([128, D], F32, 2, "memld")
    for t in range(MEM // 128):
        lt = ld.next()
        P.dma("sp", out=lt[:], in_=io["mem"][t * 128:(t + 1) * 128, :])
        norm_T(cx, [lt[:]], D, wbm[:], mnT, t * 128, xn_ring, tp_banks)
    ld.free()
    P.free(wbc, wbm)
    wq = P.tile([128, 16, 512], BF16, "wq")
    wk = P.tile([128, 16, 512], BF16, "wk")
    wv = P.tile([128, 16, 512], BF16, "wv")
    dma_w(P, wk, io["wk"], 16, 512)
    dma_w(P, wv, io["wv"], 16, 512)
    dma_w(P, wq, io["wq"], 16, 512)
    kT = P.tile([128, 4, MEM], BF16, "kT")
    vtm = P.tile([128, 2, 512], BF16, "vtm")
    qT = P.tile([128, 4, NT], BF16, "qT")
    for hd in range(4):
        pb = mm_banks.next()
        for kc in range(16):
            P.pe.matmul(out=pb[:, :MEM], lhsT=wk[:, kc, hd * 128:(hd + 1) * 128], rhs=mnT[:, kc, :], start=(kc == 0), stop=(kc == 15))
        P.act.activation(out=kT[:, hd, :], in_=pb[:, :MEM], func=AF.Copy)
    for mt in range(2):
        pb = mm_banks.next()
        for kc in range(16):
            P.pe.matmul(out=pb[:, :512], lhsT=mnT[:, kc, mt * 128:(mt + 1) * 128], rhs=wv[:, kc, :], start=(kc == 0), stop=(kc == 15))
        P.act.activation(out=vtm[:, mt, :], in_=pb[:, :512], func=AF.Copy)
    for hd in range(4):
        for th in range(NT // 512):
            pb = mm_banks.next()
            for kc in range(16):
                P.pe.matmul(out=pb[:, :512], lhsT=wq[:, kc, hd * 128:(hd + 1) * 128], rhs=hnT[:, kc, th * 512:(th + 1) * 512],
                            start=(kc == 0), stop=(kc == 15))
            P.act.activation(out=qT[:, hd, th * 512:(th + 1) * 512], in_=pb[:, :512], func=AF.Copy)
    P.free(wq, wk, wv, mnT, hnT)
    oT = P.tile([128, 4, NT], BF16, "oT")
    sc = 128 ** -0.5
    shr = P.ring([128, 4, MEM], F32, 2, "xsh")
    pnr = P.ring([128, 4, MEM], BF16, 2, "xpn")
    pTr = P.ring([128, 8, 128], BF16, 2, "xpT")
    sc_banks = Ring([P.bank(4, 2), P.bank(6, 2)], P)
    ob_banks = Ring([P.bank(2), P.bank(3)], P)
    for t in range(NTT):
        psc = sc_banks.next()
        for hd in range(4):
            P.pe.matmul(out=psc[:, hd * MEM:(hd + 1) * MEM], lhsT=qT[:, hd, t * 128:(t + 1) * 128], rhs=kT[:, hd, :], start=True, stop=True)
        ps3 = psc[:, :].rearrange("p (h m) -> p h m", h=4)
        sm = cx.small.next()
        P.dve.tensor_reduce(out=sm[:, 0:4], in_=ps3, axis=AX.X, op=ALU.max)
        sh = shr.next()
        P.dve.tensor_tensor(out=sh[:, :, :], in0=ps3, in1=bc3(sm[:, 0:4], 4, MEM), op=ALU.subtract)
        P.act.activation(out=sh[:, :, :], in_=sh[:, :, :], func=AF.Exp, scale=sc)
        P.dve.tensor_reduce(out=sm[:, 4:8], in_=sh[:, :, :], axis=AX.X, op=ALU.add)
        sm2 = cx.small.next()
        P.dve.reciprocal(out=sm2[:, 0:4], in_=sm[:, 4:8])
        pn = pnr.next()
        P.dve.tensor_tensor(out=pn[:, :, :], in0=sh[:, :, :], in1=bc3(sm2[:, 0:4], 4, MEM), op=ALU.mult)
        tb = tp_banks.next()
        for hd in range(4):
            for mc in range(2):
                j = hd * 2 + mc
                P.pe.transpose(out=tb[:, j * 128:(j + 1) * 128], in_=pn[:, hd, mc * 128:(mc + 1) * 128], identity=cx.identb)
        pT = pTr.next()
        P.act.activation(out=pT[:, :, :], in_=tb[:, :1024].rearrange("p (c t) -> p c t", c=8), func=AF.Copy)
        ob = ob_banks.next()
        for hd in range(4):
            for mc in range(2):
                P.pe.matmul(out=ob[:, hd * 128:(hd + 1) * 128], lhsT=vtm[:, mc, hd * 128:(hd + 1) * 128], rhs=pT[:, hd * 2 + mc, :],
                            start=(mc == 0), stop=(mc == 1))
        P.act.activation(out=oT[:, :, t * 128:(t + 1) * 128], in_=ob[:, :512].rearrange("p (h s) -> p h s", h=4), func=AF.Copy)
    for r_ in (shr, pnr, pTr):
        r_.free()
    P.free(qT, kT, vtm)
    wo = P.tile([128, 4, D], BF16, "wo")
    for c in range(4):
        P.dma("pool", out=wo[:, c, :], in_=io["wo"][c * 128:(c + 1) * 128, :])
    for t in range(NTT):
        for cg in range(4):
            pb = mm_banks.next()
            for hd in range(4):
                P.pe.matmul(out=pb[:, :512], lhsT=oT[:, hd, t * 128:(t + 1) * 128], rhs=wo[:, hd, cg * 512:(cg + 1) * 512],
                            start=(hd == 0), stop=(hd == 3))
            P.dve.tensor_tensor(out=h[t][:, cg * 512:(cg + 1) * 512], in0=pb[:, :512], in1=h[t][:, cg * 512:(cg + 1) * 512], op=ALU.add)
    P.free(wo, oT)
    hnT = P.tile([128, 16, NT], BF16, "hn2T")
    wbc = bcast_load(P, io["vec_ffn"], D, "wbc_ffn")
    norm_T(cx, [h[t][:] for t in range(NTT)], D, wbc[:], hnT, 0, xn_ring, tp_banks)
    P.free(wbc)
    FB = 256
    ffn_banks = Ring([P.bank(b_) for b_ in (2, 3, 4, 5, 6, 7, 0, 1)], P)
    wgr = P.ring([128, 16, FB], BF16, 2, "wg")
    wur = P.ring([128, 16, FB], BF16, 2, "wu")
    wdr = P.ring([128, FB // 128, D], BF16, 2, "wd")
    actr = P.ring([128, FB // 128, NT], BF16, 2, "act")
    sgr = P.ring([128, 512], F32, 2, "sg")
    for fb in range(DFF // FB):
        wg = wgr.next()
        wu = wur.next()
        wd = wdr.next()
        dma_w(P, wg, io["w_gate"][:, fb * FB:(fb + 1) * FB], 16, FB)
        dma_w(P, wu, io["w_up"][:, fb * FB:(fb + 1) * FB], 16, FB)
        for s in range(FB // 128):
            P.dma("pool", out=wd[:, s, :], in_=io["w_down"][fb * FB + s * 128:fb * FB + (s + 1) * 128, :])
        act = actr.next()
        for s in range(FB // 128):
            for th in range(NT // 512):
                pg = ffn_banks.next()
                for kc in range(16):
                    P.pe.matmul(out=pg[:, :512], lhsT=wg[:, kc, s * 128:(s + 1) * 128], rhs=hnT[:, kc, th * 512:(th + 1) * 512],
                                start=(kc == 0), stop=(kc == 15))
                pu = ffn_banks.next()
                for kc in range(16):
                    P.pe.matmul(out=pu[:, :512], lhsT=wu[:, kc, s * 128:(s + 1) * 128], rhs=hnT[:, kc, th * 512:(th + 1) * 512],
                                start=(kc == 0), stop=(kc == 15))
                sg = sgr.next()
                P.act.activation(out=sg[:], in_=pg[:, :512], func=AF.Silu)
                P.dve.tensor_tensor(out=act[:, s, th * 512:(th + 1) * 512], in0=pu[:, :512], in1=sg[:], op=ALU.mult)
        for t in range(NTT):
            for cg in range(4):
                pb = ffn_banks.next()
                for s in range(FB // 128):
                    P.pe.matmul(out=pb[:, :512], lhsT=act[:, s, t * 128:(t + 1) * 128], rhs=wd[:, s, cg * 512:(cg + 1) * 512],
                                start=(s == 0), stop=(s == FB // 128 - 1))
                P.dve.tensor_tensor(out=h[t][:, cg * 512:(cg + 1) * 512], in0=pb[:, :512], in1=h[t][:, cg * 512:(cg + 1) * 512], op=ALU.add)
    for r in (wgr, wur, wdr, actr, sgr):
        r.free()
    P.free(hnT)
    outs = []
    if final:
        wbc = bcast_load(P, io["vec_final"], D, "wbc_fin")
        orr = P.ring([128, D], F32, 2, "fin")
        for t in range(NTT):
            ss = cx.small.next()
            ot = orr.next()
            P.act.activation(out=ot[:], in_=h[t][:], func=AF.Square, accum_out=ss[:, 0:1])
            P.dve.tensor_scalar(out=ss[:, 1:2], in0=ss[:, 0:1], scalar1=1.0 / D, scalar2=EPS, op0=ALU.mult, op1=ALU.add)
            P.act.activation(out=ss[:, 2:3], in_=ss[:, 1:2], func=AF.Ln)
            P.act.activation(out=ss[:, 2:3], in_=ss[:, 2:3], func=AF.Exp, scale=-0.5)
            P.dve.scalar_tensor_tensor(out=ot[:], in0=h[t][:], scalar=ss[:, 2:3], in1=wbc[:], op0=ALU.mult, op1=ALU.mult)
            outs.append(P.dma("sp", out=io["hout"][t * 128:(t + 1) * 128, :], in_=ot[:]))
        orr.free()
        P.free(wbc)
    else:
        wbc = bcast_load(P, io["vec_mix_next"], D, "wbc_nx")
        ur = P.ring([128, D], BF16, 2, "ub")
        for t in range(NTT):
            outs.append(P.dma("sp", out=io["hout"][t * 128:(t + 1) * 128, :], in_=h[t][:]))
            ss = cx.small.next()
            ub = ur.next()
            P.act.activation(out=ub[:], in_=h[t][:], func=AF.Square, accum_out=ss[:, 0:1])
            P.dve.tensor_scalar(out=ss[:, 1:2], in0=ss[:, 0:1], scalar1=1.0 / D, scalar2=EPS, op0=ALU.mult, op1=ALU.add)
            P.act.activation(out=ss[:, 2:3], in_=ss[:, 1:2], func=AF.Ln)
            P.act.activation(out=ss[:, 2:3], in_=ss[:, 2:3], func=AF.Exp, scale=-0.5)
            P.dve.scalar_tensor_tensor(out=ub[:], in0=h[t][:], scalar=ss[:, 2:3], in1=wbc[:], op0=ALU.mult, op1=ALU.mult)
            outs.append(P.dma("sp", out=io["u_half"][t // 2][(t % 2) * 128:(t % 2 + 1) * 128, :], in_=ub[:]))
    P.release(mark)
    return outs


SEQ = 2048
NTS = SEQ // 128
NC2 = 512


def softplus_inplace(P, x, tmp):
    P.dve.tensor_scalar(out=tmp, in0=x, scalar1=-1.0, scalar2=None, op0=ALU.mult)
    P.dve.tensor_tensor(out=tmp, in0=tmp, in1=x, op=ALU.max)
    P.act.activation(out=tmp, in_=tmp, func=AF.Exp, scale=-1.0)
    P.act.activation(out=tmp, in_=tmp, func=AF.Ln, bias=1.0, scale=1.0)
    P.dve.tensor_scalar(out=x, in0=x, scalar1=0.0, scalar2=None, op0=ALU.max)
    P.dve.tensor_tensor(out=x, in0=x, in1=tmp, op=ALU.add)


def conv_silu_fm(P, cx, w_t, uT, col0, xpad, acc, cw, cb, dst_bf, mm_banks, silu=True):
    for th in range(SEQ // 512):
        pb = mm_banks.next()
        for kc in range(16):
            P.pe.matmul(out=pb[:, :512], lhsT=w_t[:, kc, col0:col0 + 128], rhs=uT[:, kc, th * 512:(th + 1) * 512],
                        start=(kc == 0), stop=(kc == 15))
        P.act.activation(out=xpad[:, 3 + th * 512:3 + (th + 1) * 512], in_=pb[:, :512], func=AF.Copy)
    if cb is not None:
        P.dve.tensor_scalar(out=acc[:, :], in0=xpad[:, 3:3 + SEQ], scalar1=cw[:, 3:4], scalar2=cb, op0=ALU.mult, op1=ALU.add)
    else:
        P.dve.tensor_scalar(out=acc[:, :], in0=xpad[:, 3:3 + SEQ], scalar1=cw[:, 3:4], scalar2=None, op0=ALU.mult)
    for k in range(3):
        P.dve.scalar_tensor_tensor(out=acc[:, :], in0=xpad[:, k:k + SEQ], scalar=cw[:, k:k + 1], in1=acc[:, :], op0=ALU.mult, op1=ALU.add)
    P.act.activation(out=dst_bf, in_=acc[:, :], func=AF.Silu if silu else AF.Copy)


def fm_to_tm(P, cx, src_list, dst, tp_banks, parity=[0]):
    n = len(src_list)
    for t in range(NTS):
        for c0 in range(0, n, 8):
            m = min(8, n - c0)
            pb = tp_banks.next()
            for c in range(m):
                P.pe.transpose(out=pb[:, c * 128:(c + 1) * 128], in_=src_list[c0 + c][:, t * 128:(t + 1) * 128], identity=cx.identb)
            parity[0] ^= 1
            if parity[0]:
                P.dve.tensor_copy(out=dst[:, t, c0 * 128:(c0 + m) * 128], in_=pb[:, :m * 128])
            else:
                P.act.activation(out=dst[:, t, c0 * 128:(c0 + m) * 128], in_=pb[:, :m * 128], func=AF.Copy)


def partial_proj(P, cx, src, K, w_dram, part, norm_vec=None, sumsq=False):
    mark = P.mark()
    KC = K // 128
    tp_banks = Ring([P.bank_bf(0), P.bank_bf(1)], P)
    mm_banks = Ring([P.bank(b_) for b_ in (2, 3, 4, 5, 6, 7)], P)
    srcT = P.tile([128, KC, SEQ], BF16, "srcT")
    ld = P.ring([128, K], F32, 2, "ppld")
    xnr = P.ring([128, K], BF16, 2, "ppxn")
    wbc = bcast_load(P, norm_vec, K, "ppw") if norm_vec is not None else None
    sqr = P.ring([128, 16], F32, 2, "ppsq") if sumsq else None
    outs = []
    for t in range(NTS):
        lt = ld.next()
        P.dma("sp", out=lt[:], in_=src[t * 128:(t + 1) * 128, :])
        xn = xnr.next()
        if sumsq:
            sq = sqr.next()
            P.dve.memset(sq[:, :], 0.0)
            P.act.activation(out=xn[:, :], in_=lt[:, :], func=AF.Square, accum_out=sq[:, 0:1])
            outs.append(P.dma("sp", out=part[t * 128:(t + 1) * 128, 2048:2064], in_=sq[:, :]))
        if wbc is not None:
            P.dve.tensor_tensor(out=xn[:, :], in0=lt[:, :], in1=wbc[:, :], op=ALU.mult)
        else:
            P.dve.tensor_copy(out=xn[:, :], in_=lt[:, :])
        for j, c0 in enumerate(range(0, KC, 8)):
            pb = tp_banks.next()
            n = min(8, KC - c0)
            for c in range(n):
                P.pe.transpose(out=pb[:, c * 128:(c + 1) * 128], in_=xn[:, (c0 + c) * 128:(c0 + c + 1) * 128], identity=cx.identb)
            P.act.activation(out=srcT[:, c0:c0 + n, t * 128:(t + 1) * 128], in_=pb[:, :n * 128].rearrange("p (c t) -> p c t", c=n), func=AF.Copy)
    ld.free()
    xnr.free()
    wring = P.ring([128, KC, 512], BF16, 2, "ppwr")
    stg = P.ring([128, 512], F32, 3, "ppst")
    for cg in range(4):
        wt = wring.next()
        dma_w(P, wt, w_dram[:, cg * 512:(cg + 1) * 512], KC, 512)
        for t in range(NTS):
            pb = mm_banks.next()
            for kc in range(KC):
                P.pe.matmul(out=pb[:, :512], lhsT=srcT[:, kc, t * 128:(t + 1) * 128], rhs=wt[:, kc, :], start=(kc == 0), stop=(kc == KC - 1))
            st = stg.next()
            if t % 2 == 0:
                P.dve.tensor_copy(out=st[:, :], in_=pb[:, :512])
            else:
                P.act.activation(out=st[:, :], in_=pb[:, :512], func=AF.Copy)
            outs.append(P.dma("sp", out=part[t * 128:(t + 1) * 128, cg * 512:(cg + 1) * 512], in_=st[:, :]))
    P.release(mark)
    return outs


def stage_ssd(P, cx, io):
    GW = 1288
    mark = P.mark()
    tp_banks = Ring([P.bank_bf(0), P.bank_bf(1)], P)
    mm_banks = Ring([P.bank(2), P.bank(3)], P)
    wide_banks = Ring([P.bank(b_) for b_ in (2, 3, 4, 5, 6, 7)], P)
    c2 = P.tile([128, NC2], F32, "c2")
    P.dma("sp", out=c2[:], in_=io["c2"][:, :])
    U2, SL2, HA, HB = c2[:, 0:128], c2[:, 128:256], c2[:, 256:384], c2[:, 384:512]
    uT = P.tile([128, 16, SEQ], BF16, "uT")
    ld = P.ring([128, D], BF16, 2, "ld")
    for t in range(NTS):
        lt = ld.next()
        r0_ = (t // 8) * 256 + (t % 2) * 128
        P.dma("sp", out=lt[:], in_=io["u_full"][(t % 8) // 2][r0_:r0_ + 128, :])
        for j, c0 in enumerate((0, 8)):
            pb = tp_banks.next()
            for c in range(8):
                P.pe.transpose(out=pb[:, c * 128:(c + 1) * 128], in_=lt[:, (c0 + c) * 128:(c0 + c + 1) * 128], identity=cx.identb)
            o = uT[:, c0:c0 + 8, t * 128:(t + 1) * 128]
            sv = pb[:, :1024].rearrange("p (c t) -> p c t", c=8)
            if j == 0:
                P.dve.tensor_copy(out=o, in_=sv)
            else:
                P.act.activation(out=o, in_=sv, func=AF.Copy)
    ld.free()
    cwx = P.tile([128, 16, 4], F32, "cwx"); P.dma("sp", out=cwx[:], in_=io["cwx"][:, :, :])
    cbx = P.tile([128, 16], F32, "cbx"); P.dma("sp", out=cbx[:], in_=io["cbx"][:, :])
    cwbc = P.tile([128, 8, 4], F32, "cwbc"); P.dma("sp", out=cwbc[:], in_=io["cwbc"][:, :, :])
    cbbc = P.tile([128, 8], F32, "cbbc"); P.dma("sp", out=cbbc[:], in_=io["cbbc"][:, :])
    dtb = bcast_load(P, io["dtb"], 32, "dtb")
    aneg = bcast_load(P, io["alog"], 32, "aneg")
    P.act.activation(out=aneg[:], in_=aneg[:], func=AF.Exp)
    P.dve.tensor_scalar(out=aneg[:], in0=aneg[:], scalar1=-1.0, scalar2=None, op0=ALU.mult)
    outs = []
    wAr = P.ring([128, 16, 512], BF16, 2, "wA")
    wSr = P.ring([128, 16, 264], BF16, 2, "wS")
    for g in range(4):
        wA = wAr.next()
        wS = wSr.next()
        wZ = wAr.next()
        dma_w(P, wA, io["w_in"][:, g * GW:g * GW + 512], 16, 512)
        dma_w(P, wS, io["w_in"][:, g * GW + 1024:g * GW + 1288], 16, 264)
        dma_w(P, wZ, io["w_in"][:, g * GW + 512:g * GW + 1024], 16, 512)
        xpad = P.tile([128, SEQ + 3], F32, "xpad")
        P.dve.memset(xpad[:, 0:3], 0.0)
        acc = P.tile([128, SEQ], F32, "acc")
        fm = [P.tile([128, SEQ], BF16, f"fm{i}") for i in range(6)]
        for cc in range(4):
            conv_silu_fm(P, cx, wA, uT, cc * 128, xpad, acc, cwx[:, g * 4 + cc, :], cbx[:, g * 4 + cc:g * 4 + cc + 1], fm[cc][:, :], wide_banks)
        conv_silu_fm(P, cx, wS, uT, 0, xpad, acc, cwbc[:, g, :], cbbc[:, g:g + 1], fm[4][:, :], wide_banks)
        conv_silu_fm(P, cx, wS, uT, 128, xpad, acc, cwbc[:, 4 + g, :], cbbc[:, 4 + g:5 + g], fm[5][:, :], wide_banks)
        BT, CT = fm[4], fm[5]
        dt = P.tile([128, NTS, 8], F32, "dt")
        for t in range(NTS):
            pb = wide_banks.next()
            for kc in range(16):
                P.pe.matmul(out=pb[:, :8], lhsT=uT[:, kc, t * 128:(t + 1) * 128], rhs=wS[:, kc, 256:264], start=(kc == 0), stop=(kc == 15))
            P.dve.tensor_tensor(out=dt[:, t, :], in0=pb[:, :8], in1=dtb[:, g * 8:(g + 1) * 8], op=ALU.add)
        tmp = P.tile([128, NTS, 8], F32, "dttmp")
        softplus_inplace(P, dt[:, :, :], tmp[:, :, :])
        av = P.tile([128, NTS, 8], F32, "av")
        P.dve.tensor_tensor(out=av[:, :, :], in0=dt[:, :, :], in1=aneg[:, g * 8:(g + 1) * 8].unsqueeze(1).to_broadcast([128, NTS, 8]), op=ALU.mult)
        P.free(tmp, xpad, acc)
        sz = P.tile([128, NTS, 512], BF16, "sz")
        for t in range(NTS):
            pb = wide_banks.next()
            for kc in range(16):
                P.pe.matmul(out=pb[:, :512], lhsT=uT[:, kc, t * 128:(t + 1) * 128], rhs=wZ[:, kc, :], start=(kc == 0), stop=(kc == 15))
            P.act.activation(out=sz[:, t, :], in_=pb[:, :512], func=AF.Silu)
        x_tm = P.tile([128, NTS, 512], BF16, "x_tm")
        fm_to_tm(P, cx, [fm[i][:, :] for i in range(4)], x_tm, tp_banks)
        B_tm = P.tile([128, NTS, 128], BF16, "B_tm")
        fm_to_tm(P, cx, [BT[:, :]], B_tm, tp_banks)
        P.free(*fm[:4])
        dfull = bcast_load(P, io["dfull"][:, g * 512:(g + 1) * 512], 512, "dfull")
        S = P.tile([128, 512], F32, "S")
        P.dve.memset(S[:, :], 0.0)
        Sb = P.ring([128, 512], BF16, 3, "Sb")
        S0b = Sb.next()
        P.dve.memset(S0b[:, :], 0.0)
        aUr = P.ring([128, 8, 128], F32, 2, "aU")
        Lr = P.ring([128, 8, 128], F32, 2, "L")
        Mr = P.ring([128, 8, 128], BF16, 2, "M")
        mcbr = P.ring([128, 128], F32, 2, "mcb")
        xcr = P.ring([128, 512], BF16, 2, "xc")
        xdr = P.ring([128, 512], BF16, 2, "xdec")
        yor = P.ring([128, 512], F32, 2, "yoff")
        ygr = P.ring([128, 512], F32, 2, "yg")
        ps_E = P.bank(4, 2)
        ps_sm = P.bank(6)
        ps_y = P.bank(7)
        def ssd_f1(t):
            a_t = av[:, t, :]
            P.pe.matmul(out=ps_sm[:, 0:8], lhsT=U2, rhs=a_t, start=True, stop=True)
            P.pe.matmul(out=ps_sm[:, 8:16], lhsT=SL2, rhs=a_t, start=True, stop=True)
            P.pe.matmul(out=ps_sm[:, 16:24], lhsT=HA, rhs=a_t, start=True, stop=True)
            P.pe.matmul(out=ps_sm[:, 24:32], lhsT=HB, rhs=a_t, start=True, stop=True)
            P.pe.matmul(out=ps_sm[:, 128:256], lhsT=BT[:, t * 128:(t + 1) * 128], rhs=CT[:, t * 128:(t + 1) * 128], start=True, stop=True)
            sm = cx.small32.next()
            P.act.activation(out=sm[:, 0:32], in_=ps_sm[:, 0:32], func=AF.Exp)
            mcb = mcbr.next()
            P.dve.tensor_tensor(out=mcb[:, :], in0=ps_sm[:, 128:256], in1=U2, op=ALU.mult)
            aU = aUr.next()
            P.dve.tensor_tensor(out=aU[:, :, :], in0=U2.unsqueeze(1).to_broadcast([128, 8, 128]),
                                in1=a_t.unsqueeze(2).to_broadcast([128, 8, 128]), op=ALU.mult)
            return sm, mcb, aU

        def ssd_f2(t, sm, mcb, aU):
            for q in range(2):
                P.pe.matmul(out=ps_E[:, q * 512:(q + 1) * 512], lhsT=SL2, rhs=aU[:, q * 4:(q + 1) * 4, :], start=True, stop=True)
            L = Lr.next()
            P.act.activation(out=L[:, :, :], in_=ps_E[:, :].rearrange("p (h i) -> p h i", h=8), func=AF.Exp)
            M = Mr.next()
            P.dve.tensor_tensor(out=M[:, :, :], in0=L[:, :, :], in1=mcb[:, :].unsqueeze(1).to_broadcast([128, 8, 128]), op=ALU.mult)
            xc = xcr.next()
            P.dve.tensor_tensor(out=xc[:, :].rearrange("p (h q) -> p h q", h=8), in0=x_tm[:, t, :].rearrange("p (h q) -> p h q", h=8),
                                in1=dt[:, t, :].unsqueeze(2).to_broadcast([128, 8, 64]), op=ALU.mult)
            xd = xdr.next()
            P.dve.tensor_tensor(out=xd[:, :].rearrange("p (h q) -> p h q", h=8), in0=xc[:, :].rearrange("p (h q) -> p h q", h=8),
                                in1=sm[:, 8:16].unsqueeze(2).to_broadcast([128, 8, 64]), op=ALU.mult)
            return sm, M, xc, xd

        def ssd_back(t, S0b, sm, M, xc, xd):
            ps_st = mm_banks.next()
            P.pe.matmul(out=ps_st[:, :512], lhsT=B_tm[0:64, t, :], rhs=xd[0:64, :], start=True, stop=True)
            S1b = Sb.next()
            P.dve.tensor_tensor(out=S[:, :].rearrange("p (h q) -> p h q", h=8), in0=S[:, :].rearrange("p (h q) -> p h q", h=8),
                                in1=sm[:, 16:24].unsqueeze(2).to_broadcast([128, 8, 64]), op=ALU.mult)
            P.dve.tensor_tensor(out=S[:, :], in0=ps_st[:, :512], in1=S[:, :], op=ALU.add)
            P.act.activation(out=S1b[:, :], in_=S[:, :], func=AF.Copy)
            ps_st2 = mm_banks.next()
            P.pe.matmul(out=ps_st2[:, :512], lhsT=B_tm[64:128, t, :], rhs=xd[64:128, :], start=True, stop=True)
            S2b = Sb.next()
            P.dve.tensor_tensor(out=S[:, :].rearrange("p (h q) -> p h q", h=8), in0=S[:, :].rearrange("p (h q) -> p h q", h=8),
                                in1=sm[:, 24:32].unsqueeze(2).to_broadcast([128, 8, 64]), op=ALU.mult)
            P.dve.tensor_tensor(out=S[:, :], in0=ps_st2[:, :512], in1=S[:, :], op=ALU.add)
            P.act.activation(out=S2b[:, :], in_=S[:, :], func=AF.Copy)
            ps_o = mm_banks.next()
            P.pe.matmul(out=ps_o[0:64, :512], lhsT=CT[:, t * 128:t * 128 + 64], rhs=S0b[:, :], start=True, stop=True)
            P.pe.matmul(out=ps_o[64:128, :512], lhsT=CT[:, t * 128 + 64:(t + 1) * 128], rhs=S1b[:, :], start=True, stop=True)
            yo = yor.next()
            P.dve.tensor_tensor(out=yo[:, :].rearrange("p (h q) -> p h q", h=8), in0=ps_o[:, :512].rearrange("p (h q) -> p h q", h=8),
                                in1=sm[:, 0:8].unsqueeze(2).to_broadcast([128, 8, 64]), op=ALU.mult)
            for hh in range(8):
                P.pe.matmul(out=ps_y[:, hh * 64:(hh + 1) * 64], lhsT=M[:, hh, :], rhs=xc[:, hh * 64:(hh + 1) * 64], start=True, stop=True)
            P.dve.tensor_tensor(out=yo[:, :], in0=ps_y[:, :512], in1=yo[:, :], op=ALU.add)
            yg = ygr.next()
            P.dve.tensor_tensor(out=yg[:, :], in0=x_tm[:, t, :], in1=dfull[:, :], op=ALU.mult)
            P.dve.tensor_tensor(out=yg[:, :], in0=yg[:, :], in1=yo[:, :], op=ALU.add)
            P.dve.tensor_tensor(out=yg[:, :], in0=yg[:, :], in1=sz[:, t, :], op=ALU.mult)
            outs.append(P.dma("sp", out=io["yg"][t * 128:(t + 1) * 128, g * 512:(g + 1) * 512], in_=yg[:, :]))
            return S2b

        st1, st2 = {}, {}
        for step in range(NTS + 2):
            if step < NTS:
                st1[step] = ssd_f1(step)
            if 0 <= step - 1 < NTS:
                st2[step - 1] = ssd_f2(step - 1, *st1.pop(step - 1))
            if 0 <= step - 2 < NTS:
                S0b = ssd_back(step - 2, S0b, *st2.pop(step - 2))
        for r in (Sb, aUr, Lr, Mr, mcbr, xcr, xdr, yor, ygr):
            r.free()
        P.free(S, dfull, x_tm, B_tm, BT, CT, sz, dt, av)
    P.release(mark)
    outs += partial_proj(P, cx, io["yg"], 2048, io["w_out"], io["part"], norm_vec=io["vec_ssdnorm"], sumsq=True)
    return outs


L2EPS = 1e-6
def proj_fm(P, w_t, uT, col0, dst, mm_banks, func=None):
    for th in range(SEQ // 512):
        pb = mm_banks.next()
        for kc in range(16):
            P.pe.matmul(out=pb[:, :512], lhsT=w_t[:, kc, col0:col0 + 128], rhs=uT[:, kc, th * 512:(th + 1) * 512],
                        start=(kc == 0), stop=(kc == 15))
        P.act.activation(out=dst[:, th * 512:(th + 1) * 512], in_=pb[:, :512], func=func or AF.Copy)


def proj_tm(P, w_t, uT, col0, ncols, dst, mm_banks, func=None):
    for t in range(NTS):
        pb = mm_banks.next()
        for kc in range(16):
            P.pe.matmul(out=pb[:, :ncols], lhsT=uT[:, kc, t * 128:(t + 1) * 128], rhs=w_t[:, kc, col0:col0 + ncols],
                        start=(kc == 0), stop=(kc == 15))
        P.act.activation(out=dst[:, t, :], in_=pb[:, :ncols], func=func or AF.Copy)


def bc3(v, n_mid, n_last):
    return v.unsqueeze(2).to_broadcast([v.shape[0], n_mid, n_last])


def bcm(v, n_mid):
    return v.unsqueeze(1).to_broadcast([v.shape[0], n_mid, v.shape[1]])


def head_norm_out(P, cx, o, gate_t, wn, dst_dram, nh, rings):
    sq = rings["sq"].next()
    P.dve.tensor_tensor(out=sq[:, :, :], in0=o, in1=o, op=ALU.mult)
    sm = cx.small.next()
    P.dve.tensor_reduce(out=sm[:, 0:nh], in_=sq[:, :, :], axis=AX.X, op=ALU.add)
    P.dve.tensor_scalar(out=sm[:, 0:nh], in0=sm[:, 0:nh], scalar1=1.0 / 128, scalar2=EPS, op0=ALU.mult, op1=ALU.add)
    P.act.activation(out=sm[:, 4:4 + nh], in_=sm[:, 0:nh], func=AF.Ln)
    P.act.activation(out=sm[:, 4:4 + nh], in_=sm[:, 4:4 + nh], func=AF.Exp, scale=-0.5)
    y = rings["y"].next()
    P.dve.tensor_tensor(out=y[:, :, :], in0=o, in1=bc3(sm[:, 4:4 + nh], nh, 128), op=ALU.mult)
    yf = y[:, :, :].rearrange("p h d -> p (h d)")
    P.dve.tensor_tensor(out=yf, in0=yf, in1=wn, op=ALU.mult)
    P.dve.tensor_tensor(out=yf, in0=yf, in1=gate_t, op=ALU.mult)
    return P.dma("sp", out=dst_dram, in_=yf)


def stage_hy(P, cx, io):
    mark = P.mark()
    tp_banks = Ring([P.bank_bf(0), P.bank_bf(1)], P)
    mm_banks = Ring([P.bank(2), P.bank(3)], P)
    wide_banks = Ring([P.bank(b_) for b_ in (2, 3, 4, 5, 6, 7)], P)
    c2 = P.tile([128, NC2], F32, "c2")
    P.dma("sp", out=c2[:], in_=io["c2"][:, :])
    U2, SL2, HA, HB = c2[:, 0:128], c2[:, 128:256], c2[:, 256:384], c2[:, 384:512]
    onesb = P.tile([128, 128], BF16, "onesb")
    P.dve.memset(onesb[:, :], 1.0)
    uT = P.tile([128, 16, SEQ], BF16, "uT")
    wbc = bcast_load(P, io["vec_mix"], D, "wbc")
    ld = P.ring([128, D], F32, 2, "ld")
    xn_ring = P.ring([128, D], BF16, 2, "xn")
    for t in range(NTS):
        lt = ld.next()
        P.dma("sp", out=lt[:], in_=io["xfull"][t * 128:(t + 1) * 128, :])
        norm_T(cx, [lt[:]], D, wbc[:], uT, t * 128, xn_ring, tp_banks)
    ld.free()
    xn_ring.free()
    P.free(wbc)
    outs = []
    lbl = P.tile([128, 8], F32, "lbl")
    P.dma("sp", out=lbl[:], in_=io["lbl"][:, :])
    lb = P.tile([128, 16], F32, "lb")
    P.dve.tensor_tensor(out=lb[:, 12:16], in0=lbl[:, 0:4], in1=lbl[:, 4:8], op=ALU.subtract)
    P.act.activation(out=lb[:, 0:4], in_=lb[:, 12:16], func=AF.Sigmoid)
    P.dve.tensor_scalar(out=lb[:, 4:8], in0=lb[:, 0:4], scalar1=-1.0, scalar2=1.0, op0=ALU.mult, op1=ALU.add)
    P.dve.tensor_scalar(out=lb[:, 8:12], in0=lb[:, 4:8], scalar1=-1.0, scalar2=None, op0=ALU.mult)
    rmask = P.tile([128, SEQ], BF16, "rmask")
    P.dve.memset(rmask[:, :], 1.0)
    P.dve.memset(rmask[:, :].rearrange("p (c j) -> p c j", j=64)[:, :, 0:1], 0.0)
    wtm = P.tile([128, 16, 512], BF16, "wtm")
    wtm2 = P.tile([128, 16, 512], BF16, "wtmb")
    dma_w(P, wtm, io["w_in"][:, 1024:1536], 16, 512)
    dma_w(P, wtm2, io["w_in"][:, 1536:2048], 16, 512)
    v_tm = P.tile([128, NTS, 512], BF16, "v_tm")
    proj_tm(P, wtm, uT, 0, 512, v_tm, wide_banks)
    sg_tm = P.tile([128, NTS, 512], BF16, "sg_tm")
    proj_tm(P, wtm2, uT, 0, 512, sg_tm, wide_banks, func=AF.Silu)
    P.free(wtm, wtm2)
    wn = bcast_load(P, io["hgrn_norm"], 512, "wn")
    rings1 = None
    qt_l, kt_l, kd_l, egl_l = [], [], [], []
    whr = P.ring([128, 16, 256], BF16, 2, "wh")
    for h in range(4):
        wh = whr.next()
        dma_w(P, wh, io["w_in"][:, h * 256:(h + 1) * 256], 16, 256)
        qf = P.tile([128, SEQ], BF16, "qf")
        ff = P.tile([128, SEQ], F32, "ff")
        proj_fm(P, wh, uT, 0, qf[:, :], wide_banks, func=AF.Silu)
        proj_fm(P, wh, uT, 128, ff[:, :], wide_banks, func=AF.Sigmoid)
        lf = P.tile([128, SEQ], F32, "lf")
        P.dve.tensor_scalar(out=lf[:, :], in0=ff[:, :], scalar1=lb[:, 4 + h:5 + h], scalar2=lb[:, h:h + 1], op0=ALU.mult, op1=ALU.add)
        P.act.activation(out=lf[:, :], in_=lf[:, :], func=AF.Ln)
        kk = P.tile([128, SEQ], F32, "kk")
        P.dve.tensor_scalar(out=kk[:, :], in0=ff[:, :], scalar1=lb[:, 8 + h:9 + h], scalar2=lb[:, 4 + h:5 + h], op0=ALU.mult, op1=ALU.add)
        g = ff
        P.dve.tensor_tensor_scan(out=g[:, :], data0=rmask[:, :], data1=lf[:, :], initial=0.0, op0=ALU.mult, op1=ALU.add)
        eg = lf
        P.act.activation(out=eg[:, :], in_=g[:, :], func=AF.Exp)
        qt = P.tile([128, SEQ], BF16, "qt")
        P.dve.tensor_tensor(out=qt[:, :], in0=qf[:, :], in1=eg[:, :], op=ALU.mult)
        eng = qf
        P.act.activation(out=eng[:, :], in_=g[:, :], func=AF.Exp, scale=-1.0)
        kt = P.tile([128, SEQ], BF16, "kt")
        P.dve.tensor_tensor(out=kk[:, :], in0=kk[:, :], in1=eng[:, :], op=ALU.mult)
        P.act.activation(out=kt[:, :], in_=kk[:, :], func=AF.Copy)
        egl = P.tile([128, 32], F32, "egl")
        P.dve.tensor_copy(out=egl[:, :], in_=eg[:, :].rearrange("p (c j) -> p c j", j=64)[:, :, 63])
        kd = qf
        P.dve.tensor_tensor(out=kd[:, :].rearrange("p (c j) -> p c j", j=64), in0=kk[:, :].rearrange("p (c j) -> p c j", j=64),
                            in1=bc3(egl[:, :], 32, 64), op=ALU.mult)
        kd_tm = P.tile([128, NTS, 128], BF16, "kd_tm")
        fm_to_tm(P, cx, [kd[:, :]], kd_tm, tp_banks)
        P.free(qf, ff, lf, kk)
        qt_l.append(qt); kt_l.append(kt); kd_l.append(kd_tm); egl_l.append(egl)
    whr.free()
    qb = Ring([P.psum_slice(bk, 0, 128) for bk in range(2, 8)], P)
    S_l, Sb_l, S0_l = [], [], []
    for h in range(4):
        S = P.tile([128, 128], F32, "S")
        P.dve.memset(S[:, :], 0.0)
        Sb = P.ring([128, 128], BF16, 3, "Sb")
        S0b = Sb.next()
        P.dve.memset(S0b[:, :], 0.0)
        S_l.append(S); Sb_l.append(Sb); S0_l.append(S0b)
    Amr = P.ring([128, 128], BF16, 8, "Am")
    oir = P.ring([128, 128], F32, 4, "oi")
    o1r = P.ring([128, 1, 128], F32, 1, "o1")
    o4r = P.ring([128, 4, 128], F32, 2, "o4")
    rings4 = {"sq": P.ring([128, 4, 128], F32, 2, "sq4"), "y": P.ring([128, 4, 128], F32, 2, "y4")}
    for t in range(NTS):
        ts = slice(t * 128, (t + 1) * 128)
        keep = {}
        for h in range(4):
            qt, kt, kd_tm, egl, S, Sb, S0b = qt_l[h], kt_l[h], kd_l[h], egl_l[h], S_l[h], Sb_l[h], S0_l[h]
            pa = qb.next()
            P.pe.matmul(out=pa[:, :], lhsT=kt[:, ts], rhs=qt[:, ts], start=True, stop=True)
            Am = Amr.next()
            P.dve.tensor_tensor(out=Am[:, :], in0=pa[:, :], in1=U2, op=ALU.mult)
            ps1 = qb.next()
            P.pe.matmul(out=ps1[:, :], lhsT=kd_tm[0:64, t, :], rhs=v_tm[0:64, t, h * 128:(h + 1) * 128], start=True, stop=True)
            P.dve.scalar_tensor_tensor(out=S[:, :], in0=S[:, :], scalar=egl[:, 2 * t:2 * t + 1], in1=ps1[:, :], op0=ALU.mult, op1=ALU.add)
            S1b = Sb.next()
            P.act.activation(out=S1b[:, :], in_=S[:, :], func=AF.Copy)
            ps2 = qb.next()
            P.pe.matmul(out=ps2[:, :], lhsT=kd_tm[64:128, t, :], rhs=v_tm[64:128, t, h * 128:(h + 1) * 128], start=True, stop=True)
            P.dve.scalar_tensor_tensor(out=S[:, :], in0=S[:, :], scalar=egl[:, 2 * t + 1:2 * t + 2], in1=ps2[:, :], op0=ALU.mult, op1=ALU.add)
            S2b = Sb.next()
            P.act.activation(out=S2b[:, :], in_=S[:, :], func=AF.Copy)
            keep[h] = (Am, S1b, S2b)
        o4 = o4r.next()
        for h in range(4):
            qt, S0b = qt_l[h], S0_l[h]
            Am, S1b, S2b = keep[h]
            pi = qb.next()
            P.pe.matmul(out=pi[0:64, :], lhsT=qt[:, t * 128:t * 128 + 64], rhs=S0b[:, :], start=True, stop=True)
            P.pe.matmul(out=pi[64:128, :], lhsT=qt[:, t * 128 + 64:(t + 1) * 128], rhs=S1b[:, :], start=True, stop=True)
            oi = oir.next()
            P.act.activation(out=oi[:, :], in_=pi[:, :], func=AF.Copy)
            po = qb.next()
            P.pe.matmul(out=po[:, :], lhsT=Am[:, :], rhs=v_tm[:, t, h * 128:(h + 1) * 128], start=True, stop=True)
            P.dve.tensor_tensor(out=o4[:, h, :], in0=po[:, :], in1=oi[:, :], op=ALU.add)
            S0_l[h] = S2b
        outs.append(head_norm_out(P, cx, o4[:, :, :], sg_tm[:, t, :], wn[:, :], io["omix"][t * 128:(t + 1) * 128, 0:512], 4, rings4))
    for r in Sb_l + [Amr, oir, o1r, o4r, rings4["sq"], rings4["y"]]:
        r.free()
    P.free(*(S_l + qt_l + kt_l + kd_l + egl_l))
    P.free(v_tm, sg_tm, wn, rmask, lb, lbl)
    rings = {"sq": P.ring([128, 4, 128], F32, 2, "sq"), "y": P.ring([128, 4, 128], F32, 2, "y")}
    cw = P.tile([128, 12, 4], F32, "cw")
    P.dma("sp", out=cw[:], in_=io["gcw"][:, :, :])
    qn = [P.tile([128, SEQ], BF16, f"qn{h}") for h in range(4)]
    kn = [P.tile([128, SEQ], BF16, f"kn{h}") for h in range(4)]
    kv_tm = [P.tile([128, NTS, 256], BF16, f"kv{h}") for h in range(4)]
    whgr = P.ring([128, 16, 384], BF16, 2, "whg")
    for h in range(4):
        wh = whgr.next()
        dma_w(P, wh, io["w_in"][:, 2048 + h * 384:2048 + (h + 1) * 384], 16, 384)
        xpad = P.tile([128, SEQ + 3], F32, "xpad")
        P.dve.memset(xpad[:, 0:3], 0.0)
        acc = P.tile([128, SEQ], F32, "acc")
        sq = P.tile([128, SEQ], BF16, "sqb")
        rn = P.tile([128, SEQ], F32, "rn")
        vb = P.tile([128, SEQ], BF16, "vb")
        for j, dst in enumerate((qn[h], kn[h])):
            conv_silu_fm(P, cx, wh, uT, j * 128, xpad, acc, cw[:, h * 3 + j, :], None, acc[:, :], wide_banks)
            P.act.activation(out=sq[:, :], in_=acc[:, :], func=AF.Square)
            for th in range(4):
                pb = mm_banks.next()
                P.pe.matmul(out=pb[:, :512], lhsT=onesb[:, :], rhs=sq[:, th * 512:(th + 1) * 512], start=True, stop=True)
                P.dve.tensor_scalar(out=rn[:, th * 512:(th + 1) * 512], in0=pb[:, :512], scalar1=L2EPS, scalar2=None, op0=ALU.add)
                P.act.activation(out=rn[:, th * 512:(th + 1) * 512], in_=rn[:, th * 512:(th + 1) * 512], func=AF.Ln)
                P.act.activation(out=rn[:, th * 512:(th + 1) * 512], in_=rn[:, th * 512:(th + 1) * 512], func=AF.Exp, scale=-0.5)
            if j == 0:
                P.dve.scalar_tensor_tensor(out=dst[:, :], in0=acc[:, :], scalar=128 ** -0.5, in1=rn[:, :], op0=ALU.mult, op1=ALU.mult)
            else:
                P.dve.tensor_tensor(out=dst[:, :], in0=acc[:, :], in1=rn[:, :], op=ALU.mult)
        conv_silu_fm(P, cx, wh, uT, 256, xpad, acc, cw[:, h * 3 + 2, :], None, vb[:, :], wide_banks)
        fm_to_tm(P, cx, [kn[h][:, :], vb[:, :]], kv_tm[h], tp_banks)
        P.free(xpad, acc, sq, rn, vb)
    whgr.free()
    wtm = P.tile([128, 16, 520], BF16, "wtm2")
    dma_w(P, wtm, io["w_in"][:, 3584:4104], 16, 520)
    sg_tm = P.tile([128, NTS, 512], BF16, "sg2")
    proj_tm(P, wtm, uT, 0, 512, sg_tm, wide_banks, func=AF.Silu)
    bd = P.tile([128, NTS, 8], F32, "bd")
    proj_tm(P, wtm, uT, 512, 8, bd, wide_banks)
    P.free(wtm, uT)
    gp = P.tile([128, 12], F32, "gp")
    P.dma("sp", out=gp[:, 0:8], in_=io["gdn_p"][:, :].to_broadcast([128, 8]))
    P.act.activation(out=gp[:, 0:4], in_=gp[:, 0:4], func=AF.Exp)
    P.dve.tensor_scalar(out=gp[:, 0:4], in0=gp[:, 0:4], scalar1=-1.0, scalar2=None, op0=ALU.mult)
    beta = P.tile([128, NTS, 4], F32, "beta")
    nbeta = P.tile([128, NTS, 4], F32, "nbeta")
    gg = P.tile([128, NTS, 4], F32, "gg")
    tmp = P.tile([128, NTS, 4], F32, "tmpg")
    P.act.activation(out=beta[:, :, :], in_=bd[:, :, 0:4], func=AF.Sigmoid)
    P.dve.tensor_scalar(out=nbeta[:, :, :], in0=beta[:, :, :], scalar1=-1.0, scalar2=None, op0=ALU.mult)
    P.dve.tensor_tensor(out=gg[:, :, :], in0=bd[:, :, 4:8], in1=gp[:, 4:8].unsqueeze(1).to_broadcast([128, NTS, 4]), op=ALU.add)
    P.dve.tensor_scalar(out=tmp[:, :, :], in0=gg[:, :, :], scalar1=-1.0, scalar2=None, op0=ALU.mult)
    P.dve.tensor_tensor(out=tmp[:, :, :], in0=tmp[:, :, :], in1=gg[:, :, :], op=ALU.max)
    P.act.activation(out=tmp[:, :, :], in_=tmp[:, :, :], func=AF.Exp, scale=-1.0)
    P.act.activation(out=tmp[:, :, :], in_=tmp[:, :, :], func=AF.Ln, bias=1.0, scale=1.0)
    P.dve.tensor_scalar(out=gg[:, :, :], in0=gg[:, :, :], scalar1=0.0, scalar2=None, op0=ALU.max)
    P.dve.tensor_tensor(out=gg[:, :, :], in0=gg[:, :, :], in1=tmp[:, :, :], op=ALU.add)
    P.dve.tensor_tensor(out=gg[:, :, :], in0=gg[:, :, :], in1=gp[:, 0:4].unsqueeze(1).to_broadcast([128, NTS, 4]), op=ALU.mult)
    P.free(tmp, bd)
    wn = bcast_load(P, io["gdn_norm"], 512, "wn2")
    S = P.tile([128, 4, 128], F32, "Sg")
    P.dve.memset(S[:, :, :], 0.0)
    Sbr = P.ring([128, 4, 128], BF16, 3, "Sgb")
    Sb_cur = Sbr.next()
    P.dve.memset(Sb_cur[:, :, :], 0.0)
    R3 = lambda nm, dt=F32, n=2: P.ring([128, 4, 128], dt, n, nm)
    gSLr, gUr, Dr, DTr, Zr, Yr, Pr, qkr, bvr, kdr, Rr, vnr, otr, TTr = (R3("gSL"), R3("gU"), R3("D"), R3("DT"), R3("Z", F32, 3), R3("Y", F32, 3), R3("P"),
                                                                   R3("qk", BF16), R3("bv"), R3("kdc", BF16), R3("R", BF16), R3("vn", BF16), R3("ot"), R3("TT", BF16))
    rfr = R3("rf")
    identf = cx.identf
    bE, bET, bKK, bKQ, bSM = P.bank(4), P.bank(5), P.bank(6), P.bank(7), P.bank(2)
    for t in range(NTS):
        ts = slice(t * 128, (t + 1) * 128)
        g_t = gg[:, t, :]
        P.pe.matmul(out=bSM[:, 0:4], lhsT=U2, rhs=g_t, start=True, stop=True)
        P.pe.matmul(out=bSM[:, 4:8], lhsT=SL2, rhs=g_t, start=True, stop=True)
        P.pe.matmul(out=bSM[:, 8:12], lhsT=HA, rhs=g_t, start=True, stop=True)
        P.pe.matmul(out=bSM[:, 12:16], lhsT=HB, rhs=g_t, start=True, stop=True)
        sm = cx.small32.next()
        P.act.activation(out=sm[:, 0:16], in_=bSM[:, 0:16], func=AF.Exp)
        P.dve.tensor_tensor(out=sm[:, 16:20], in0=sm[:, 0:4], in1=nbeta[:, t, :], op=ALU.mult)
        gSL, gU = gSLr.next(), gUr.next()
        P.dve.tensor_tensor(out=gSL[:, :, :], in0=bcm(SL2, 4), in1=bc3(g_t, 4, 128), op=ALU.mult)
        P.dve.tensor_tensor(out=gU[:, :, :], in0=bcm(U2, 4), in1=bc3(g_t, 4, 128), op=ALU.mult)
        P.pe.matmul(out=bE[:, :512], lhsT=U2, rhs=gSL[:, :, :], start=True, stop=True)
        P.pe.matmul(out=bET[:, :512], lhsT=SL2, rhs=gU[:, :, :], start=True, stop=True)
        for h in range(4):
            P.pe.matmul(out=bKK[:, h * 128:(h + 1) * 128], lhsT=kn[h][:, ts], rhs=kn[h][:, ts], start=True, stop=True)
            P.pe.matmul(out=bKQ[:, h * 128:(h + 1) * 128], lhsT=kn[h][:, ts], rhs=qn[h][:, ts], start=True, stop=True)
        Dm, DT = Dr.next(), DTr.next()
        P.act.activation(out=Dm[:, :, :], in_=bE[:, :512].rearrange("p (h j) -> p h j", h=4), func=AF.Exp)
        P.act.activation(out=DT[:, :, :], in_=bET[:, :512].rearrange("p (h j) -> p h j", h=4), func=AF.Exp)
        Z = Zr.next()
        P.dve.tensor_tensor(out=Z[:, :, :], in0=bKK[:, :512].rearrange("p (h j) -> p h j", h=4), in1=Dm[:, :, :], op=ALU.mult)
        P.dve.tensor_tensor(out=Z[:, :, :], in0=Z[:, :, :], in1=bcm(SL2, 4), op=ALU.mult)
        P.dve.tensor_tensor(out=Z[:, :, :], in0=Z[:, :, :], in1=bc3(nbeta[:, t, :], 4, 128), op=ALU.mult)
        qk = qkr.next()
        P.dve.tensor_tensor(out=DT[:, :, :], in0=bKQ[:, :512].rearrange("p (h j) -> p h j", h=4), in1=DT[:, :, :], op=ALU.mult)
        P.dve.tensor_tensor(out=qk[:, :, :], in0=DT[:, :, :], in1=bcm(U2, 4), op=ALU.mult)
        pT = mm_banks.next()
        for h in range(4):
            P.pe.transpose(out=pT[:, h * 128:(h + 1) * 128], in_=Z[:, h, :], identity=identf)
        Y = Yr.next()
        P.act.activation(out=Y[:, :, :], in_=pT[:, :512].rearrange("p (h j) -> p h j", h=4), func=AF.Copy)
        Pm = Pr.next()
        P.dve.tensor_tensor(out=Pm[:, :, :], in0=Y[:, :, :], in1=bcm(identf, 4), op=ALU.add)
        for m in range(1, 6):
            Zn = Zr.next()
            pz = bE
            for h in range(4):
                P.pe.matmul(out=pz[:, h * 128:(h + 1) * 128], lhsT=Y[:, h, :], rhs=Z[:, h, :], start=True, stop=True)
            P.act.activation(out=Zn[:, :, :], in_=pz[:, :512].rearrange("p (h j) -> p h j", h=4), func=AF.Copy)
            if m < 5:
                Yn = Yr.next()
                py = bET
                for h in range(4):
                    P.pe.matmul(out=py[:, h * 128:(h + 1) * 128], lhsT=Z[:, h, :], rhs=Y[:, h, :], start=True, stop=True)
                P.dve.tensor_copy(out=Yn[:, :, :], in_=py[:, :512].rearrange("p (h j) -> p h j", h=4))
            pp = bKK
            for h in range(4):
                P.pe.matmul(out=pp[:, h * 128:(h + 1) * 128], lhsT=Zn[:, h, :], rhs=Pm[:, h, :], start=True, stop=True)
            Pn = Pr.next()
            P.dve.tensor_tensor(out=Pn[:, :, :], in0=pp[:, :512].rearrange("p (h j) -> p h j", h=4), in1=Pm[:, :, :], op=ALU.add)
            Z, Pm = Zn, Pn
            if m < 5:
                Y = Yn
        TT = TTr.next()
        P.act.activation(out=TT[:, :, :], in_=Pm[:, :, :], func=AF.Copy)
        bv, kdc = bvr.next(), kdr.next()
        for h in range(4):
            P.dve.tensor_scalar(out=bv[:, h, :], in0=kv_tm[h][:, t, 128:256], scalar1=beta[:, t, h:h + 1], scalar2=None, op0=ALU.mult)
            P.dve.tensor_scalar(out=kdc[:, h, :], in0=kv_tm[h][:, t, 0:128], scalar1=sm[:, 4 + h:5 + h], scalar2=None, op0=ALU.mult)
        ot = otr.next()
        for c in range(2):
            r = slice(c * 64, (c + 1) * 64)
            tr = slice(t * 128 + c * 64, t * 128 + (c + 1) * 64)
            pks, pqs = mm_banks.next(), mm_banks.next()
            for h in range(4):
                P.pe.matmul(out=pks[r, h * 128:(h + 1) * 128], lhsT=kn[h][:, tr], rhs=Sb_cur[:, h, :], start=True, stop=True)
                P.pe.matmul(out=pqs[r, h * 128:(h + 1) * 128], lhsT=qn[h][:, tr], rhs=Sb_cur[:, h, :], start=True, stop=True)
            Rt = Rr.next()
            rf = rfr.next()
            P.dve.tensor_tensor(out=rf[r, :, :], in0=pks[r, :512].rearrange("p (h j) -> p h j", h=4), in1=bc3(sm[r, 16:20], 4, 128), op=ALU.mult)
            P.dve.tensor_tensor(out=Rt[r, :, :], in0=rf[r, :, :], in1=bv[r, :, :], op=ALU.add)
            pvn = bKQ
            for h in range(4):
                P.pe.matmul(out=pvn[r, h * 128:(h + 1) * 128], lhsT=TT[r, h, c * 64:(c + 1) * 64], rhs=Rt[r, h, :], start=True, stop=True)
            vn = vnr.next()
            P.act.activation(out=vn[r, :, :], in_=pvn[r, :512].rearrange("p (h j) -> p h j", h=4), func=AF.Copy)
            poi = bE
            for h in range(4):
                P.pe.matmul(out=poi[r, h * 128:(h + 1) * 128], lhsT=qk[r, h, c * 64:(c + 1) * 64], rhs=vn[r, h, :], start=True, stop=True)
            P.dve.tensor_tensor(out=rf[r, :, :], in0=pqs[r, :512].rearrange("p (h j) -> p h j", h=4), in1=bc3(sm[r, 0:4], 4, 128), op=ALU.mult)
            P.dve.tensor_tensor(out=ot[r, :, :], in0=poi[r, :512].rearrange("p (h j) -> p h j", h=4), in1=rf[r, :, :], op=ALU.add)
            pst = bET
            for h in range(4):
                P.pe.matmul(out=pst[:, h * 128:(h + 1) * 128], lhsT=kdc[r, h, :], rhs=vn[r, h, :], start=True, stop=True)
            cd = sm[:, 8 + 4 * c:12 + 4 * c]
            P.dve.tensor_tensor(out=S[:, :, :], in0=S[:, :, :], in1=bc3(cd, 4, 128), op=ALU.mult)
            P.dve.tensor_tensor(out=S[:, :, :], in0=pst[:, :512].rearrange("p (h j) -> p h j", h=4), in1=S[:, :, :], op=ALU.add)
            Sb_cur = Sbr.next()
            P.act.activation(out=Sb_cur[:, :, :], in_=S[:, :, :], func=AF.Copy)
        outs.append(head_norm_out(P, cx, ot[:, :, :], sg_tm[:, t, :], wn[:, :], io["omix"][t * 128:(t + 1) * 128, 512:1024], 4, rings))
    P.release(mark)
    outs += partial_proj(P, cx, io["omix"], 1024, io["w_out"], io["part"])
    return outs


def make_consts():
    c = np.zeros((128, NCONST), np.float32)
    c[:, 0:128] = np.eye(128)
    t = np.arange(128)[:, None] % 64
    i = np.arange(64)[None, :]
    c[:, 128:192] = (t <= i)
    c[:, 192:256] = (t > i)
    c[:, 256:320] = (t < i)
    c[:, 320:448] = 1.0
    return c


def make_c2():
    c = np.zeros((128, NC2), np.float32)
    t = np.arange(128)[:, None]; i = np.arange(128)[None, :]
    same = (t // 64) == (i // 64)
    c[:, 0:128] = (t <= i) & same
    c[:, 128:256] = (t > i) & same
    c[:, 256:384] = (t < 64)
    c[:, 384:512] = (t >= 64)
    return c


def ssd_host(d, half):
    w = d["ssd_w_in"][0]
    cols = []
    for g in range(4):
        gg = half * 4 + g
        cols += list(range(4096 + gg * 512, 4096 + (gg + 1) * 512))
        cols += list(range(gg * 512, (gg + 1) * 512))
        cols += list(range(8192 + gg * 128, 8192 + (gg + 1) * 128))
        cols += list(range(9216 + gg * 128, 9216 + (gg + 1) * 128))
        cols += list(range(10240 + gg * 8, 10240 + (gg + 1) * 8))
    w_in = np.ascontiguousarray(w[:, cols])
    cw = d["ssd_conv_w"][0]; cb = d["ssd_conv_b"][0]
    xs = slice(half * 2048, (half + 1) * 2048)
    cwx = cw[:, xs].T.reshape(16, 128, 4).transpose(1, 0, 2)
    cbx = cb[xs].reshape(16, 128).T
    bcols = np.concatenate([np.arange(4096 + (half * 4 + g) * 128, 4096 + (half * 4 + g + 1) * 128) for g in range(4)] +
                           [np.arange(5120 + (half * 4 + g) * 128, 5120 + (half * 4 + g + 1) * 128) for g in range(4)])
    cwbc = cw[:, bcols].T.reshape(8, 128, 4).transpose(1, 0, 2)
    cbbc = cb[bcols].reshape(8, 128).T
    hs = slice(half * 32, (half + 1) * 32)
    m = dict(w_in=w_in, cwx=cwx, cbx=cbx, cwbc=cwbc, cbbc=cbbc, dtb=d["ssd_dt_bias"][0][hs][None], alog=d["ssd_a_log"][0][hs][None],
             dfull=np.repeat(d["ssd_d"][0][hs], 64)[None], vec_mix=d["norm_mix"][1][None], c2=make_c2(), consts=make_consts())
    return {k: np.ascontiguousarray(v, dtype=np.float32) for k, v in m.items()}


def hy_host(d, half):
    w = d["hy_w_in"][0]
    hs = [half * 4 + h for h in range(4)]
    cols = []
    for h in hs:
        cols += list(range(h * 128, (h + 1) * 128)) + list(range(1024 + h * 128, 1024 + (h + 1) * 128))
    cols += list(range(2048 + hs[0] * 128, 2048 + (hs[-1] + 1) * 128))
    cols += list(range(3072 + hs[0] * 128, 3072 + (hs[-1] + 1) * 128))
    for h in hs:
        cols += list(range(4096 + h * 128, 4096 + (h + 1) * 128)) + list(range(5120 + h * 128, 5120 + (h + 1) * 128)) + \
                list(range(6144 + h * 128, 6144 + (h + 1) * 128))
    cols += list(range(7168 + hs[0] * 128, 7168 + (hs[-1] + 1) * 128))
    cols += [8192 + h for h in hs] + [8200 + h for h in hs]
    assert len(cols) == 4104
    w_in = w[:, cols]
    lg = d["hgrn_lb_logits"]
    lbl = np.concatenate([lg[0, hs[0] * 128:(hs[-1] + 1) * 128].reshape(4, 128).T, lg[1, hs[0] * 128:(hs[-1] + 1) * 128].reshape(4, 128).T], axis=1)
    cw = d["gdn_conv_w"][0]
    ccols = []
    for h in hs:
        ccols += list(range(h * 128, (h + 1) * 128)) + list(range(1024 + h * 128, 1024 + (h + 1) * 128)) + list(range(2048 + h * 128, 2048 + (h + 1) * 128))
    gcw = cw[:, ccols].T.reshape(12, 128, 4).transpose(1, 0, 2)
    gdn_p = np.concatenate([d["gdn_a_log"][0][hs], d["gdn_dt_bias"][0][hs]])[None]
    sl = slice(hs[0] * 128, (hs[-1] + 1) * 128)
    m = dict(w_in=w_in, lbl=lbl, gcw=gcw, gdn_p=gdn_p, hgrn_norm=d["hgrn_norm"][0][sl][None], gdn_norm=d["gdn_norm"][0][sl][None],
             vec_mix=d["norm_mix"][0][None], c2=make_c2(), consts=make_consts())
    return {k: np.ascontiguousarray(v, dtype=np.float32) for k, v in m.items()}


GROUPS = [[0, 1], [2, 3], [4, 5], [6, 7]]
NCU = 8

EXT_INPUTS = [
    ("consts", [128, NCONST]), ("c2", [128, NC2]), ("xfull", [SEQ, D]), ("x_own", [NT, D]), ("mem", [MEM, D]),
    ("w_in_hy", [D, 4104]), ("vec_mix0", [1, D]), ("lbl", [128, 8]), ("gcw", [128, 12, 4]), ("gdn_p", [1, 8]),
    ("hgrn_norm", [1, 512]), ("gdn_norm", [1, 512]), ("w_out0", [1024, D]),
    ("w_in_ssd", [D, 4 * 1288]), ("cwx", [128, 16, 4]), ("cbx", [128, 16]), ("cwbc", [128, 8, 4]), ("cbbc", [128, 8]),
    ("dtb", [1, 32]), ("alog", [1, 32]), ("dfull", [1, 2048]), ("w_out1", [2048, D]), ("vec_ssdnorm", [1, 2048]), ("vec_mix1", [1, D]),
    ("vec_final", [1, D]),
]
for _l in range(2):
    EXT_INPUTS += [(f"wq{_l}", [D, 512]), (f"wk{_l}", [D, 512]), (f"wv{_l}", [D, 512]), (f"wo{_l}", [512, D]),
                   (f"w_gate{_l}", [D, DFF]), (f"w_up{_l}", [D, DFF]), (f"w_down{_l}", [DFF, D]),
                   (f"vec_xattn{_l}", [1, D]), (f"vec_mem{_l}", [1, D]), (f"vec_ffn{_l}", [1, D])]


def build_all(upto=9):
    nc = bass.Bass("TRN2", target_bir_lowering=False)
    P = Prog(nc, same_engine_sync=SES)
    t = {}
    for nm, shp in EXT_INPUTS:
        t[nm] = P.dram(nm, shp, F32, "ExternalInput")
    t["out"] = P.dram("out", [NT, D], F32, "ExternalOutput")
    for nm, shp, dt in [("omix", [SEQ, 1024], F32), ("part0", [SEQ, D], F32), ("rs0", [NT, D], F32), ("h1_own", [NT, D], F32),
                        ("yg", [SEQ, 2048], F32),
                        ("part1", [SEQ, 2064], F32), ("rs1", [NT, 2064], F32)]:
        t[nm] = P.dram(nm, shp, dt, "Internal")
    for k_ in range(4):
        t[f"u_half{k_}"] = P.dram(f"u_half{k_}", [256, D], BF16, "Internal")
        t[f"u_full{k_}"] = P.dram(f"u_full{k_}", [512, D], BF16, "Internal")
    u_half = [t[f"u_half{k_}"] for k_ in range(4)]
    u_full = [t[f"u_full{k_}"] for k_ in range(4)]
    cx = setup_ctx(P, t["consts"])
    hy_io = dict(c2=t["c2"], xfull=t["xfull"], w_in=t["w_in_hy"], vec_mix=t["vec_mix0"], lbl=t["lbl"], gcw=t["gcw"], gdn_p=t["gdn_p"],
                 hgrn_norm=t["hgrn_norm"], gdn_norm=t["gdn_norm"], omix=t["omix"], w_out=t["w_out0"], part=t["part0"])
    if upto != 4:
        stage_hy(P, cx, hy_io)
    if upto == 5:
        stage_hy(P, cx, hy_io)
    P.collective("ReduceScatter", ALU.add, GROUPS[:NCU // 2], t["part0"].ap, t["rs0"].ap)
    if upto in (1, 5):
        o = P.dma("sp", out=t["out"][:, :], in_=t["rs0"][:, :])
        P.emit(final_wait_ops=[o])
        return nc

    def b_io(l):
        d_ = dict(mem=t["mem"], wq=t[f"wq{l}"], wk=t[f"wk{l}"], wv=t[f"wv{l}"], wo=t[f"wo{l}"], w_gate=t[f"w_gate{l}"], w_up=t[f"w_up{l}"],
                  w_down=t[f"w_down{l}"], vec_xattn=t[f"vec_xattn{l}"], vec_mem=t[f"vec_mem{l}"], vec_ffn=t[f"vec_ffn{l}"],
                  vec_final=t["vec_final"], vec_mix_next=t["vec_mix1"])
        return d_
    io0 = b_io(0)
    io0.update(hres=t["x_own"], rs=t["rs0"], hout=t["h1_own"], u_half=u_half)
    stage_b(P, cx, io0, 0, False)
    if upto in (3, 4):
        o = P.dma("sp", out=t["out"][:, :], in_=t["h1_own"][:, :])
        P.emit(final_wait_ops=[o])
        return nc
    for k_ in range(4):
        P.collective("AllGather", ALU.bypass, GROUPS[:NCU // 2], u_half[k_].ap, u_full[k_].ap)
    if upto == 2:
        o = P.dma("sp", out=t["out"][:, :], in_=t["h1_own"][:, :])
        P.emit(final_wait_ops=[o])
        return nc
    ssd_io = dict(c2=t["c2"], u_full=u_full, w_in=t["w_in_ssd"], cwx=t["cwx"], cbx=t["cbx"], cwbc=t["cwbc"], cbbc=t["cbbc"], dtb=t["dtb"],
                  alog=t["alog"], dfull=t["dfull"], yg=t["yg"], w_out=t["w_out1"], vec_ssdnorm=t["vec_ssdnorm"], part=t["part1"])
    stage_ssd(P, cx, ssd_io)
    P.collective("ReduceScatter", ALU.add, GROUPS[:NCU // 2], t["part1"].ap, t["rs1"].ap)
    io1 = b_io(1)
    io1.update(hres=t["h1_own"], rs=t["rs1"], hout=t["out"])
    outs = stage_b(P, cx, io1, 1, True)
    P.emit(final_wait_ops=outs)
    print("ops", len(P.ops), "waits", P.n_waits, "sb_peak", P.sb_peak)
    return nc


_CACHE = {}
UPTO = 9
TRACE = False
SES = True


def _f32(a):
    return np.ascontiguousarray(a, dtype=np.float32)


def kernel(**inputs):
    d = {k: np.asarray(v) for k, v in inputs.items()}
    x = d["x"]
    B = x.shape[0]
    cores = list(range(NCU))
    if "nc" not in _CACHE:
        _CACHE["nc"] = build_all(UPTO)
    nc = _CACHE["nc"]
    shared = dict(consts=make_consts(), c2=make_c2(), vec_mix0=_f32(d["norm_mix"][0][None]), vec_mix1=_f32(d["norm_mix"][1][None]),
                  vec_final=_f32(d["norm_final"][None]))
    for l in range(2):
        shared.update({f"wq{l}": _f32(d["xa_wq"][l]), f"wk{l}": _f32(d["xa_wk"][l]), f"wv{l}": _f32(d["xa_wv"][l]), f"wo{l}": _f32(d["xa_wo"][l]),
                       f"w_gate{l}": _f32(d["ffn_w_gate"][l]), f"w_up{l}": _f32(d["ffn_w_up"][l]), f"w_down{l}": _f32(d["ffn_w_down"][l]),
                       f"vec_xattn{l}": _f32(d["norm_xattn"][l][None]), f"vec_mem{l}": _f32(d["norm_mem"][l][None]),
                       f"vec_ffn{l}": _f32(d["norm_ffn"][l][None])})
    per_half = []
    for half in range(2):
        hh = hy_host(d, half)
        sh = ssd_host(d, half)
        m = dict(w_in_hy=hh["w_in"], lbl=hh["lbl"], gcw=hh["gcw"], gdn_p=hh["gdn_p"], hgrn_norm=hh["hgrn_norm"], gdn_norm=hh["gdn_norm"],
                 w_out0=_f32(np.concatenate([d["hy_w_out"][0][half * 512:(half + 1) * 512], d["hy_w_out"][0][1024 + half * 512:1024 + (half + 1) * 512]], axis=0)),
                 w_in_ssd=sh["w_in"], cwx=sh["cwx"], cbx=sh["cbx"], cwbc=sh["cwbc"], cbbc=sh["cbbc"], dtb=sh["dtb"], alog=sh["alog"], dfull=sh["dfull"],
                 w_out1=_f32(d["ssd_w_out"][0][half * 2048:(half + 1) * 2048]), vec_ssdnorm=_f32(d["ssd_norm"][0][half * 2048:(half + 1) * 2048][None]))
        per_half.append(m)
    maps = []
    for c in cores:
        b, half = c // 2, c % 2
        m = dict(shared)
        m.update(per_half[half])
        m["xfull"] = _f32(x[b])
        m["x_own"] = _f32(x[b, half * NT:(half + 1) * NT])
        m["mem"] = _f32(d["mem"][b])
        maps.append(m)
    res = run_bass_kernel_spmd(nc, maps, core_ids=cores, **({'trace': True} if TRACE else {}))
    if TRACE:
        print('EXEC_NS', getattr(res, 'exec_time_ns', None))
    out = np.empty((B, SEQ, D), np.float32)
    for c in cores:
        b, half = c // 2, c % 2
        out[b, half * NT:(half + 1) * NT] = res.results[c]["out"]
    return out
```
